# Optimizing a Trainium2 kernel written in Bass

```python
import math
import jax, jax.numpy as jnp
from jax import lax
import numpy as np

D_MODEL = 1024
BATCH = 8
SEQ = 4096
DEPTH = 2

CHUNK = 64
N_EVEN = (DEPTH + 1) // 2
N_ODD = DEPTH // 2
EPS = 1e-6

S5_WIDTH = D_MODEL
S5_GROUP = 16
S5_GROUPS = S5_WIDTH // S5_GROUP
S5_STATE = 64

RET_HEADS = 8
RET_DK = D_MODEL // (2 * RET_HEADS)
RET_DV = D_MODEL // RET_HEADS
RET_QK = RET_HEADS * RET_DK
RET_WIDTH = RET_HEADS * RET_DV
ROPE_BASE = 10000.0

AB_IN = 2 * S5_WIDTH + 2 * RET_QK + 2 * RET_WIDTH
AB_MIX = S5_WIDTH + RET_WIDTH

CONV_WIDTH = D_MODEL
CONV_K = 31
C_IN = 3 * CONV_WIDTH

kernel_name = 'hybrid_s5_retention_conformer_conv'


def rmsnorm(x, g):
    xf = x.astype(jnp.float32)
    y = xf * lax.rsqrt(jnp.mean(xf * xf, axis=-1, keepdims=True) + EPS)
    return (y * g.astype(jnp.float32)).astype(x.dtype)


def _complex_affine_combine(e1, e2):
    a1r, a1i, b1r, b1i = e1
    a2r, a2i, b2r, b2i = e2
    ar = a2r * a1r - a2i * a1i
    ai = a2r * a1i + a2i * a1r
    br = a2r * b1r - a2i * b1i + b2r
    bi = a2r * b1i + a2i * b1r + b2i
    return ar, ai, br, bi


def s5_mixer(u, a_re, a_im, log_dt, b_re, b_im, c_re, c_im, d_skip, glu_w, glu_b):
    f32 = jnp.float32
    bsz, seq, _ = u.shape
    n_chunks = seq // CHUNK
    uf = u.astype(f32)
    are, aim = a_re.astype(f32), a_im.astype(f32)
    dt = jnp.exp(log_dt.astype(f32))[:, None]
    mag = jnp.exp(are * dt)
    lb_re = mag * jnp.cos(aim * dt)
    lb_im = mag * jnp.sin(aim * dt)
    nr, ni = lb_re - 1.0, lb_im
    den = are * are + aim * aim
    coef_re = (nr * are + ni * aim) / den
    coef_im = (ni * are - nr * aim) / den
    bre, bim = b_re.astype(f32), b_im.astype(f32)
    bb_re = coef_re[..., None] * bre - coef_im[..., None] * bim
    bb_im = coef_re[..., None] * bim + coef_im[..., None] * bre
    cre, cim = c_re.astype(f32), c_im.astype(f32)
    dsk = d_skip.astype(f32).reshape(S5_GROUPS, S5_GROUP)
    a_r = jnp.broadcast_to(lb_re, (CHUNK, 1, S5_GROUPS, S5_STATE))
    a_i = jnp.broadcast_to(lb_im, (CHUNK, 1, S5_GROUPS, S5_STATE))

    u_blk = uf.reshape(bsz, n_chunks, CHUNK, S5_GROUPS, S5_GROUP).transpose(1, 2, 0, 3, 4)

    def chunk_step(carry, u_c):
        s_re, s_im = carry
        bu_re = jnp.einsum('tbgh,gnh->tbgn', u_c, bb_re)
        bu_im = jnp.einsum('tbgh,gnh->tbgn', u_c, bb_im)
        pr, pim, zr, zi = lax.associative_scan(
            _complex_affine_combine, (a_r, a_i, bu_re, bu_im), axis=0)
        xr = zr + pr * s_re - pim * s_im
        xi = zi + pr * s_im + pim * s_re
        y = (jnp.einsum('tbgn,ghn->tbgh', xr, cre)
             - jnp.einsum('tbgn,ghn->tbgh', xi, cim)
             + dsk * u_c)
        return (xr[-1], xi[-1]), y

    init = (jnp.zeros((bsz, S5_GROUPS, S5_STATE), f32),
            jnp.zeros((bsz, S5_GROUPS, S5_STATE), f32))
    _, y = lax.scan(chunk_step, init, u_blk)
    y = y.transpose(2, 0, 1, 3, 4).reshape(bsz, seq, S5_WIDTH)
    y = jax.nn.gelu(y)
    y = y * jax.nn.sigmoid(y @ glu_w.astype(f32) + glu_b.astype(f32))
    return y.astype(u.dtype)


def rope(x, positions):
    half = x.shape[-1] // 2
    freqs = ROPE_BASE ** (-jnp.arange(half, dtype=jnp.float32) / half)
    ang = positions.astype(jnp.float32)[:, None] * freqs[None, :]
    cos = jnp.cos(ang)[None, :, None, :]
    sin = jnp.sin(ang)[None, :, None, :]
    x1, x2 = x[..., :half], x[..., half:]
    return jnp.concatenate([x1 * cos - x2 * sin, x1 * sin + x2 * cos], axis=-1)


def retention(q, k, v):
    f32 = jnp.float32
    bsz, seq = q.shape[0], q.shape[1]
    n_chunks = seq // CHUNK
    positions = jnp.arange(seq, dtype=jnp.int32)
    q = rope(q.astype(f32), positions) * (RET_DK ** -0.5)
    k = rope(k.astype(f32), positions)
    v = v.astype(f32)
    log_g = jnp.log(1.0 - 2.0 ** (-5.0 - jnp.arange(RET_HEADS, dtype=f32)))
    idx = jnp.arange(CHUNK, dtype=f32)
    intra = jnp.exp(log_g[:, None, None] * jnp.abs(idx[:, None] - idx[None, :]))
    qc = q.reshape(bsz, n_chunks, CHUNK, RET_HEADS, RET_DK)
    kc = k.reshape(bsz, n_chunks, CHUNK, RET_HEADS, RET_DK)
    vc = v.reshape(bsz, n_chunks, CHUNK, RET_HEADS, RET_DV)
    scores = jnp.einsum('bnihd,bnjhd->bnhij', qc, kc) * intra
    intra_out = jnp.einsum('bnhij,bnjhe->bnihe', scores, vc)
    k_dec = jnp.exp(log_g[None, :] * (CHUNK - 1 - idx)[:, None])
    kv = jnp.einsum('bnjhd,jh,bnjhe->bnhde', kc, k_dec, vc)
    chunk_decay = jnp.exp(log_g * CHUNK)[:, None, None]

    def state_step(s, kv_n):
        return chunk_decay * s + kv_n, s

    _, s_prev = lax.scan(state_step, jnp.zeros((bsz, RET_HEADS, RET_DK, RET_DV), f32),
                         kv.transpose(1, 0, 2, 3, 4))
    s_prev = s_prev.transpose(1, 0, 2, 3, 4)
    q_dec = jnp.exp(log_g[None, :] * (idx + 1.0)[:, None])
    cross = jnp.einsum('bnihd,ih,bnhde->bnihe', qc, q_dec, s_prev)
    out = (intra_out + cross).reshape(bsz, seq, RET_HEADS, RET_DV)
    mu = jnp.mean(out, axis=-1, keepdims=True)
    var = jnp.mean(jnp.square(out - mu), axis=-1, keepdims=True)
    out = (out - mu) * lax.rsqrt(var + EPS)
    return out.reshape(bsz, seq, RET_WIDTH)


def mixer_ab(h, w_in, a_re, a_im, log_dt, b_re, b_im, c_re, c_im, d_skip,
             glu_w, glu_b, w_out):
    bsz, seq, _ = h.shape
    proj = h @ w_in
    cuts = np.cumsum([S5_WIDTH, S5_WIDTH, RET_QK, RET_QK, RET_WIDTH]).tolist()
    u_s5, g_s5, q, k, v, g_ret = jnp.split(proj, cuts, axis=-1)
    y_s5 = s5_mixer(u_s5, a_re, a_im, log_dt, b_re, b_im, c_re, c_im, d_skip, glu_w, glu_b)
    y_s5 = y_s5 * jax.nn.silu(g_s5)
    q = q.reshape(bsz, seq, RET_HEADS, RET_DK)
    k = k.reshape(bsz, seq, RET_HEADS, RET_DK)
    v = v.reshape(bsz, seq, RET_HEADS, RET_DV)
    y_ret = retention(q, k, v).astype(h.dtype) * jax.nn.silu(g_ret)
    return jnp.concatenate([y_s5, y_ret], axis=-1) @ w_out


def mixer_conv(h, w_in, conv_w, conv_b, ln_g, ln_b, w_out):
    proj = h @ w_in
    a, b, g = jnp.split(proj, 3, axis=-1)
    u = a * jax.nn.sigmoid(b)
    u = lax.conv_general_dilated(
        u, conv_w[:, None, :].astype(u.dtype), window_strides=(1,),
        padding=[(CONV_K - 1, 0)], dimension_numbers=('NWC', 'WIO', 'NWC'),
        feature_group_count=CONV_WIDTH) + conv_b
    uf = u.astype(jnp.float32)
    mu = jnp.mean(uf, axis=-1, keepdims=True)
    var = jnp.mean(jnp.square(uf - mu), axis=-1, keepdims=True)
    uf = (uf - mu) * lax.rsqrt(var + EPS) * ln_g.astype(jnp.float32) + ln_b.astype(jnp.float32)
    u = jax.nn.silu(uf).astype(h.dtype) * jax.nn.silu(g)
    return u @ w_out


def setup_inputs(seed: int = 0) -> dict:
    key = jax.random.key(seed)
    ks = jax.random.split(key, 24)
    f32 = jnp.float32
    nrm = lambda k, shape, s: jax.random.normal(k, shape, f32) * s
    G, N, H = S5_GROUPS, S5_STATE, S5_GROUP
    a_re = -0.5 + nrm(ks[4], (N_EVEN, G, N), 0.01)
    a_im = math.pi * jnp.arange(N, dtype=f32)[None, None, :] + nrm(ks[5], (N_EVEN, G, N), 0.01)
    log_dt = jax.random.uniform(ks[6], (N_EVEN, G), f32, math.log(1e-3), math.log(1e-1))
    return {
        'x': nrm(ks[0], (BATCH, SEQ, D_MODEL), 1.0),
        'norm_g': 1.0 + nrm(ks[1], (DEPTH, D_MODEL), 0.01),
        'final_g': 1.0 + nrm(ks[2], (D_MODEL,), 0.01),
        'w_in_ab': nrm(ks[3], (N_EVEN, D_MODEL, AB_IN), D_MODEL ** -0.5),
        's5_a_re': a_re,
        's5_a_im': a_im,
        's5_log_dt': log_dt,
        's5_b_re': nrm(ks[7], (N_EVEN, G, N, H), (2.0 * H) ** -0.5),
        's5_b_im': nrm(ks[8], (N_EVEN, G, N, H), (2.0 * H) ** -0.5),
        's5_c_re': nrm(ks[9], (N_EVEN, G, H, N), (2.0 * N) ** -0.5),
        's5_c_im': nrm(ks[10], (N_EVEN, G, H, N), (2.0 * N) ** -0.5),
        's5_d': nrm(ks[11], (N_EVEN, S5_WIDTH), 1.0),
        's5_glu_w': nrm(ks[12], (N_EVEN, S5_WIDTH, S5_WIDTH), S5_WIDTH ** -0.5),
        's5_glu_b': nrm(ks[13], (N_EVEN, S5_WIDTH), 0.01),
        'w_out_ab': nrm(ks[14], (N_EVEN, AB_MIX, D_MODEL), AB_MIX ** -0.5),
        'w_in_c': nrm(ks[15], (N_ODD, D_MODEL, C_IN), D_MODEL ** -0.5),
        'conv_w': nrm(ks[16], (N_ODD, CONV_K, CONV_WIDTH), CONV_K ** -0.5),
        'conv_b': nrm(ks[17], (N_ODD, CONV_WIDTH), 0.01),
        'conv_ln_g': 1.0 + nrm(ks[18], (N_ODD, CONV_WIDTH), 0.01),
        'conv_ln_b': nrm(ks[19], (N_ODD, CONV_WIDTH), 0.01),
        'w_out_c': nrm(ks[20], (N_ODD, CONV_WIDTH, D_MODEL), CONV_WIDTH ** -0.5),
    }


def reference(x, norm_g, final_g, w_in_ab, s5_a_re, s5_a_im, s5_log_dt, s5_b_re,
              s5_b_im, s5_c_re, s5_c_im, s5_d, s5_glu_w, s5_glu_b, w_out_ab,
              w_in_c, conv_w, conv_b, conv_ln_g, conv_ln_b, w_out_c):
    for layer in range(DEPTH):
        h = rmsnorm(x, norm_g[layer])
        i = layer // 2
        if layer % 2 == 0:
            x = x + mixer_ab(h, w_in_ab[i], s5_a_re[i], s5_a_im[i], s5_log_dt[i],
                             s5_b_re[i], s5_b_im[i], s5_c_re[i], s5_c_im[i], s5_d[i],
                             s5_glu_w[i], s5_glu_b[i], w_out_ab[i])
        else:
            x = x + mixer_conv(h, w_in_c[i], conv_w[i], conv_b[i], conv_ln_g[i],
                               conv_ln_b[i], w_out_c[i])
    return rmsnorm(x, final_g)
```

```python
import math
import os
import numpy as np
import concourse.bass as bass
import concourse.mybir as mybir
from concourse.bass_utils import run_bass_kernel_spmd
from contextlib import ExitStack

F32 = mybir.dt.float32
BF16 = mybir.dt.bfloat16
AF = mybir.ActivationFunctionType
ALU = mybir.AluOpType
AX = mybir.AxisListType

TT = 256
EPS = 1e-6


class StopBuild(Exception):
    pass


class Prog:
    COMPUTE = ("pe", "act", "dve", "pool")
    RING = 8

    def __init__(self, nc):
        self.nc = nc
        self.ops = []
        self.lw = {}
        self.rd = {}
        self.ndma = {"sp": 0, "act": 0, "pool": 0}
        self.stack = ExitStack()
        self.last = {}
        self.pending_dma = []

    def sb(self, name, shape, dt):
        return self.stack.enter_context(self.nc.sbuf_tensor(name, list(shape), dt))

    def ps(self, name, shape, dt):
        return self.stack.enter_context(self.nc.psum_tensor(name, list(shape), dt))

    def op(self, eng, fn, r=(), w=(), dma=False, nobar=False, extra=()):
        i = len(self.ops)
        raw = set()
        oth = set()
        for t in r:
            if t in self.lw:
                raw.add(self.lw[t])
        for t in w:
            if t in self.lw:
                oth.add(self.lw[t])
            for _, j in self.rd.get(t, {}).items():
                oth.add(j)
        deps = set(extra)
        for j in raw | oth:
            oj = self.ops[j]
            if (not dma) and (not oj["dma"]) and oj["eng"] == eng and eng == "pe" and j not in raw:
                continue
            deps.add(j)
        o = dict(eng=eng, fn=fn, deps=deps, dma=dma, sig=False, val=None, slot=None)
        if dma:
            o["slot"] = self.ndma[eng] % self.RING
            self.ndma[eng] += 1
            if not nobar:
                self.pending_dma.append(i)
        else:
            self.last[eng] = i
        self.ops.append(o)
        key = ("dma", eng, i) if dma else eng
        for t in r:
            self.rd.setdefault(t, {})[key] = i
        for t in w:
            self.lw[t] = i
            self.rd[t] = {}
        return i

    def barrier(self):
        deps = set(self.last.values()) | set(self.pending_dma)
        self.pending_dma = []
        for e in ("pe", "act", "dve", "pool", "sp"):
            self.op(e, lambda g: g.nop(), extra=deps)

    def emit(self):
        nc = self.nc
        ops = self.ops
        for o in ops:
            for j in o["deps"]:
                ops[j]["sig"] = True
        cnt = {e: 0 for e in self.COMPUTE + ("sp",)}
        dcnt = {}
        for o in ops:
            if o["dma"]:
                k = (o["eng"], o["slot"])
                dcnt[k] = dcnt.get(k, 0) + 16
                o["val"] = dcnt[k]
            elif o["sig"]:
                cnt[o["eng"]] += 1
                o["val"] = cnt[o["eng"]]
        st = self.stack
        csem = {e: st.enter_context(nc.semaphore(f"s_{e}")) for e in self.COMPUTE + ("sp",)}
        dsem = {}
        for q in ("sp", "act", "pool"):
            for s in range(min(self.RING, self.ndma[q])):
                dsem[(q, s)] = st.enter_context(nc.semaphore(f"d_{q}{s}"))
        block = st.enter_context(nc.Block())
        per = {e: [] for e in ("pe", "act", "dve", "pool", "sp")}
        for i, o in enumerate(ops):
            per[o["eng"]].append(i)

        def semof(o):
            if o["dma"]:
                return dsem[(o["eng"], o["slot"])]
            return csem[o["eng"]]

        def run(e, eng):
            waited = {}
            for i in per[e]:
                o = ops[i]
                need = {}
                for j in o["deps"]:
                    oj = ops[j]
                    s = semof(oj)
                    k = id(s)
                    if waited.get(k, 0) >= oj["val"]:
                        continue
                    if k not in need or need[k][1] < oj["val"]:
                        need[k] = (s, oj["val"])
                if o["dma"]:
                    s = semof(o)
                    prev = o["val"] - 16
                    if prev > 0 and waited.get(id(s), 0) < prev:
                        if id(s) not in need or need[id(s)][1] < prev:
                            need[id(s)] = (s, prev)
                for k, (s, v) in need.items():
                    eng.wait_ge(s, v)
                    waited[k] = v
                ins = o["fn"](eng)
                if o["dma"]:
                    ins.then_inc(semof(o), 16)
                elif o["sig"]:
                    ins.then_inc(semof(o), 1)

        if per["pe"]:
            @block.tensor
            def _(eng):
                run("pe", eng)
        if per["act"]:
            @block.scalar
            def _(eng):
                run("act", eng)
        if per["dve"]:
            @block.vector
            def _(eng):
                run("dve", eng)
        if per["pool"]:
            @block.gpsimd
            def _(eng):
                run("pool", eng)
        if per["sp"]:
            @block.sync
            def _(eng):
                run("sp", eng)

    def close(self):
        self.stack.close()


def V(t, p0, pn, off, pat, dt=None):
    a = t[:] if dt is None else t[:].bitcast(dt)
    base = a[p0:p0 + pn, off:off + 1]
    return bass.AP(base.tensor, base.offset, [list(base.ap[0])] + [list(x) for x in pat])


CT_IDENT = 0
CT_M16 = 128
CT_ONES = 256
CT_MASK = 384
CT_KDEC = CT_MASK + 512
CT_QDEC = CT_KDEC + 8
CT_G64 = CT_QDEC + 8
CT_ROPE = CT_G64 + 8


def make_ctab(nsub):
    n = CT_ROPE + 2 * nsub * 32
    c = np.zeros((128, n), np.float64)
    c[:, CT_IDENT:CT_IDENT + 128] = np.eye(128)
    r = np.arange(128)
    c[:, CT_M16:CT_M16 + 128] = (r[:, None] // 16 == r[None, :] // 16)
    c[:, CT_ONES:CT_ONES + 128] = 1.0
    gam = 1.0 - 2.0 ** (-5.0 - np.arange(8))
    j = r % 64
    i = np.arange(64)
    m = gam[None, :, None] ** np.abs(i[None, None, :] - j[:, None, None]) * (64 ** -0.5)
    c[:, CT_MASK:CT_MASK + 512] = m.reshape(128, 512)
    c[:, CT_KDEC:CT_KDEC + 8] = gam[None, :] ** (63 - j[:, None])
    c[:, CT_QDEC:CT_QDEC + 8] = gam[None, :] ** (j[:, None] + 1.0) * (64 ** -0.5)
    par = r // 64
    for hp in range(4):
        c[:, CT_G64 + hp] = gam[2 * hp + par] ** 64
    freqs = 10000.0 ** (-np.arange(32) / 32.0)
    pos = (np.arange(nsub)[None, :] * 128 + r[:, None]).astype(np.float64)
    ang = pos[:, :, None] * freqs[None, None, :]
    c[:, CT_ROPE:CT_ROPE + nsub * 32] = np.cos(ang).reshape(128, -1)
    c[:, CT_ROPE + nsub * 32:CT_ROPE + 2 * nsub * 32] = np.sin(ang).reshape(128, -1)
    return c.astype(np.float32)


def build(NT, debug=False, stop=99):
    nc = bass.Bass("TRN2", target_bir_lowering=False)
    P = Prog(nc)
    T = NT * TT
    NSUB = NT * 2
    NCT = CT_ROPE + 2 * NSUB * 32

    def din(name, shape):
        return nc.dram_tensor(name, list(shape), F32, kind="ExternalInput")

    x_d = din("x", [T, 1024])
    norm_g_d = din("norm_g", [16, 128])
    final_g_d = din("final_g", [1024])
    w_in_ab_d = din("w_in_ab", [1024, 5120])
    a_re_d = din("a_re", [32, 128])
    a_im_d = din("a_im", [32, 128])
    log_dt_d = din("log_dt", [32, 2])
    b_re_d = din("b_re", [64 * 64 * 16])
    b_im_d = din("b_im", [64 * 64 * 16])
    c_re_d = din("c_re", [64 * 16 * 64])
    c_im_d = din("c_im", [64 * 16 * 64])
    s5_d_d = din("s5_d", [8, 128])
    glu_w_d = din("glu_w", [1024, 1024])
    glu_b_d = din("glu_b", [8, 128])
    w_out_ab_d = din("w_out_ab", [2048, 1024])
    w_in_c_d = din("w_in_c", [1024, 3072])
    conv_w_d = din("conv_w", [248, 128])
    conv_b_d = din("conv_b", [8, 128])
    ln_g_d = din("ln_g", [8, 128])
    ln_b_d = din("ln_b", [8, 128])
    w_out_c_d = din("w_out_c", [1024, 1024])
    ctab_d = din("ctab", [128, NCT])
    out_d = nc.dram_tensor("out", [T, 1024], F32, kind="ExternalOutput")
    DG = nc.dram_tensor("dg_conv", [8, 128, 31 * 128], BF16, kind="Internal")
    WBF = {}
    for nm_, d_ in (("w_in_ab", w_in_ab_d), ("glu_w", glu_w_d), ("w_out_ab", w_out_ab_d),
                    ("w_in_c", w_in_c_d), ("w_out_c", w_out_c_d)):
        WBF[nm_] = nc.dram_tensor("bf_" + nm_, list(d_.shape), BF16, kind="Internal")
    if debug:
        x1_d = nc.dram_tensor("x1", [T, 1024], F32, kind="ExternalOutput")

    CTAB = P.sb("CTAB", [128, CT_ROPE], F32)
    ROPE = P.sb("ROPE", [128, 256], F32)
    IDB = P.sb("IDB", [128, 128], BF16)
    ONESB = P.sb("ONESB", [128, 128], BF16)
    COLS = P.sb("COLS", [128, 384], F32)
    FG = P.sb("FG", [128, 1024], F32)
    KW = P.sb("KW", [128, 64 * 128], BF16)
    WIN = P.sb("WIN", [128, 128 * 128], BF16)
    WOUT = P.sb("WOUT", [128, 512 * 32], BF16)
    TCS = P.sb("TCS", [128, 2 * 1024], F32)
    RHOT = P.sb("RHOT", [128, 1024], F32)
    RHO = P.sb("RHO", [128, 32], F32)
    W0 = P.sb("W0", [128, 64], F32)
    SRET = P.sb("SRET", [128, 512], F32)
    XB = [P.sb(f"X{i}", [128, 2 * 1024], F32) for i in range(2)]
    HT = P.sb("HT", [128, 8 * TT], BF16)
    WB = [P.sb(f"WB{i}", [128, 8 * 512], BF16) for i in range(3)]
    YC = P.sb("YC", [128, 16 * TT], BF16)
    U1 = P.sb("U1", [128, 8 * 288], BF16)
    SMALL = P.sb("SMALL", [128, 64], F32)
    DIAG = [P.sb(f"DIAG{i}", [128, 128], BF16) for i in range(4)]
    ARENA_W = 11520
    AR = P.sb("ARENA", [128, ARENA_W], F32)
    PS = [P.ps(f"ps{i}", [128, 512], F32) for i in range(8)]
    psn = [0]

    def bank():
        i = psn[0] % 7
        psn[0] += 1
        return PS[i], f"ps{i}"

    def ar(off_words, dt=F32):
        return off_words * (2 if dt == BF16 else 1)

    def AV(off_words, p0, pn, eoff, pat, dt=F32):
        if off_words >= 200000:
            return V(WB[1], p0, pn, (off_words - 200000) + eoff, pat, F32)
        if off_words >= 100000:
            return V(WB[0], p0, pn, (off_words - 100000) + eoff, pat, F32)
        return V(AR, p0, pn, ar(off_words, dt) + eoff, pat, dt if dt != F32 else None)

    def ct(off, n, p0=0, pn=128):
        return V(CTAB, p0, pn, off, [[1, n]])

    def col(chunk, c, pn=128):
        return V(COLS, 0, pn, chunk * 128 + c, [[1, 1]])

    dve = lambda fn, r, w: P.op("dve", fn, r, w)
    act = lambda fn, r, w: P.op("act", fn, r, w)
    pe = lambda fn, r, w: P.op("pe", fn, r, w)

    def tt(out, in0, in1, op, r, w):
        dve(lambda e: e.tensor_tensor(out=out, in0=in0, in1=in1, op=op), r, w)

    def ts(out, in0, s1, s2, op0, op1, r, w):
        if op1 is None:
            dve(lambda e: e.tensor_scalar(out=out, in0=in0, scalar1=s1, scalar2=None, op0=op0), r, w)
        else:
            dve(lambda e: e.tensor_scalar(out=out, in0=in0, scalar1=s1, scalar2=s2, op0=op0, op1=op1), r, w)

    def actf(out, in_, func, r, w, scale=None, bias=None, accum=None):
        kw = {}
        if scale is not None:
            kw["scale"] = scale
        if bias is not None:
            kw["bias"] = bias
        if accum is not None:
            kw["accum_out"] = accum
        act(lambda e: e.activation(out=out, in_=in_, func=func, **kw), r, w)

    conv_order = []
    for c0 in range(0, 5120, 512):
        conv_order.append(("w_in_ab", 0, c0))
    for c0 in (0, 512):
        conv_order.append(("glu_w", 0, c0))
    for c0 in (0, 512):
        conv_order.append(("w_out_ab", 0, c0))
        conv_order.append(("w_out_ab", 1024, c0))
    for c0 in range(0, 3072, 512):
        conv_order.append(("w_in_c", 0, c0))
    for c0 in (0, 512):
        conv_order.append(("w_out_c", 0, c0))
    SRCW = {"w_in_ab": w_in_ab_d, "glu_w": glu_w_d, "w_out_ab": w_out_ab_d, "w_in_c": w_in_c_d, "w_out_c": w_out_c_d}
    for (nm_, r0, c0) in conv_order:
        P.op("pool", lambda e, nm_=nm_, r0=r0, c0=c0: e.dma_start(
            out=WBF[nm_].ap()[r0:r0 + 1024, c0:c0 + 512], in_=SRCW[nm_].ap()[r0:r0 + 1024, c0:c0 + 512]),
            w=[f"CV_{nm_}_{r0}_{c0}"], dma=True, nobar=True)
    P.op("sp", lambda e: e.dma_start(out=CTAB[:], in_=ctab_d.ap()[:, 0:CT_ROPE]), w=["CTAB"], dma=True)
    dve(lambda e: e.tensor_copy(out=IDB[:], in_=ct(CT_IDENT, 128)), ["CTAB"], ["IDB"])
    dve(lambda e: e.tensor_copy(out=ONESB[:], in_=ct(CT_ONES, 128)), ["CTAB"], ["ONESB"])
    P.op("sp", lambda e: e.dma_start(out=FG[:], in_=bass.AP(final_g_d, 0, [[0, 128], [1, 1024]])), w=["FG"], dma=True)
    dve(lambda e: e.memset(W0[:], 0.0), [], ["W0"])
    dve(lambda e: e.memset(SRET[:], 0.0), [], ["SRET"])
    dve(lambda e: e.memset(U1[:], 0.0), [], ["U1"])
    dve(lambda e: e.memset(SMALL[:], 0.0), [], ["SMALL"])
    dve(lambda e: e.memset(SMALL[:, 60:61], EPS), ["SMALL"], ["SMALL"])
    dve(lambda e: e.memset(SMALL[:, 61:62], math.pi / 2), ["SMALL"], ["SMALL"])
    EPSC = SMALL[:, 60:61]
    HPIC = SMALL[:, 61:62]

    STG_O = 0
    dve(lambda e: e.memset(AV(STG_O, 0, 128, 0, [[1, 384]]), 0.0), [], ["STG"])
    pieces = [(norm_g_d, 16, 0, 0), (s5_d_d, 8, 0, 16), (glu_b_d, 8, 0, 24), (conv_b_d, 8, 0, 32),
              (ln_g_d, 8, 0, 40), (ln_b_d, 8, 0, 48)]
    for (d, n, ch, r0) in pieces:
        P.op("sp", lambda e, d=d, n=n, ch=ch, r0=r0: e.dma_start(
            out=AV(STG_O, r0, n, ch * 128, [[1, 128]]), in_=d.ap()), r=["STG"], w=["STG"], dma=True)
    P.op("sp", lambda e: e.dma_start(out=AV(STG_O, 0, 128, 128, [[1, 128]]), in_=conv_w_d.ap()[0:128, :]),
         r=["STG"], w=["STG"], dma=True)
    P.op("sp", lambda e: e.dma_start(out=AV(STG_O, 0, 120, 256, [[1, 128]]), in_=conv_w_d.ap()[128:248, :]),
         r=["STG"], w=["STG"], dma=True)
    for ch in range(3):
        b, bt = bank()
        pe(lambda e, b=b, ch=ch: e.transpose(out=b[:, 0:128], in_=AV(STG_O, 0, 128, ch * 128, [[1, 128]]),
                                             identity=ct(CT_IDENT, 128)), ["STG", "CTAB"], [bt])
        act(lambda e, b=b, ch=ch: e.copy(out=COLS[:, ch * 128:(ch + 1) * 128], in_=b[:, 0:128]), [bt], ["COLS"])

    def cw_col(k, c):
        row = k * 8 + c
        return col(1 + row // 128, row % 128)

    DGS_O = 8800
    for ft in range(8):
        for k in range(31):
            ts(AV(DGS_O, 0, 128, k * 128, [[1, 128]], BF16), IDB[:], cw_col(k, ft), None, ALU.mult, None,
               ["IDB", "COLS", "DGS"], ["DGS"])
        P.op("sp", lambda e, ft=ft: e.dma_start(out=DG.ap()[ft], in_=AV(DGS_O, 0, 128, 0, [[1, 31 * 128]], BF16)),
             r=["DGS"], w=[f"DG{ft}"], dma=True)

    o = 384
    PRM_O = o; o += 384
    LDT_O = o; o += 2
    PT_O = o; o += 96
    DT_O = o; o += 32
    ADT_O = o; o += 32
    ANG_O = o; o += 32
    SC_O = o; o += 64
    T1_O = o; o += 32
    T2_O = o; o += 32
    UPC_O = o; o += 288
    UPS_O = o; o += 288
    MG_O = o; o += 288
    LPR_O = o; o += 288
    LPI_O = o; o += 288
    CF_O = o; o += 64
    BR_O = o; o += 512
    BI_O = o; o += 512
    E0R_O = o; o += 512
    E0I_O = o; o += 512
    CIN_O = o; o += 1024
    CR_O = o; o += 512
    CI_O = o; o += 512
    ER_O = o; o += 512
    EI_O = o; o += 512
    EBR_O = 100000
    EBI_O = 101024
    CBR_O = 200000
    CBI_O = 201024
    TA_O = o; o += 512
    TB_O = o; o += 512
    assert o <= ARENA_W, o

    def a32(off, n=32, eoff=0, p0=0, pn=128):
        return AV(off, p0, pn, eoff, [[1, n]])

    P.op("sp", lambda e: e.dma_start(out=AV(PRM_O, 0, 32, 0, [[1, 128]]), in_=a_re_d.ap()), w=["PRM"], dma=True)
    P.op("sp", lambda e: e.dma_start(out=AV(PRM_O, 0, 32, 128, [[1, 128]]), in_=a_im_d.ap()), w=["PRM"], dma=True)
    P.op("sp", lambda e: e.dma_start(out=AV(LDT_O, 0, 32, 0, [[1, 2]]), in_=log_dt_d.ap()), w=["LDT"], dma=True)
    dve(lambda e: e.tensor_copy(out=AV(PRM_O, 0, 32, 256, [[64, 2], [1, 64]]),
                                in_=AV(LDT_O, 0, 32, 0, [[1, 2], [0, 64]])), ["LDT", "PRM"], ["PRM"])
    b, bt = bank()
    for k in range(3):
        pe(lambda e, b=b, k=k: e.transpose(out=b[:, k * 32:(k + 1) * 32], in_=AV(PRM_O, 0, 32, k * 128, [[1, 128]]),
                                           identity=V(CTAB, 0, 32, CT_IDENT, [[1, 32]])), ["PRM", "CTAB"], [bt])
    act(lambda e, b=b: e.copy(out=a32(PT_O, 96), in_=b[:, 0:96]), [bt], ["PT"])
    ARE = a32(PT_O, 32, 0)
    AIM = a32(PT_O, 32, 32)
    actf(a32(DT_O), a32(PT_O, 32, 64), AF.Exp, ["PT"], ["DT"])
    tt(a32(ADT_O), ARE, a32(DT_O), ALU.mult, ["PT", "DT"], ["ADT"])
    tt(a32(ANG_O), AIM, a32(DT_O), ALU.mult, ["PT", "DT"], ["ANG"])
    actf(a32(SC_O, 32, 0), a32(ANG_O), AF.Sin, ["ANG"], ["SC"], scale=1.0 / 16)
    actf(a32(SC_O, 32, 32), a32(ANG_O), AF.Sin, ["ANG", "SMALL"], ["SC"], scale=1.0 / 16, bias=HPIC)
    for _ in range(4):
        tt(a32(T1_O), a32(SC_O, 32, 0), a32(SC_O, 32, 32), ALU.mult, ["SC"], ["T1"])
        tt(a32(T2_O), a32(SC_O, 32, 0), a32(SC_O, 32, 0), ALU.mult, ["SC"], ["T2"])
        ts(a32(SC_O, 32, 0), a32(T1_O), 2.0, None, ALU.mult, None, ["T1"], ["SC"])
        ts(a32(SC_O, 32, 32), a32(T2_O), -2.0, 1.0, ALU.mult, ALU.add, ["T2"], ["SC"])
    dve(lambda e: e.memset(a32(UPC_O, 32, 0), 1.0), [], ["UPC"])
    dve(lambda e: e.memset(a32(UPS_O, 32, 0), 0.0), [], ["UPS"])
    dve(lambda e: e.tensor_copy(out=a32(UPC_O, 32, 32), in_=a32(SC_O, 32, 32)), ["SC"], ["UPC"])
    dve(lambda e: e.tensor_copy(out=a32(UPS_O, 32, 32), in_=a32(SC_O, 32, 0)), ["SC"], ["UPS"])
    C1 = a32(SC_O, 32, 32)
    S1 = a32(SC_O, 32, 0)
    for e_ in range(1, 8):
        ce = a32(UPC_O, 32, 32 * e_)
        se = a32(UPS_O, 32, 32 * e_)
        tt(a32(T1_O), ce, C1, ALU.mult, ["UPC", "SC"], ["T1"])
        tt(a32(T2_O), se, S1, ALU.mult, ["UPS", "SC"], ["T2"])
        tt(a32(UPC_O, 32, 32 * (e_ + 1)), a32(T1_O), a32(T2_O), ALU.subtract, ["T1", "T2"], ["UPC"])
        tt(a32(T1_O), se, C1, ALU.mult, ["UPS", "SC"], ["T1"])
        tt(a32(T2_O), ce, S1, ALU.mult, ["UPC", "SC"], ["T2"])
        tt(a32(UPS_O, 32, 32 * (e_ + 1)), a32(T1_O), a32(T2_O), ALU.add, ["T1", "T2"], ["UPS"])
    for e_ in range(9):
        actf(a32(MG_O, 32, 32 * e_), a32(ADT_O), AF.Exp, ["ADT"], ["MG"], scale=float(e_))
    tt(a32(LPR_O, 288), a32(MG_O, 288), a32(UPC_O, 288), ALU.mult, ["MG", "UPC"], ["LPR"])
    tt(a32(LPI_O, 288), a32(MG_O, 288), a32(UPS_O, 288), ALU.mult, ["MG", "UPS"], ["LPI"])
    dve(lambda e: e.tensor_copy(out=RHO[:], in_=a32(MG_O, 32, 256)), ["MG"], ["RHO"])
    dve(lambda e: e.tensor_copy(out=V(RHOT, 0, 128, 0, [[32, 32], [1, 32]]),
                                in_=V(RHO, 0, 128, 0, [[1, 32], [0, 32]])), ["RHO"], ["RHOT"])
    dve(lambda e: e.memset(V(RHOT, 0, 128, 0, [[32, 32], [1, 1]]), 0.0), ["RHOT"], ["RHOT"])
    TC = lambda lo, n: V(TCS, 0, 128, lo, [[32, 32], [1, n]])
    TS_ = lambda lo, n: V(TCS, 0, 128, 1024 + lo, [[32, 32], [1, n]])
    dve(lambda e: e.tensor_copy(out=TC(0, 1), in_=AV(UPC_O, 0, 128, 256, [[1, 32], [1, 1]])), ["UPC"], ["TCS"])
    dve(lambda e: e.tensor_copy(out=TS_(0, 1), in_=AV(UPS_O, 0, 128, 256, [[1, 32], [1, 1]])), ["UPS"], ["TCS"])
    n_ = 1
    while n_ < 32:
        pc = V(TCS, 0, 128, n_ - 1, [[32, 32], [0, n_]])
        psn_ = V(TCS, 0, 128, 1024 + n_ - 1, [[32, 32], [0, n_]])
        ta = AV(TA_O, 0, 128, 0, [[n_, 32], [1, n_]])
        tb = AV(TB_O, 0, 128, 0, [[n_, 32], [1, n_]])
        tt(ta, TC(0, n_), pc, ALU.mult, ["TCS"], ["TA"])
        tt(tb, TS_(0, n_), psn_, ALU.mult, ["TCS"], ["TB"])
        tt(TC(n_, n_), ta, tb, ALU.subtract, ["TA", "TB", "TCS"], ["TCS"])
        tt(ta, TC(0, n_), psn_, ALU.mult, ["TCS"], ["TA"])
        tt(tb, TS_(0, n_), pc, ALU.mult, ["TCS"], ["TB"])
        tt(TS_(n_, n_), ta, tb, ALU.add, ["TA", "TB", "TCS"], ["TCS"])
        n_ *= 2
    LR1 = a32(LPR_O, 32, 32)
    LI1 = a32(LPI_O, 32, 32)
    ts(a32(T1_O), LR1, -1.0, None, ALU.add, None, ["LPR"], ["T1"])
    tt(a32(T2_O), ARE, ARE, ALU.mult, ["PT"], ["T2"])
    tt(a32(DT_O), AIM, AIM, ALU.mult, ["PT"], ["DT"])
    tt(a32(T2_O), a32(T2_O), a32(DT_O), ALU.add, ["T2", "DT"], ["T2"])
    dve(lambda e: e.reciprocal(out=a32(T2_O), in_=a32(T2_O)), ["T2"], ["T2"])
    tt(a32(DT_O), a32(T1_O), ARE, ALU.mult, ["T1", "PT"], ["DT"])
    tt(a32(ANG_O), LI1, AIM, ALU.mult, ["LPI", "PT"], ["ANG"])
    tt(a32(DT_O), a32(DT_O), a32(ANG_O), ALU.add, ["DT", "ANG"], ["DT"])
    tt(a32(CF_O, 32, 0), a32(DT_O), a32(T2_O), ALU.mult, ["DT", "T2"], ["CF"])
    tt(a32(DT_O), LI1, ARE, ALU.mult, ["LPI", "PT"], ["DT"])
    tt(a32(ANG_O), a32(T1_O), AIM, ALU.mult, ["T1", "PT"], ["ANG"])
    tt(a32(DT_O), a32(DT_O), a32(ANG_O), ALU.subtract, ["DT", "ANG"], ["DT"])
    tt(a32(CF_O, 32, 32), a32(DT_O), a32(T2_O), ALU.mult, ["DT", "T2"], ["CF"])
    P.op("sp", lambda e: e.dma_start(out=AV(BR_O, 0, 128, 0, [[16, 32], [1, 16]]),
                                     in_=bass.AP(b_re_d, 0, [[16, 128], [2048, 32], [1, 16]])), w=["BR"], dma=True)
    P.op("sp", lambda e: e.dma_start(out=AV(BI_O, 0, 128, 0, [[16, 32], [1, 16]]),
                                     in_=bass.AP(b_im_d, 0, [[16, 128], [2048, 32], [1, 16]])), w=["BI"], dma=True)
    for ri, cd in enumerate((c_re_d, c_im_d)):
        for pl in range(8):
            for i4 in range(4):
                P.op("sp", lambda e, ri=ri, cd=cd, pl=pl, i4=i4: e.dma_start(
                    out=AV(CIN_O, pl * 16, 16, ri * 512 + i4 * 128, [[64, 2], [1, 64]]),
                    in_=bass.AP(cd, pl * 2048 + i4 * 16384, [[64, 16], [1024, 2], [1, 64]])), w=["CIN"], dma=True)
    for ri, co in enumerate((CR_O, CI_O)):
        b, bt = bank()
        for i4 in range(4):
            pe(lambda e, b=b, ri=ri, i4=i4: e.transpose(
                out=b[:, i4 * 128:(i4 + 1) * 128], in_=AV(CIN_O, 0, 128, ri * 512 + i4 * 128, [[1, 128]]),
                identity=ct(CT_IDENT, 128)), ["CIN", "CTAB"], [bt])
        act(lambda e, b=b, co=co: e.copy(out=a32(co, 512), in_=b[:, 0:512]), [bt], ["CR" if ri == 0 else "CI"])

    def bc16(off, eoff):
        return AV(off, 0, 128, eoff, [[1, 32], [0, 16]])

    def v512(off):
        return AV(off, 0, 128, 0, [[16, 32], [1, 16]])

    def cmul(outr, outi, ar_, ai_, xr, xi, rtoks, wr, wi, neg_im=False):
        tt(v512(TA_O), ar_, xr, ALU.mult, rtoks, ["TA"])
        tt(v512(TB_O), ai_, xi, ALU.mult, rtoks, ["TB"])
        tt(outr, v512(TA_O), v512(TB_O), ALU.subtract, ["TA", "TB"], [wr])
        tt(v512(TA_O), ar_, xi, ALU.mult, rtoks, ["TA"])
        tt(v512(TB_O), ai_, xr, ALU.mult, rtoks, ["TB"])
        if neg_im:
            dve(lambda e: e.scalar_tensor_tensor(out=outi, in0=v512(TA_O), scalar=-1.0, in1=v512(TB_O),
                                                 op0=ALU.mult, op1=ALU.subtract), ["TA", "TB"], [wi])
        else:
            tt(outi, v512(TA_O), v512(TB_O), ALU.add, ["TA", "TB"], [wi])

    cmul(v512(E0R_O), v512(E0I_O), bc16(CF_O, 0), bc16(CF_O, 32), v512(BR_O), v512(BI_O),
         ["CF", "BR", "BI"], "E0R", "E0I")
    dve(lambda e: e.memset(a32(EBR_O, 1024), 0.0), [], ["EBR"])
    dve(lambda e: e.memset(a32(EBI_O, 1024), 0.0), [], ["EBI"])
    dve(lambda e: e.memset(a32(CBR_O, 1024), 0.0), [], ["CBR"])
    dve(lambda e: e.memset(a32(CBI_O, 1024), 0.0), [], ["CBI"])

    def expand(dst_o, src_o, rt, wt):
        for gp in range(2):
            dve(lambda e, gp=gp: e.tensor_copy(
                out=AV(dst_o, gp * 64, 64, gp * 16, [[32, 32], [1, 16]]),
                in_=AV(src_o, gp * 64, 64, 0, [[16, 32], [1, 16]])), [rt, wt], [wt])

    for r_ in range(8):
        cmul(v512(ER_O), v512(EI_O), bc16(LPR_O, 32 * (r_ + 1)), bc16(LPI_O, 32 * (r_ + 1)), v512(CR_O), v512(CI_O),
             ["LPR", "LPI", "CR", "CI"], "ER", "EI", neg_im=True)
        expand(CBR_O, ER_O, "ER", "CBR")
        expand(CBI_O, EI_O, "EI", "CBI")
        for ri, so in enumerate((CBR_O, CBI_O)):
            act(lambda e, r_=r_, ri=ri, so=so: e.copy(
                out=V(WOUT, 0, 128, (r_ * 2 + ri) * 32, [[512, 32], [1, 32]]),
                in_=AV(so, 0, 128, 0, [[32, 32], [1, 32]])), ["CBR" if ri == 0 else "CBI"], ["WOUT"])
    dve(lambda e: e.tensor_copy(out=v512(ER_O), in_=v512(CR_O)), ["CR", "ER"], ["ER"])
    ts(v512(EI_O), v512(CI_O), -1.0, None, ALU.mult, None, ["CI", "EI"], ["EI"])
    expand(CBR_O, ER_O, "ER", "CBR")
    expand(CBI_O, EI_O, "EI", "CBI")
    for tau in range(8):
        cmul(v512(ER_O), v512(EI_O), bc16(LPR_O, 32 * tau), bc16(LPI_O, 32 * tau), v512(E0R_O), v512(E0I_O),
             ["LPR", "LPI", "E0R", "E0I"], "ER", "EI")
        expand(EBR_O, ER_O, "ER", "EBR")
        expand(EBI_O, EI_O, "EI", "EBI")
        s_ = 7 - tau
        for ft in range(8):
            b, bt = bank()
            for ri, so in enumerate((EBR_O, EBI_O)):
                pe(lambda e, b=b, ri=ri, so=so, ft=ft: e.transpose(
                    out=b[:, ri * 128:(ri + 1) * 128], in_=AV(so, 0, 128, ft * 128, [[1, 128]]),
                    identity=ct(CT_IDENT, 128)), ["EBR" if ri == 0 else "EBI", "CTAB"], [bt])
            act(lambda e, b=b, ft=ft, s_=s_: e.copy(
                out=V(WIN, 0, 128, ((ft * 8 + s_) * 2) * 128, [[1, 256]]), in_=b[:, 0:256]), [bt], ["WIN"])
            pe(lambda e, b=b, ft=ft: e.matmul(b[:, 256:384], lhsT=AV(EBR_O, 0, 128, ft * 128, [[1, 128]]),
                                             rhs=AV(CBR_O, 0, 128, ft * 128, [[1, 128]]), start=True, stop=False),
               ["EBR", "CBR"], [bt])
            pe(lambda e, b=b, ft=ft: e.matmul(b[:, 256:384], lhsT=AV(EBI_O, 0, 128, ft * 128, [[1, 128]]),
                                             rhs=AV(CBI_O, 0, 128, ft * 128, [[1, 128]]), start=False, stop=True),
               ["EBI", "CBI"], [bt])
            kdst = V(KW, 0, 128, (ft * 8 + tau) * 128, [[1, 128]])
            if tau == 0:
                tt(AV(TA_O, 0, 128, 0, [[1, 128]]), b[:, 256:384], ct(CT_M16, 128), ALU.mult, [bt, "CTAB"], ["TA"])
                dve(lambda e, kdst=kdst, ft=ft: e.scalar_tensor_tensor(
                    out=kdst, in0=ct(CT_IDENT, 128), scalar=col(0, 16 + ft), in1=AV(TA_O, 0, 128, 0, [[1, 128]]),
                    op0=ALU.mult, op1=ALU.add), ["TA", "CTAB", "COLS"], ["KW"])
            else:
                tt(kdst, b[:, 256:384], ct(CT_M16, 128), ALU.mult, [bt, "CTAB"], ["KW"])
    P.barrier()
    pe(lambda e: e.matmul(PS[0][0:32, 0:32], lhsT=IDB[:, 0:32], rhs=IDB[:, 0:32], start=True, stop=True), ["IDB"], ["ps0"])
    P.barrier()
    if stop == 0:
        NT = 0

    wbn = [0]

    def load_w(dram, r0, c0):
        i = wbn[0] % 3
        wbn[0] += 1
        nm_ = [k for k, v in SRCW.items() if v is dram][0]
        src = WBF[nm_].ap()[r0:r0 + 1024, c0:c0 + 512].rearrange("(c p) f -> p c f", p=128)
        P.op("sp", lambda e, i=i, src=src: e.dma_start(out=V(WB[i], 0, 128, 0, [[512, 8], [1, 512]]), in_=src),
             r=[f"CV_{nm_}_{r0}_{c0}"], w=[f"WB{i}"], dma=True, nobar=True)
        return WB[i], f"WB{i}"

    def rms_to_ht(gl, XS_O, X, XT):
        import os
        ksub = int(os.environ.get("KSUB", "9"))
        if ksub < 1:
            return
        for s in range(2):
            actf(AV(XS_O, 0, 128, 0, [[1, 1024]], BF16), X[:, s * 1024:(s + 1) * 1024], AF.Square,
                 [XT], ["XS", "SMALL"], accum=SMALL[:, s:s + 1])
        if ksub < 2:
            return
        ts(SMALL[:, 2:4], SMALL[:, 0:2], 1.0 / 1024, EPS, ALU.mult, ALU.add, ["SMALL"], ["SMALL"])
        actf(SMALL[:, 2:4], SMALL[:, 2:4], AF.Sqrt, ["SMALL"], ["SMALL"])
        dve(lambda e: e.reciprocal(out=SMALL[:, 4:6], in_=SMALL[:, 2:4]), ["SMALL"], ["SMALL"])
        if ksub < 3:
            return
        for s in range(2):
            ts(AV(XS_O, 0, 128, s * 1024, [[1, 1024]], BF16), X[:, s * 1024:(s + 1) * 1024], SMALL[:, 4 + s:5 + s],
               None, ALU.mult, None, [XT, "SMALL"], ["XS"])
        if ksub < 4:
            return
        for c in range(8):
            b, bt = bank()
            for s in range(2):
                pe(lambda e, b=b, c=c, s=s: e.transpose(
                    out=V(b, 0, 128, s * 128, [[1, 128]], BF16),
                    in_=AV(XS_O, 0, 128, s * 1024 + c * 128, [[1, 128]], BF16), identity=IDB[:]), ["XS", "IDB"], [bt])
            ts(HT[:, c * TT:(c + 1) * TT], V(b, 0, 128, 0, [[1, TT]], BF16), col(0, gl * 8 + c), None, ALU.mult, None,
               [bt, "COLS"], ["HT"])

    def proj_fm(wb, wt, cc, rhs_t, rhs_tok, rhs_fn=None):
        b, bt = bank()
        for c in range(8):
            rhs = rhs_fn(c) if rhs_fn is not None else rhs_t[:, c * TT:(c + 1) * TT]
            pe(lambda e, b=b, c=c, rhs=rhs: e.matmul(b[:, 0:TT], lhsT=V(wb, 0, 128, c * 512 + cc, [[1, 128]]),
                                                     rhs=rhs, start=(c == 0), stop=(c == 7)),
               [wt, rhs_tok], [bt])
        return b, bt

    def proj_tm(wb, wt, s):
        b, bt = bank()
        for c in range(8):
            pe(lambda e, b=b, c=c: e.matmul(b[:, 0:512], lhsT=HT[:, c * TT + s * 128:c * TT + s * 128 + 128],
                                            rhs=V(wb, 0, 128, c * 512, [[1, 512]]), start=(c == 0), stop=(c == 7)),
               [wt, "HT"], [bt])
        return b, bt

    XS_O = 0
    ZT1_O = 0; ZT2_O = 1024; ZR_O = 2048; ZI_O = 3072; ZPR_O = 4096; ZPI_O = 5120
    U0_O = 6144
    SG0_O = 7168
    SAL_O = 8192
    GYB_O = 9280
    ZS_O = 10304
    assert ZS_O + 1024 <= ARENA_W

    def chk(k):
        if stop == k:
            P.barrier()
            raise StopBuild()

    try:
        def load_x(tt_):
            P.op("pool", lambda e, tt_=tt_: e.dma_start(
                out=V(XB[tt_ % 2], 0, 128, 0, [[1024, 2], [1, 1024]]),
                in_=x_d.ap()[tt_ * TT:(tt_ + 1) * TT, :].rearrange("(s p) d -> p s d", p=128)),
                w=[f"X{tt_ % 2}"], dma=True, nobar=True)

        if NT > 0:
            load_x(0)
            rms_to_ht(0, XS_O, XB[0], "X0")
        for t in range(NT):
            tok0 = t * TT
            X = XB[t % 2]
            XT = f"X{t % 2}"
            rp = (t % 2) * 128
            for cs in range(2):
                P.op("pool", lambda e, cs=cs, t=t, rp=rp: e.dma_start(
                    out=ROPE[:, rp + cs * 64:rp + cs * 64 + 64],
                    in_=ctab_d.ap()[:, CT_ROPE + cs * NSUB * 32 + 2 * t * 32:CT_ROPE + cs * NSUB * 32 + 2 * t * 32 + 64]),
                    w=[f"ROPE{t % 2}"], dma=True, nobar=True)
            if t + 1 < NT:
                load_x(t + 1)

            chk(10)
            U0 = lambda p0, pn, eoff, pat: AV(U0_O, p0, pn, eoff, pat, BF16)
            U0M = lambda p0, pn, eoff, pat: AV(ZPR_O, p0, pn, eoff, pat, BF16)
            for half in range(2):
                wb, wt = load_w(w_in_ab_d, 0, half * 512)
                for f in range(4):
                    ft = half * 4 + f
                    b, bt = proj_fm(wb, wt, f * 128, HT, "HT")
                    act(lambda e, b=b, ft=ft: e.copy(out=U0(0, 128, ft * TT, [[1, 32], [32, 8]]),
                                                     in_=V(b, 0, 128, 0, [[8, 32], [1, 8]])), [bt], ["U0"])
                    act(lambda e, b=b, ft=ft: e.copy(out=U0M(64, 64, ft * TT, [[1, 32], [32, 8]]),
                                                     in_=V(b, 64, 64, 0, [[8, 32], [1, 8]])), [bt], ["U0M"])
                    dve(lambda e, ft=ft: e.memset(U0M(64, 32, ft * TT, [[1, TT]]), 0.0), ["U0M"], ["U0M"])
            for half in range(2):
                wb, wt = load_w(w_in_ab_d, 0, 1024 + half * 512)
                for f in range(4):
                    ft = half * 4 + f
                    b, bt = proj_fm(wb, wt, f * 128, HT, "HT")
                    actf(AV(SG0_O, 0, 128, ft * TT, [[1, TT]], BF16), b[:, 0:TT], AF.Silu, [bt], ["SG0"])
            chk(11)
            for q in range(4):
                if q > int(os.environ.get("KQ", "3")):
                    continue
                b, bt = bank()
                for ft in range(8):
                    for ri in range(2):
                        for s in range(8):
                            if q < 3:
                                pe(lambda e, b=b, ri=ri, s=s, ft=ft, q=q: e.matmul(
                                    b[:, ri * 256 + ft * 32: ri * 256 + ft * 32 + 32],
                                    lhsT=V(WIN, 32 * q, 32, ((ft * 8 + s) * 2 + ri) * 128, [[1, 128]]),
                                    rhs=U0(32 * q, 32, ft * TT + s * 32, [[1, 32]]),
                                    start=(s == 0), stop=(s == 7), tile_position=(32 * q, 0)), ["WIN", "U0"], [bt])
                            else:
                                pe(lambda e, b=b, ri=ri, s=s, ft=ft: e.matmul(
                                    b[:, ri * 256 + ft * 32: ri * 256 + ft * 32 + 32],
                                    lhsT=V(WIN, 64, 64, ((ft * 8 + s) * 2 + ri) * 128, [[1, 128]]),
                                    rhs=U0M(64, 64, ft * TT + s * 32, [[1, 32]]),
                                    start=(s == 0), stop=(s == 7), tile_position=(64, 0)), ["WIN", "U0M"], [bt])
                act(lambda e, b=b, q=q: e.copy(out=AV(ZR_O, 0, 128, q * 32, [[128, 8], [1, 32]]),
                                               in_=V(b, 0, 128, 0, [[32, 8], [1, 32]])), [bt], ["ZR"])
                act(lambda e, b=b, q=q: e.copy(out=AV(ZI_O, 0, 128, q * 32, [[128, 8], [1, 32]]),
                                               in_=V(b, 0, 128, 256, [[32, 8], [1, 32]])), [bt], ["ZI"])
            chk(12)
            SAL = lambda ri, lo, n: AV(SAL_O, 0, 128, ri * 1056 + lo, [[33, 32], [1, n]], BF16)
            for ri in range(2):
                dve(lambda e, ri=ri: e.tensor_copy(out=SAL(ri, 0, 1), in_=V(W0, 0, 128, ri * 32, [[1, 32], [1, 1]])),
                    ["W0"], ["SAL"])
            f1k = lambda off: AV(off, 0, 128, 0, [[1, 1024]])
            TCf = V(TCS, 0, 128, 0, [[1, 1024]])
            TSf = V(TCS, 0, 128, 1024, [[1, 1024]])
            tt(f1k(ZT1_O), TCf, f1k(ZR_O), ALU.mult, ["TCS", "ZR"], ["ZT1", "XS"])
            tt(f1k(ZT2_O), TSf, f1k(ZI_O), ALU.mult, ["TCS", "ZI"], ["ZT2"])
            tt(f1k(ZPR_O), f1k(ZT1_O), f1k(ZT2_O), ALU.add, ["ZT1", "ZT2"], ["ZPR", "U0M"])
            tt(f1k(ZT1_O), TCf, f1k(ZI_O), ALU.mult, ["TCS", "ZI"], ["ZT1"])
            tt(f1k(ZT2_O), TSf, f1k(ZR_O), ALU.mult, ["TCS", "ZR"], ["ZT2"])
            tt(f1k(ZPI_O), f1k(ZT1_O), f1k(ZT2_O), ALU.subtract, ["ZT1", "ZT2"], ["ZPI"])
            for ri, zo, wo_, tk in ((0, ZPR_O, ZR_O, "ZPR"), (1, ZPI_O, ZI_O, "ZPI")):
                z0 = AV(zo, 0, 128, 0, [[32, 32], [1, 1]])
                tt(AV(ZT1_O, 0, 128, 0, [[1, 32], [1, 1]]), V(RHO, 0, 128, 0, [[1, 32], [1, 1]]),
                   V(W0, 0, 128, ri * 32, [[1, 32], [1, 1]]), ALU.mult, ["RHO", "W0"], ["ZT1"])
                tt(z0, z0, AV(ZT1_O, 0, 128, 0, [[1, 32], [1, 1]]), ALU.add, [tk, "ZT1"], [tk])
                wtk = "ZR" if ri == 0 else "ZI"
                dve(lambda e, zo=zo, wo_=wo_: e.tensor_tensor_scan(
                    out=f1k(wo_), data0=RHOT[:], data1=f1k(zo), initial=0.0, op0=ALU.mult, op1=ALU.add),
                    [tk, "RHOT"], [wtk])
            tt(f1k(ZT1_O), TCf, f1k(ZR_O), ALU.mult, ["TCS", "ZR"], ["ZT1"])
            tt(f1k(ZT2_O), TSf, f1k(ZI_O), ALU.mult, ["TCS", "ZI"], ["ZT2"])
            tt(f1k(ZPR_O), f1k(ZT1_O), f1k(ZT2_O), ALU.subtract, ["ZT1", "ZT2"], ["ZPR"])
            tt(f1k(ZT1_O), TCf, f1k(ZI_O), ALU.mult, ["TCS", "ZI"], ["ZT1"])
            tt(f1k(ZT2_O), TSf, f1k(ZR_O), ALU.mult, ["TCS", "ZR"], ["ZT2"])
            tt(f1k(ZPI_O), f1k(ZT1_O), f1k(ZT2_O), ALU.add, ["ZT1", "ZT2"], ["ZPI"])
            for ri, zo, tk in ((0, ZPR_O, "ZPR"), (1, ZPI_O, "ZPI")):
                dve(lambda e, ri=ri, zo=zo: e.tensor_copy(out=SAL(ri, 1, 32), in_=AV(zo, 0, 128, 0, [[32, 32], [1, 32]])),
                    [tk], ["SAL"])
                dve(lambda e, ri=ri, zo=zo: e.tensor_copy(out=V(W0, 0, 128, ri * 32, [[1, 32], [1, 1]]),
                                                          in_=AV(zo, 0, 128, 31, [[32, 32], [1, 1]])), [tk], ["W0"])
            chk(13)
            C_G = 0.7978845608028654
            for ft in range(8):
                b, bt = bank()
                bv = lambda p0, pn, lo, n, b=b: V(b, p0, pn, lo, [[8, 32], [1, n]])
                for tau in range(8):
                    nn = (8 - tau) * 32
                    pe(lambda e, b=b, ft=ft, tau=tau, nn=nn: e.matmul(
                        b[:, tau * 32:TT], lhsT=V(KW, 0, 128, (ft * 8 + tau) * 128, [[1, 128]]),
                        rhs=U0(0, 128, ft * TT, [[1, nn]]), start=(tau == 0), stop=False), ["KW", "U0"], [bt])
                for q in range(4):
                    pair = ft * 4 + q
                    for r_ in range(8):
                        for ri in range(2):
                            last = (r_ == 7 and ri == 1)
                            pe(lambda e, b=b, q=q, pair=pair, r_=r_, ri=ri, last=last, bv=bv: e.matmul(
                                V(b, 32 * q, 32, r_ * 32, [[1, 32]]), lhsT=V(WOUT, 0, 128, ((pair * 8 + r_) * 2 + ri) * 32, [[1, 32]]),
                                rhs=AV(SAL_O, 0, 128, ri * 1056 + pair * 33, [[1, 32]], BF16),
                                start=False, stop=last, tile_position=(0, 32 * q)), ["WOUT", "SAL"], [bt])
                t1 = AV(ZT1_O, 0, 128, (ft % 2) * TT, [[1, TT]])
                t2 = AV(ZT2_O, 0, 128, (ft % 2) * TT, [[1, TT]])
                k1, k2 = ("ZT1", "ZT2")
                actf(t1, b[:, 0:TT], AF.Square, [bt], [k1])
                ts(t1, t1, 0.044715, 1.0, ALU.mult, ALU.add, [k1], [k1])
                tt(t2, t1, b[:, 0:TT], ALU.mult, [k1, bt], [k2])
                actf(t2, t2, AF.Sigmoid, [k2], [k2], scale=2.0 * C_G)
                tt(AV(GYB_O, 0, 128, ft * TT, [[1, 8], [8, 32]], BF16), AV(ZT2_O, 0, 128, (ft % 2) * TT, [[32, 8], [1, 32]]),
                   V(b, 0, 128, 0, [[32, 8], [1, 32]]), ALU.mult, [k2, bt], ["GYB"])
            chk(14)
            GYB = AV(GYB_O, 0, 128, 0, [[1, 8 * TT]], BF16)
            for half in range(2):
                wb, wt = load_w(glu_w_d, 0, half * 512)
                for f in range(4):
                    ft = half * 4 + f
                    b, bt = proj_fm(wb, wt, f * 128, GYB, "GYB")
                    zs = AV(ZS_O, 0, 128, ft * TT, [[1, TT]], BF16)
                    actf(zs, b[:, 0:TT], AF.Sigmoid, [bt, "COLS"], ["ZS"], bias=col(0, 24 + ft))
                    tt(zs, zs, AV(GYB_O, 0, 128, ft * TT, [[1, TT]], BF16), ALU.mult, ["ZS", "GYB"], ["ZS"])
                    tt(YC[:, ft * TT:(ft + 1) * TT], zs, AV(SG0_O, 0, 128, ft * TT, [[1, TT]], BF16), ALU.mult,
                       ["ZS", "SG0"], ["YC"])
            P.barrier()
            if stop == 1:
                break

            QR_O = 0; KR_O = 512; KD0_O = 1024; KD1_O = 1536; QD_O = 2048
            QT0_O = 2560; QT1_O = 3072; KT_O = 3584; QDT0_O = 4096; QDT1_O = 4608
            VB_O = 5120
            SMF_O = 6144
            SB_O = 7168
            SGR_O = 8192
            RT_O = 9216
            OF_O = 10240
            ST_O = 9216
            QT_OS = (QT0_O, QT1_O)
            QDT_OS = (QDT0_O, QDT1_O)
            KD_OS = (KD0_O, KD1_O)
            for off, nm in ((QT0_O, "QT0"), (QDT0_O, "QDT0"), (KD0_O, "KD0")):
                dve(lambda e, off=off: e.memset(AV(off, 64, 64, 0, [[1, 1024]], BF16), 0.0), [], [nm])
            for off, nm in ((QT1_O, "QT1"), (QDT1_O, "QDT1"), (KD1_O, "KD1")):
                dve(lambda e, off=off: e.memset(AV(off, 0, 64, 0, [[1, 1024]], BF16), 0.0), [], [nm])
            dve(lambda e: e.memset(AV(SMF_O, 0, 128, 0, [[1, 2048]], BF16), 0.0), [], ["SMF"])
            ropeC = lambda s_: V(ROPE, 0, 128, rp + s_ * 32, [[0, 8], [1, 32]])
            ropeS = lambda s_: V(ROPE, 0, 128, rp + 64 + s_ * 32, [[0, 8], [1, 32]])
            RTK = f"ROPE{t % 2}"
            for qk, col0, dst in ((0, 2048, QR_O), (1, 2560, KR_O)):
                wb, wt = load_w(w_in_ab_d, 0, col0)
                for s in range(2):
                    b, bt = proj_tm(wb, wt, s)
                    sub = s
                    x1 = V(b, 0, 128, 0, [[64, 8], [1, 32]])
                    x2 = V(b, 0, 128, 32, [[64, 8], [1, 32]])
                    r4 = lambda k: AV(RT_O, 0, 128, k * 256, [[32, 8], [1, 32]])
                    tt(r4(0), x1, ropeC(sub), ALU.mult, [bt, RTK], ["RT0"])
                    tt(r4(1), x2, ropeS(sub), ALU.mult, [bt, RTK], ["RT1"])
                    tt(r4(2), x1, ropeS(sub), ALU.mult, [bt, RTK], ["RT2"])
                    tt(r4(3), x2, ropeC(sub), ALU.mult, [bt, RTK], ["RT3"])
                    o1 = AV(dst, 0, 128, s * 512, [[64, 8], [1, 32]], BF16)
                    o2 = AV(dst, 0, 128, s * 512 + 32, [[64, 8], [1, 32]], BF16)
                    tk = "QR" if qk == 0 else "KR"
                    tt(o1, r4(0), r4(1), ALU.subtract, ["RT0", "RT1"], [tk])
                    tt(o2, r4(2), r4(3), ALU.add, ["RT2", "RT3"], [tk])
                    if qk == 0:
                        tt(AV(QD_O, 0, 128, s * 512, [[64, 8], [1, 64]], BF16),
                           AV(dst, 0, 128, s * 512, [[64, 8], [1, 64]], BF16),
                           V(CTAB, 0, 128, CT_QDEC, [[1, 8], [0, 64]]), ALU.mult, [tk, "CTAB"], ["QD"])
                    else:
                        for hf in range(2):
                            tt(AV(KD_OS[hf], 64 * hf, 64, s * 512, [[64, 8], [1, 64]], BF16),
                               AV(dst, 64 * hf, 64, s * 512, [[64, 8], [1, 64]], BF16),
                               V(CTAB, 64 * hf, 64, CT_KDEC, [[1, 8], [0, 64]]), ALU.mult, [tk, "CTAB"], [f"KD{hf}"])
            for src, stk, dsts in ((QR_O, "QR", (QT0_O, QT1_O)), (KR_O, "KR", None), (QD_O, "QD", (QDT0_O, QDT1_O))):
                for s in range(2):
                    b, bt = bank()
                    for hp in range(4):
                        pe(lambda e, b=b, src=src, s=s, hp=hp: e.transpose(
                            out=V(b, 0, 128, hp * 128, [[1, 128]], BF16),
                            in_=AV(src, 0, 128, s * 512 + hp * 128, [[1, 128]], BF16), identity=IDB[:]), [stk, "IDB"], [bt])
                    if dsts is None:
                        act(lambda e, b=b, s=s: e.copy(
                            out=AV(KT_O, 0, 128, s * 128, [[TT, 4], [1, 128]], BF16),
                            in_=V(b, 0, 128, 0, [[128, 4], [1, 128]], BF16)), [bt], ["KT"])
                    else:
                        for hf in range(2):
                            nm = ("QT" if stk == "QR" else "QDT") + str(hf)
                            act(lambda e, b=b, s=s, hf=hf, dsts=dsts: e.copy(
                                out=AV(dsts[hf], 64 * hf, 64, s * 128, [[TT, 4], [1, 128]], BF16),
                                in_=V(b, 64 * hf, 64, 0, [[128, 4], [1, 128]], BF16)), [bt], [nm])
            for half in range(2):
                wb, wt = load_w(w_in_ab_d, 0, 3072 + half * 512)
                for s in range(2):
                    b, bt = proj_tm(wb, wt, s)
                    act(lambda e, b=b, s=s, half=half: e.copy(
                        out=AV(VB_O, 0, 128, s * 1024 + half * 512, [[1, 512]], BF16), in_=b[:, 0:512]), [bt], ["VB"])
            for half in range(2):
                wb, wt = load_w(w_in_ab_d, 0, 4096 + half * 512)
                for f in range(4):
                    ft = half * 4 + f
                    b, bt = proj_fm(wb, wt, f * 128, HT, "HT")
                    actf(AV(SGR_O, 0, 128, ft * TT, [[1, TT]], BF16), b[:, 0:TT], AF.Silu, [bt], ["SGR"])
            for h in range(8):
                hp, par = h // 2, h % 2
                b, bt = bank()
                for c in range(4):
                    s, cpar = c // 2, c % 2
                    tk0 = s * 128 + cpar * 64
                    pe(lambda e, b=b, hp=hp, par=par, s=s, cpar=cpar, tk0=tk0: e.matmul(
                        b[64 * cpar:64 * cpar + 64, s * 64:(s + 1) * 64],
                        lhsT=AV(KT_O, 0, 128, hp * TT + tk0, [[1, 64]], BF16),
                        rhs=AV(QT_OS[par], 0, 128, hp * TT + tk0, [[1, 64]], BF16), start=True, stop=True,
                        tile_position=(0, 64 * cpar)), ["KT", f"QT{par}"], [bt])
                for cpar in range(2):
                    tt(AV(SMF_O, 64 * cpar, 64, h * 256 + cpar * 64, [[128, 2], [1, 64]], BF16),
                       V(b, 64 * cpar, 64, 0, [[64, 2], [1, 64]]),
                       V(CTAB, 64 * cpar, 64, CT_MASK + h * 64, [[0, 2], [1, 64]]), ALU.mult, [bt, "CTAB"], ["SMF"])
            for hp in range(4):
                b, bt = bank()
                for c in range(4):
                    s, cpar = c // 2, c % 2
                    for par in range(2):
                        h = hp * 2 + par
                        pe(lambda e, b=b, c=c, s=s, cpar=cpar, par=par, h=h: e.matmul(
                            b[64 * par:64 * par + 64, c * 128:(c + 1) * 128],
                            lhsT=AV(KD_OS[cpar], 0, 128, s * 512 + h * 64, [[1, 64]], BF16),
                            rhs=AV(VB_O, 0, 128, s * 1024 + h * 128, [[1, 128]], BF16), start=True, stop=True,
                            tile_position=(0, 64 * par)), [f"KD{cpar}", "VB"], [bt])
                for c in range(4):
                    sst = SRET[:, hp * 128:(hp + 1) * 128]
                    dve(lambda e, hp=hp, c=c, sst=sst: e.tensor_copy(
                        out=AV(SB_O, 0, 128, (hp * 4 + c) * 128, [[1, 128]], BF16), in_=sst), ["SRET"], ["SB"])
                    dve(lambda e, b=b, hp=hp, c=c, sst=sst: e.scalar_tensor_tensor(
                        out=sst, in0=sst, scalar=V(CTAB, 0, 128, CT_G64 + hp, [[1, 1]]), in1=b[:, c * 128:(c + 1) * 128],
                        op0=ALU.mult, op1=ALU.add), ["SRET", "CTAB", bt], ["SRET"])
            for h in range(8):
                hp, par = h // 2, h % 2
                b, bt = bank()
                for s in range(2):
                    pe(lambda e, b=b, h=h, s=s: e.matmul(
                        b[:, s * 128:(s + 1) * 128], lhsT=AV(VB_O, 0, 128, s * 1024 + h * 128, [[1, 128]], BF16),
                        rhs=AV(SMF_O, 0, 128, h * 256 + s * 128, [[1, 128]], BF16), start=(s == 0), stop=False),
                       ["VB", "SMF"], [bt])
                for c in range(4):
                    tk0 = (c // 2) * 128 + (c % 2) * 64
                    pe(lambda e, b=b, hp=hp, par=par, c=c, tk0=tk0: e.matmul(
                        b[:, c * 64:(c + 1) * 64], lhsT=AV(SB_O, 0, 128, (hp * 4 + c) * 128, [[1, 128]], BF16),
                        rhs=AV(QDT_OS[par], 0, 128, hp * TT + tk0, [[1, 64]], BF16), start=False, stop=(c == 3)),
                       ["SB", f"QDT{par}"], [bt])
                ob = (h % 2) * 512
                OF = AV(OF_O, 0, 128, ob, [[1, TT]])
                OFB = AV(OF_O, 0, 128, 2 * (ob + 256), [[1, TT]], BF16)
                OSQ = AV(OF_O, 0, 128, 2 * (ob + 384), [[1, TT]], BF16)
                otk = f"OF{h % 2}"
                act(lambda e, b=b, OF=OF: e.copy(out=OF, in_=b[:, 0:TT]), [bt], [otk])
                act(lambda e, b=b, OFB=OFB: e.copy(out=OFB, in_=b[:, 0:TT]), [bt], [otk])
                actf(OSQ, b[:, 0:TT], AF.Square, [bt], [otk])
                b2, bt2 = bank()
                pe(lambda e, b2=b2, OFB=OFB: e.matmul(b2[:, 0:TT], lhsT=ONESB[:], rhs=OFB, start=True, stop=True),
                   [otk, "ONESB"], [bt2])
                pe(lambda e, b2=b2, OSQ=OSQ: e.matmul(b2[:, 256:256 + TT], lhsT=ONESB[:], rhs=OSQ, start=True, stop=True),
                   [otk, "ONESB"], [bt2])
                sm = AV(ST_O, 0, 128, 0, [[1, TT]])
                sv = AV(ST_O, 0, 128, 256, [[1, TT]])
                sx = AV(ST_O, 0, 128, 512, [[1, TT]])
                ts(sm, b2[:, 0:TT], 1.0 / 128, None, ALU.mult, None, [bt2], ["ST0", "RT0"])
                tt(sv, sm, sm, ALU.mult, ["ST0"], ["ST1", "RT1"])
                dve(lambda e, b2=b2, sv=sv: e.scalar_tensor_tensor(out=sv, in0=b2[:, 256:256 + TT], scalar=1.0 / 128, in1=sv,
                                                                   op0=ALU.mult, op1=ALU.subtract), [bt2, "ST1"], ["ST1"])
                ts(sv, sv, EPS, None, ALU.add, None, ["ST1"], ["ST1"])
                actf(sv, sv, AF.Sqrt, ["ST1"], ["ST1"])
                dve(lambda e, sv=sv: e.reciprocal(out=sv, in_=sv), ["ST1"], ["ST1"])
                tt(sx, OF, sm, ALU.subtract, [otk, "ST0"], ["ST2", "RT2"])
                tt(sx, sx, sv, ALU.mult, ["ST2", "ST1"], ["ST2"])
                tt(YC[:, (8 + h) * TT:(9 + h) * TT], sx, AV(SGR_O, 0, 128, h * TT, [[1, TT]], BF16), ALU.mult,
                   ["ST2", "SGR"], ["YC"])
            for ns in range(2):
                wa, wat = load_w(w_out_ab_d, 0, ns * 512)
                wb2, wbt = load_w(w_out_ab_d, 1024, ns * 512)
                for s in range(2):
                    b, bt = bank()
                    for kc in range(16):
                        w_, wt_ = (wa, wat) if kc < 8 else (wb2, wbt)
                        pe(lambda e, b=b, kc=kc, s=s, w_=w_: e.matmul(
                            b[:, 0:512], lhsT=YC[:, kc * TT + s * 128:kc * TT + s * 128 + 128],
                            rhs=V(w_, 0, 128, (kc % 8) * 512, [[1, 512]]), start=(kc == 0), stop=(kc == 15)),
                           ["YC", wt_], [bt])
                    xs_ = X[:, s * 1024 + ns * 512:s * 1024 + ns * 512 + 512]
                    tt(xs_, xs_, b[:, 0:512], ALU.add, [XT, bt], [XT])
            if debug:
                P.op("pool", lambda e, tok0=tok0, X=X: e.dma_start(
                    out=x1_d.ap()[tok0:tok0 + TT, :].rearrange("(s p) d -> p s d", p=128),
                    in_=V(X, 0, 128, 0, [[1024, 2], [1, 1024]])), r=[XT], w=["OUT"], dma=True)
            P.barrier()
            if stop == 2:
                break

            L1XS_O = 0
            SG1_O = 1024
            SIG_O = 2048
            VV_O = 2560
            VSQ_O = 4608
            LST_O = 5120
            Y1_O = 6144
            LT_O = 7168
            rms_to_ht(1, L1XS_O, X, XT)
            for ft in range(8):
                dve(lambda e, ft=ft: e.tensor_copy(out=U1[:, ft * 288 + 2:ft * 288 + 32],
                                                   in_=U1[:, ft * 288 + 258:ft * 288 + 288]), ["U1"], ["U1"])
            for half in range(2):
                wa, wat = load_w(w_in_c_d, 0, half * 512)
                wb2, wbt = load_w(w_in_c_d, 0, 1024 + half * 512)
                for f in range(4):
                    ft = half * 4 + f
                    ba, bat = proj_fm(wa, wat, f * 128, HT, "HT")
                    bb, bbt = proj_fm(wb2, wbt, f * 128, HT, "HT")
                    sg = AV(SIG_O, 0, 128, (ft % 2) * 256, [[1, TT]])
                    actf(sg, bb[:, 0:TT], AF.Sigmoid, [bbt], [f"SIG{ft % 2}"])
                    tt(U1[:, ft * 288 + 32:ft * 288 + 288], ba[:, 0:TT], sg, ALU.mult, [bat, f"SIG{ft % 2}"], ["U1"])
            for half in range(2):
                wb, wt = load_w(w_in_c_d, 0, 2048 + half * 512)
                for f in range(4):
                    ft = half * 4 + f
                    b, bt = proj_fm(wb, wt, f * 128, HT, "HT")
                    actf(AV(SG1_O, 0, 128, ft * TT, [[1, TT]], BF16), b[:, 0:TT], AF.Silu, [bt], ["SG1"])
            if t + 1 < NT:
                rms_to_ht(0, L1XS_O, XB[(t + 1) % 2], f"X{(t + 1) % 2}")
            dn = 0
            bsum, bsumt = PS[7], "ps7"
            for ft in range(8):
                b, bt = bank()
                di = wbn[0] % 3
                wbn[0] += 1
                P.op("sp", lambda e, di=di, ft=ft: e.dma_start(out=V(WB[di], 0, 128, 0, [[1, 31 * 128]]), in_=DG.ap()[ft]),
                     r=[f"DG{ft}"], w=[f"WB{di}"], dma=True, nobar=True)
                for k in range(31):
                    pe(lambda e, b=b, di=di, ft=ft, k=k: e.matmul(
                        b[:, 0:TT], lhsT=V(WB[di], 0, 128, k * 128, [[1, 128]]),
                        rhs=U1[:, ft * 288 + 2 + k:ft * 288 + 2 + k + TT],
                        start=(k == 0), stop=(k == 30)), [f"WB{di}", "U1"], [bt])
                vv = AV(VV_O, 0, 128, ft * TT, [[1, TT]])
                actf(vv, b[:, 0:TT], AF.Identity, [bt, "COLS"], ["VV"], bias=col(0, 32 + ft))
                vvb = AV(VSQ_O, 0, 128, 2 * ((ft % 2) * 256), [[1, TT]], BF16)
                vsq = AV(VSQ_O, 0, 128, 2 * ((ft % 2) * 256 + 128), [[1, TT]], BF16)
                dve(lambda e, vvb=vvb, vv=vv: e.tensor_copy(out=vvb, in_=vv), ["VV"], [f"VSQ{ft % 2}"])
                actf(vsq, vv, AF.Square, ["VV"], [f"VSQ{ft % 2}"])
                pe(lambda e, vvb=vvb, ft=ft: e.matmul(bsum[:, 0:TT], lhsT=ONESB[:], rhs=vvb,
                                                     start=(ft == 0), stop=False), [f"VSQ{ft % 2}", "ONESB"], [bsumt])
                pe(lambda e, vsq=vsq, ft=ft: e.matmul(bsum[:, 256:256 + TT], lhsT=ONESB[:], rhs=vsq,
                                                     start=False, stop=(ft == 7)), [f"VSQ{ft % 2}", "ONESB"], [bsumt])
            sm = AV(LST_O, 0, 128, 0, [[1, TT]])
            sv = AV(LST_O, 0, 128, 256, [[1, TT]])
            ts(sm, bsum[:, 0:TT], 1.0 / 1024, None, ALU.mult, None, [bsumt], ["LST0"])
            tt(sv, sm, sm, ALU.mult, ["LST0"], ["LST1"])
            dve(lambda e: e.scalar_tensor_tensor(out=sv, in0=bsum[:, 256:256 + TT], scalar=1.0 / 1024, in1=sv,
                                                 op0=ALU.mult, op1=ALU.subtract), [bsumt, "LST1"], ["LST1"])
            ts(sv, sv, EPS, None, ALU.add, None, ["LST1"], ["LST1"])
            actf(sv, sv, AF.Sqrt, ["LST1"], ["LST1"])
            dve(lambda e: e.reciprocal(out=sv, in_=sv), ["LST1"], ["LST1"])
            for ft in range(8):
                vv = AV(VV_O, 0, 128, ft * TT, [[1, TT]])
                lt = AV(LT_O, 0, 128, (ft % 2) * 256, [[1, TT]])
                ltk = f"LT{ft % 2}"
                tt(lt, vv, sm, ALU.subtract, ["VV", "LST0"], [ltk])
                tt(lt, lt, sv, ALU.mult, [ltk, "LST1"], [ltk])
                actf(lt, lt, AF.Silu, [ltk, "COLS"], [ltk], scale=col(0, 40 + ft), bias=col(0, 48 + ft))
                tt(AV(Y1_O, 0, 128, ft * TT, [[1, TT]], BF16), lt, AV(SG1_O, 0, 128, ft * TT, [[1, TT]], BF16), ALU.mult,
                   [ltk, "SG1"], ["Y1"])
            for ns in range(2):
                wb, wt = load_w(w_out_c_d, 0, ns * 512)
                for s in range(2):
                    b, bt = bank()
                    for kc in range(8):
                        pe(lambda e, b=b, kc=kc, s=s, wb=wb: e.matmul(
                            b[:, 0:512], lhsT=AV(Y1_O, 0, 128, kc * TT + s * 128, [[1, 128]], BF16),
                            rhs=V(wb, 0, 128, kc * 512, [[1, 512]]), start=(kc == 0), stop=(kc == 7)), ["Y1", wt], [bt])
                    xs_ = X[:, s * 1024 + ns * 512:s * 1024 + ns * 512 + 512]
                    tt(xs_, xs_, b[:, 0:512], ALU.add, [XT, bt], [XT])
            for s in range(2):
                actf(AV(L1XS_O, 0, 128, 0, [[1, 1024]], BF16), X[:, s * 1024:(s + 1) * 1024], AF.Square,
                     [XT], ["XS", "SMALL"], accum=SMALL[:, 8 + s:9 + s])
            ts(SMALL[:, 10:12], SMALL[:, 8:10], 1.0 / 1024, EPS, ALU.mult, ALU.add, ["SMALL"], ["SMALL"])
            actf(SMALL[:, 10:12], SMALL[:, 10:12], AF.Sqrt, ["SMALL"], ["SMALL"])
            dve(lambda e: e.reciprocal(out=SMALL[:, 12:14], in_=SMALL[:, 10:12]), ["SMALL"], ["SMALL"])
            for s in range(2):
                xs_ = X[:, s * 1024:(s + 1) * 1024]
                dve(lambda e, xs_=xs_, s=s: e.scalar_tensor_tensor(out=xs_, in0=xs_, scalar=SMALL[:, 12 + s:13 + s], in1=FG[:],
                                                                   op0=ALU.mult, op1=ALU.mult), [XT, "SMALL", "FG"], [XT])
            P.op("pool", lambda e, tok0=tok0, X=X: e.dma_start(
                out=out_d.ap()[tok0:tok0 + TT, :].rearrange("(s p) d -> p s d", p=128),
                in_=V(X, 0, 128, 0, [[1024, 2], [1, 1024]])), r=[XT], w=["OUT"], dma=True)
            P.barrier()

    except StopBuild:
        pass
    P.op("sp", lambda e: e.nop(), r=["OUT"])
    P.barrier()
    P.emit()
    P.close()
    return nc


def prep_inputs(inp, b, NT):
    T = NT * TT
    f = lambda a: np.ascontiguousarray(np.asarray(a, dtype=np.float32))
    return {
        "x": f(inp["x"][b, :T]),
        "norm_g": f(inp["norm_g"]).reshape(16, 128),
        "final_g": f(inp["final_g"]),
        "w_in_ab": f(inp["w_in_ab"][0]),
        "a_re": f(inp["s5_a_re"][0]).reshape(32, 128),
        "a_im": f(inp["s5_a_im"][0]).reshape(32, 128),
        "log_dt": f(inp["s5_log_dt"][0]).reshape(32, 2),
        "b_re": f(inp["s5_b_re"][0]).reshape(-1),
        "b_im": f(inp["s5_b_im"][0]).reshape(-1),
        "c_re": f(inp["s5_c_re"][0]).reshape(-1),
        "c_im": f(inp["s5_c_im"][0]).reshape(-1),
        "s5_d": f(inp["s5_d"][0]).reshape(8, 128),
        "glu_w": f(inp["s5_glu_w"][0]),
        "glu_b": f(inp["s5_glu_b"][0]).reshape(8, 128),
        "w_out_ab": f(inp["w_out_ab"][0]),
        "w_in_c": f(inp["w_in_c"][0]),
        "conv_w": f(inp["conv_w"][0]).reshape(248, 128),
        "conv_b": f(inp["conv_b"][0]).reshape(8, 128),
        "ln_g": f(inp["conv_ln_g"][0]).reshape(8, 128),
        "ln_b": f(inp["conv_ln_b"][0]).reshape(8, 128),
        "w_out_c": f(inp["w_out_c"][0]),
        "ctab": make_ctab(NT * 2),
    }


def kernel(**inputs):
    NT = 16
    nc = build(NT)
    in_maps = [prep_inputs(inputs, b, NT) for b in range(8)]
    res = run_bass_kernel_spmd(nc, in_maps, core_ids=list(range(8)))
    return np.stack([np.asarray(r["out"]) for r in res.results], axis=0).astype(np.float32)
```

```python
import math
import os
import numpy as np
import concourse.bass as bass
import concourse.mybir as mybir
from concourse.bass_utils import run_bass_kernel_spmd
from contextlib import ExitStack

F32 = mybir.dt.float32
BF16 = mybir.dt.bfloat16
AF = mybir.ActivationFunctionType
ALU = mybir.AluOpType
AX = mybir.AxisListType

TT = 256
EPS = 1e-6


class StopBuild(Exception):
    pass


class Prog:
    COMPUTE = ("pe", "act", "dve", "pool")
    RING = 8

    def __init__(self, nc):
        self.nc = nc
        self.ops = []
        self.lw = {}
        self.rd = {}
        self.ndma = {"sp": 0, "act": 0, "pool": 0}
        self.stack = ExitStack()
        self.last = {}
        self.pending_dma = []

    def sb(self, name, shape, dt):
        return self.stack.enter_context(self.nc.sbuf_tensor(name, list(shape), dt))

    def ps(self, name, shape, dt):
        return self.stack.enter_context(self.nc.psum_tensor(name, list(shape), dt))

    def op(self, eng, fn, r=(), w=(), dma=False, nobar=False, extra=()):
        i = len(self.ops)
        raw = set()
        oth = set()
        for t in r:
            if t in self.lw:
                raw.add(self.lw[t])
        for t in w:
            if t in self.lw:
                oth.add(self.lw[t])
            for _, j in self.rd.get(t, {}).items():
                oth.add(j)
        deps = set(extra)
        for j in raw | oth:
            oj = self.ops[j]
            if (not dma) and (not oj["dma"]) and oj["eng"] == eng and eng == "pe" and j not in raw:
                continue
            deps.add(j)
        o = dict(eng=eng, fn=fn, deps=deps, dma=dma, sig=False, val=None, slot=None)
        if dma:
            o["slot"] = self.ndma[eng] % self.RING
            self.ndma[eng] += 1
            if not nobar:
                self.pending_dma.append(i)
        else:
            self.last[eng] = i
        self.ops.append(o)
        key = ("dma", eng, i) if dma else eng
        for t in r:
            self.rd.setdefault(t, {})[key] = i
        for t in w:
            self.lw[t] = i
            self.rd[t] = {}
        return i

    def barrier(self):
        deps = set(self.last.values()) | set(self.pending_dma)
        self.pending_dma = []
        for e in ("pe", "act", "dve", "pool", "sp"):
            self.op(e, lambda g: g.nop(), extra=deps)

    def emit(self):
        nc = self.nc
        ops = self.ops
        for o in ops:
            for j in o["deps"]:
                ops[j]["sig"] = True
        cnt = {e: 0 for e in self.COMPUTE + ("sp",)}
        dcnt = {}
        for o in ops:
            if o["dma"]:
                k = (o["eng"], o["slot"])
                dcnt[k] = dcnt.get(k, 0) + 16
                o["val"] = dcnt[k]
            elif o["sig"]:
                cnt[o["eng"]] += 1
                o["val"] = cnt[o["eng"]]
        st = self.stack
        csem = {e: st.enter_context(nc.semaphore(f"s_{e}")) for e in self.COMPUTE + ("sp",)}
        dsem = {}
        for q in ("sp", "act", "pool"):
            for s in range(min(self.RING, self.ndma[q])):
                dsem[(q, s)] = st.enter_context(nc.semaphore(f"d_{q}{s}"))
        block = st.enter_context(nc.Block())
        per = {e: [] for e in ("pe", "act", "dve", "pool", "sp")}
        for i, o in enumerate(ops):
            per[o["eng"]].append(i)

        def semof(o):
            if o["dma"]:
                return dsem[(o["eng"], o["slot"])]
            return csem[o["eng"]]

        def run(e, eng):
            waited = {}
            for i in per[e]:
                o = ops[i]
                need = {}
                for j in o["deps"]:
                    oj = ops[j]
                    s = semof(oj)
                    k = id(s)
                    if waited.get(k, 0) >= oj["val"]:
                        continue
                    if k not in need or need[k][1] < oj["val"]:
                        need[k] = (s, oj["val"])
                if o["dma"]:
                    s = semof(o)
                    prev = o["val"] - 16
                    if prev > 0 and waited.get(id(s), 0) < prev:
                        if id(s) not in need or need[id(s)][1] < prev:
                            need[id(s)] = (s, prev)
                for k, (s, v) in need.items():
                    eng.wait_ge(s, v)
                    waited[k] = v
                ins = o["fn"](eng)
                if o["dma"]:
                    ins.then_inc(semof(o), 16)
                elif o["sig"]:
                    ins.then_inc(semof(o), 1)

        if per["pe"]:
            @block.tensor
            def _(eng):
                run("pe", eng)
        if per["act"]:
            @block.scalar
            def _(eng):
                run("act", eng)
        if per["dve"]:
            @block.vector
            def _(eng):
                run("dve", eng)
        if per["pool"]:
            @block.gpsimd
            def _(eng):
                run("pool", eng)
        if per["sp"]:
            @block.sync
            def _(eng):
                run("sp", eng)

    def close(self):
        self.stack.close()


def V(t, p0, pn, off, pat, dt=None):
    a = t[:] if dt is None else t[:].bitcast(dt)
    base = a[p0:p0 + pn, off:off + 1]
    return bass.AP(base.tensor, base.offset, [list(base.ap[0])] + [list(x) for x in pat])


CT_IDENT = 0
CT_M16 = 128
CT_ONES = 256
CT_MASK = 384
CT_KDEC = CT_MASK + 512
CT_QDEC = CT_KDEC + 8
CT_G64 = CT_QDEC + 8
CT_ROPE = CT_G64 + 8


def make_ctab(nsub):
    n = CT_ROPE + 2 * nsub * 32
    c = np.zeros((128, n), np.float64)
    c[:, CT_IDENT:CT_IDENT + 128] = np.eye(128)
    r = np.arange(128)
    c[:, CT_M16:CT_M16 + 128] = (r[:, None] // 16 == r[None, :] // 16)
    c[:, CT_ONES:CT_ONES + 128] = 1.0
    gam = 1.0 - 2.0 ** (-5.0 - np.arange(8))
    j = r % 64
    i = np.arange(64)
    m = gam[None, :, None] ** np.abs(i[None, None, :] - j[:, None, None]) * (64 ** -0.5)
    c[:, CT_MASK:CT_MASK + 512] = m.reshape(128, 512)
    c[:, CT_KDEC:CT_KDEC + 8] = gam[None, :] ** (63 - j[:, None])
    c[:, CT_QDEC:CT_QDEC + 8] = gam[None, :] ** (j[:, None] + 1.0) * (64 ** -0.5)
    par = r // 64
    for hp in range(4):
        c[:, CT_G64 + hp] = gam[2 * hp + par] ** 64
    freqs = 10000.0 ** (-np.arange(32) / 32.0)
    pos = (np.arange(nsub)[None, :] * 128 + r[:, None]).astype(np.float64)
    ang = pos[:, :, None] * freqs[None, None, :]
    c[:, CT_ROPE:CT_ROPE + nsub * 32] = np.cos(ang).reshape(128, -1)
    c[:, CT_ROPE + nsub * 32:CT_ROPE + 2 * nsub * 32] = np.sin(ang).reshape(128, -1)
    return c.astype(np.float32)


def build(NT, debug=False, stop=99):
    nc = bass.Bass("TRN2", target_bir_lowering=False)
    P = Prog(nc)
    T = NT * TT
    NSUB = NT * 2
    NCT = CT_ROPE + 2 * NSUB * 32

    def din(name, shape):
        return nc.dram_tensor(name, list(shape), F32, kind="ExternalInput")

    x_d = din("x", [T, 1024])
    norm_g_d = din("norm_g", [16, 128])
    final_g_d = din("final_g", [1024])
    w_in_ab_d = din("w_in_ab", [1024, 5120])
    a_re_d = din("a_re", [32, 128])
    a_im_d = din("a_im", [32, 128])
    log_dt_d = din("log_dt", [32, 2])
    b_re_d = din("b_re", [64 * 64 * 16])
    b_im_d = din("b_im", [64 * 64 * 16])
    c_re_d = din("c_re", [64 * 16 * 64])
    c_im_d = din("c_im", [64 * 16 * 64])
    s5_d_d = din("s5_d", [8, 128])
    glu_w_d = din("glu_w", [1024, 1024])
    glu_b_d = din("glu_b", [8, 128])
    w_out_ab_d = din("w_out_ab", [2048, 1024])
    w_in_c_d = din("w_in_c", [1024, 3072])
    conv_w_d = din("conv_w", [248, 128])
    conv_b_d = din("conv_b", [8, 128])
    ln_g_d = din("ln_g", [8, 128])
    ln_b_d = din("ln_b", [8, 128])
    w_out_c_d = din("w_out_c", [1024, 1024])
    ctab_d = din("ctab", [128, NCT])
    out_d = nc.dram_tensor("out", [T, 1024], F32, kind="ExternalOutput")
    DG = nc.dram_tensor("dg_conv", [8, 128, 31 * 128], BF16, kind="Internal")
    WBF = {}
    for nm_, d_ in (("w_in_ab", w_in_ab_d), ("glu_w", glu_w_d), ("w_out_ab", w_out_ab_d),
                    ("w_in_c", w_in_c_d), ("w_out_c", w_out_c_d)):
        WBF[nm_] = nc.dram_tensor("bf_" + nm_, list(d_.shape), BF16, kind="Internal")
    if debug:
        x1_d = nc.dram_tensor("x1", [T, 1024], F32, kind="ExternalOutput")

    CTAB = P.sb("CTAB", [128, CT_ROPE], F32)
    ROPE = P.sb("ROPE", [128, 256], F32)
    IDB = P.sb("IDB", [128, 128], BF16)
    ONESB = P.sb("ONESB", [128, 128], BF16)
    COLS = P.sb("COLS", [128, 384], F32)
    FG = P.sb("FG", [128, 1024], F32)
    KW = P.sb("KW", [128, 64 * 128], BF16)
    WIN = P.sb("WIN", [128, 128 * 128], BF16)
    WOUT = P.sb("WOUT", [128, 512 * 32], BF16)
    TCS = P.sb("TCS", [128, 2 * 1024], F32)
    RHOT = P.sb("RHOT", [128, 1024], F32)
    RHO = P.sb("RHO", [128, 32], F32)
    W0 = P.sb("W0", [128, 64], F32)
    SRET = P.sb("SRET", [128, 512], F32)
    XB = [P.sb(f"X{i}", [128, 2 * 1024], F32) for i in range(2)]
    HT = P.sb("HT", [128, 8 * TT], BF16)
    WB = [P.sb(f"WB{i}", [128, 8 * 512], BF16) for i in range(3)]
    YC = P.sb("YC", [128, 16 * TT], BF16)
    U1 = P.sb("U1", [128, 8 * 288], BF16)
    SMALL = P.sb("SMALL", [128, 64], F32)
    DIAG = [P.sb(f"DIAG{i}", [128, 128], BF16) for i in range(4)]
    ARENA_W = 11520
    AR = P.sb("ARENA", [128, ARENA_W], F32)
    PS = [P.ps(f"ps{i}", [128, 512], F32) for i in range(8)]
    psn = [0]

    def bank():
        i = psn[0] % 7
        psn[0] += 1
        return PS[i], f"ps{i}"

    def ar(off_words, dt=F32):
        return off_words * (2 if dt == BF16 else 1)

    def AV(off_words, p0, pn, eoff, pat, dt=F32):
        if off_words >= 200000:
            return V(WB[1], p0, pn, (off_words - 200000) + eoff, pat, F32)
        if off_words >= 100000:
            return V(WB[0], p0, pn, (off_words - 100000) + eoff, pat, F32)
        return V(AR, p0, pn, ar(off_words, dt) + eoff, pat, dt if dt != F32 else None)

    def ct(off, n, p0=0, pn=128):
        return V(CTAB, p0, pn, off, [[1, n]])

    def col(chunk, c, pn=128):
        return V(COLS, 0, pn, chunk * 128 + c, [[1, 1]])

    dve = lambda fn, r, w: P.op("dve", fn, r, w)
    act = lambda fn, r, w: P.op("act", fn, r, w)
    pe = lambda fn, r, w: P.op("pe", fn, r, w)
    pool = lambda fn, r, w: P.op("pool", fn, r, w)

    def ptt(out, in0, in1, op, r, w):
        pool(lambda e: e.tensor_tensor(out=out, in0=in0, in1=in1, op=op), r, w)

    def tt(out, in0, in1, op, r, w):
        dve(lambda e: e.tensor_tensor(out=out, in0=in0, in1=in1, op=op), r, w)

    def ts(out, in0, s1, s2, op0, op1, r, w):
        if op1 is None:
            dve(lambda e: e.tensor_scalar(out=out, in0=in0, scalar1=s1, scalar2=None, op0=op0), r, w)
        else:
            dve(lambda e: e.tensor_scalar(out=out, in0=in0, scalar1=s1, scalar2=s2, op0=op0, op1=op1), r, w)

    def actf(out, in_, func, r, w, scale=None, bias=None, accum=None):
        kw = {}
        if scale is not None:
            kw["scale"] = scale
        if bias is not None:
            kw["bias"] = bias
        if accum is not None:
            kw["accum_out"] = accum
        act(lambda e: e.activation(out=out, in_=in_, func=func, **kw), r, w)

    conv_order = []
    for c0 in range(0, 5120, 512):
        conv_order.append(("w_in_ab", 0, c0))
    for c0 in (0, 512):
        conv_order.append(("glu_w", 0, c0))
    for c0 in (0, 512):
        conv_order.append(("w_out_ab", 0, c0))
        conv_order.append(("w_out_ab", 1024, c0))
    for c0 in range(0, 3072, 512):
        conv_order.append(("w_in_c", 0, c0))
    for c0 in (0, 512):
        conv_order.append(("w_out_c", 0, c0))
    SRCW = {"w_in_ab": w_in_ab_d, "glu_w": glu_w_d, "w_out_ab": w_out_ab_d, "w_in_c": w_in_c_d, "w_out_c": w_out_c_d}
    for (nm_, r0, c0) in conv_order:
        P.op("pool", lambda e, nm_=nm_, r0=r0, c0=c0: e.dma_start(
            out=WBF[nm_].ap()[r0:r0 + 1024, c0:c0 + 512], in_=SRCW[nm_].ap()[r0:r0 + 1024, c0:c0 + 512]),
            w=[f"CV_{nm_}_{r0}_{c0}"], dma=True, nobar=True)
    P.op("sp", lambda e: e.dma_start(out=CTAB[:], in_=ctab_d.ap()[:, 0:CT_ROPE]), w=["CTAB"], dma=True)
    dve(lambda e: e.tensor_copy(out=IDB[:], in_=ct(CT_IDENT, 128)), ["CTAB"], ["IDB"])
    dve(lambda e: e.tensor_copy(out=ONESB[:], in_=ct(CT_ONES, 128)), ["CTAB"], ["ONESB"])
    P.op("sp", lambda e: e.dma_start(out=FG[:], in_=bass.AP(final_g_d, 0, [[0, 128], [1, 1024]])), w=["FG"], dma=True)
    dve(lambda e: e.memset(W0[:], 0.0), [], ["W0"])
    dve(lambda e: e.memset(SRET[:], 0.0), [], ["SRET"])
    dve(lambda e: e.memset(U1[:], 0.0), [], ["U1"])
    dve(lambda e: e.memset(SMALL[:], 0.0), [], ["SMALL"])
    dve(lambda e: e.memset(SMALL[:, 60:61], EPS), ["SMALL"], ["SMALL"])
    dve(lambda e: e.memset(SMALL[:, 61:62], math.pi / 2), ["SMALL"], ["SMALL"])
    EPSC = SMALL[:, 60:61]
    HPIC = SMALL[:, 61:62]

    STG_O = 0
    dve(lambda e: e.memset(AV(STG_O, 0, 128, 0, [[1, 384]]), 0.0), [], ["STG"])
    pieces = [(norm_g_d, 16, 0, 0), (s5_d_d, 8, 0, 16), (glu_b_d, 8, 0, 24), (conv_b_d, 8, 0, 32),
              (ln_g_d, 8, 0, 40), (ln_b_d, 8, 0, 48)]
    for (d, n, ch, r0) in pieces:
        P.op("sp", lambda e, d=d, n=n, ch=ch, r0=r0: e.dma_start(
            out=AV(STG_O, r0, n, ch * 128, [[1, 128]]), in_=d.ap()), r=["STG"], w=["STG"], dma=True)
    P.op("sp", lambda e: e.dma_start(out=AV(STG_O, 0, 128, 128, [[1, 128]]), in_=conv_w_d.ap()[0:128, :]),
         r=["STG"], w=["STG"], dma=True)
    P.op("sp", lambda e: e.dma_start(out=AV(STG_O, 0, 120, 256, [[1, 128]]), in_=conv_w_d.ap()[128:248, :]),
         r=["STG"], w=["STG"], dma=True)
    for ch in range(3):
        b, bt = bank()
        pe(lambda e, b=b, ch=ch: e.transpose(out=b[:, 0:128], in_=AV(STG_O, 0, 128, ch * 128, [[1, 128]]),
                                             identity=ct(CT_IDENT, 128)), ["STG", "CTAB"], [bt])
        act(lambda e, b=b, ch=ch: e.copy(out=COLS[:, ch * 128:(ch + 1) * 128], in_=b[:, 0:128]), [bt], ["COLS"])

    def cw_col(k, c):
        row = k * 8 + c
        return col(1 + row // 128, row % 128)

    DGS_O = 8800
    for ft in range(8):
        for k in range(31):
            ts(AV(DGS_O, 0, 128, k * 128, [[1, 128]], BF16), IDB[:], cw_col(k, ft), None, ALU.mult, None,
               ["IDB", "COLS", "DGS"], ["DGS"])
        P.op("sp", lambda e, ft=ft: e.dma_start(out=DG.ap()[ft], in_=AV(DGS_O, 0, 128, 0, [[1, 31 * 128]], BF16)),
             r=["DGS"], w=[f"DG{ft}"], dma=True)

    o = 384
    PRM_O = o; o += 384
    LDT_O = o; o += 2
    PT_O = o; o += 96
    DT_O = o; o += 32
    ADT_O = o; o += 32
    ANG_O = o; o += 32
    SC_O = o; o += 64
    T1_O = o; o += 32
    T2_O = o; o += 32
    UPC_O = o; o += 288
    UPS_O = o; o += 288
    MG_O = o; o += 288
    LPR_O = o; o += 288
    LPI_O = o; o += 288
    CF_O = o; o += 64
    BR_O = o; o += 512
    BI_O = o; o += 512
    E0R_O = o; o += 512
    E0I_O = o; o += 512
    CIN_O = o; o += 1024
    CR_O = o; o += 512
    CI_O = o; o += 512
    ER_O = o; o += 512
    EI_O = o; o += 512
    EBR_O = 100000
    EBI_O = 101024
    CBR_O = 200000
    CBI_O = 201024
    TA_O = o; o += 512
    TB_O = o; o += 512
    assert o <= ARENA_W, o

    def a32(off, n=32, eoff=0, p0=0, pn=128):
        return AV(off, p0, pn, eoff, [[1, n]])

    P.op("sp", lambda e: e.dma_start(out=AV(PRM_O, 0, 32, 0, [[1, 128]]), in_=a_re_d.ap()), w=["PRM"], dma=True)
    P.op("sp", lambda e: e.dma_start(out=AV(PRM_O, 0, 32, 128, [[1, 128]]), in_=a_im_d.ap()), w=["PRM"], dma=True)
    P.op("sp", lambda e: e.dma_start(out=AV(LDT_O, 0, 32, 0, [[1, 2]]), in_=log_dt_d.ap()), w=["LDT"], dma=True)
    dve(lambda e: e.tensor_copy(out=AV(PRM_O, 0, 32, 256, [[64, 2], [1, 64]]),
                                in_=AV(LDT_O, 0, 32, 0, [[1, 2], [0, 64]])), ["LDT", "PRM"], ["PRM"])
    b, bt = bank()
    for k in range(3):
        pe(lambda e, b=b, k=k: e.transpose(out=b[:, k * 32:(k + 1) * 32], in_=AV(PRM_O, 0, 32, k * 128, [[1, 128]]),
                                           identity=V(CTAB, 0, 32, CT_IDENT, [[1, 32]])), ["PRM", "CTAB"], [bt])
    act(lambda e, b=b: e.copy(out=a32(PT_O, 96), in_=b[:, 0:96]), [bt], ["PT"])
    ARE = a32(PT_O, 32, 0)
    AIM = a32(PT_O, 32, 32)
    actf(a32(DT_O), a32(PT_O, 32, 64), AF.Exp, ["PT"], ["DT"])
    tt(a32(ADT_O), ARE, a32(DT_O), ALU.mult, ["PT", "DT"], ["ADT"])
    tt(a32(ANG_O), AIM, a32(DT_O), ALU.mult, ["PT", "DT"], ["ANG"])
    actf(a32(SC_O, 32, 0), a32(ANG_O), AF.Sin, ["ANG"], ["SC"], scale=1.0 / 16)
    actf(a32(SC_O, 32, 32), a32(ANG_O), AF.Sin, ["ANG", "SMALL"], ["SC"], scale=1.0 / 16, bias=HPIC)
    for _ in range(4):
        tt(a32(T1_O), a32(SC_O, 32, 0), a32(SC_O, 32, 32), ALU.mult, ["SC"], ["T1"])
        tt(a32(T2_O), a32(SC_O, 32, 0), a32(SC_O, 32, 0), ALU.mult, ["SC"], ["T2"])
        ts(a32(SC_O, 32, 0), a32(T1_O), 2.0, None, ALU.mult, None, ["T1"], ["SC"])
        ts(a32(SC_O, 32, 32), a32(T2_O), -2.0, 1.0, ALU.mult, ALU.add, ["T2"], ["SC"])
    dve(lambda e: e.memset(a32(UPC_O, 32, 0), 1.0), [], ["UPC"])
    dve(lambda e: e.memset(a32(UPS_O, 32, 0), 0.0), [], ["UPS"])
    dve(lambda e: e.tensor_copy(out=a32(UPC_O, 32, 32), in_=a32(SC_O, 32, 32)), ["SC"], ["UPC"])
    dve(lambda e: e.tensor_copy(out=a32(UPS_O, 32, 32), in_=a32(SC_O, 32, 0)), ["SC"], ["UPS"])
    C1 = a32(SC_O, 32, 32)
    S1 = a32(SC_O, 32, 0)
    for e_ in range(1, 8):
        ce = a32(UPC_O, 32, 32 * e_)
        se = a32(UPS_O, 32, 32 * e_)
        tt(a32(T1_O), ce, C1, ALU.mult, ["UPC", "SC"], ["T1"])
        tt(a32(T2_O), se, S1, ALU.mult, ["UPS", "SC"], ["T2"])
        tt(a32(UPC_O, 32, 32 * (e_ + 1)), a32(T1_O), a32(T2_O), ALU.subtract, ["T1", "T2"], ["UPC"])
        tt(a32(T1_O), se, C1, ALU.mult, ["UPS", "SC"], ["T1"])
        tt(a32(T2_O), ce, S1, ALU.mult, ["UPC", "SC"], ["T2"])
        tt(a32(UPS_O, 32, 32 * (e_ + 1)), a32(T1_O), a32(T2_O), ALU.add, ["T1", "T2"], ["UPS"])
    for e_ in range(9):
        actf(a32(MG_O, 32, 32 * e_), a32(ADT_O), AF.Exp, ["ADT"], ["MG"], scale=float(e_))
    tt(a32(LPR_O, 288), a32(MG_O, 288), a32(UPC_O, 288), ALU.mult, ["MG", "UPC"], ["LPR"])
    tt(a32(LPI_O, 288), a32(MG_O, 288), a32(UPS_O, 288), ALU.mult, ["MG", "UPS"], ["LPI"])
    dve(lambda e: e.tensor_copy(out=RHO[:], in_=a32(MG_O, 32, 256)), ["MG"], ["RHO"])
    dve(lambda e: e.tensor_copy(out=V(RHOT, 0, 128, 0, [[32, 32], [1, 32]]),
                                in_=V(RHO, 0, 128, 0, [[1, 32], [0, 32]])), ["RHO"], ["RHOT"])
    dve(lambda e: e.memset(V(RHOT, 0, 128, 0, [[32, 32], [1, 1]]), 0.0), ["RHOT"], ["RHOT"])
    TC = lambda lo, n: V(TCS, 0, 128, lo, [[32, 32], [1, n]])
    TS_ = lambda lo, n: V(TCS, 0, 128, 1024 + lo, [[32, 32], [1, n]])
    dve(lambda e: e.tensor_copy(out=TC(0, 1), in_=AV(UPC_O, 0, 128, 256, [[1, 32], [1, 1]])), ["UPC"], ["TCS"])
    dve(lambda e: e.tensor_copy(out=TS_(0, 1), in_=AV(UPS_O, 0, 128, 256, [[1, 32], [1, 1]])), ["UPS"], ["TCS"])
    n_ = 1
    while n_ < 32:
        pc = V(TCS, 0, 128, n_ - 1, [[32, 32], [0, n_]])
        psn_ = V(TCS, 0, 128, 1024 + n_ - 1, [[32, 32], [0, n_]])
        ta = AV(TA_O, 0, 128, 0, [[n_, 32], [1, n_]])
        tb = AV(TB_O, 0, 128, 0, [[n_, 32], [1, n_]])
        tt(ta, TC(0, n_), pc, ALU.mult, ["TCS"], ["TA"])
        tt(tb, TS_(0, n_), psn_, ALU.mult, ["TCS"], ["TB"])
        tt(TC(n_, n_), ta, tb, ALU.subtract, ["TA", "TB", "TCS"], ["TCS"])
        tt(ta, TC(0, n_), psn_, ALU.mult, ["TCS"], ["TA"])
        tt(tb, TS_(0, n_), pc, ALU.mult, ["TCS"], ["TB"])
        tt(TS_(n_, n_), ta, tb, ALU.add, ["TA", "TB", "TCS"], ["TCS"])
        n_ *= 2
    LR1 = a32(LPR_O, 32, 32)
    LI1 = a32(LPI_O, 32, 32)
    ts(a32(T1_O), LR1, -1.0, None, ALU.add, None, ["LPR"], ["T1"])
    tt(a32(T2_O), ARE, ARE, ALU.mult, ["PT"], ["T2"])
    tt(a32(DT_O), AIM, AIM, ALU.mult, ["PT"], ["DT"])
    tt(a32(T2_O), a32(T2_O), a32(DT_O), ALU.add, ["T2", "DT"], ["T2"])
    dve(lambda e: e.reciprocal(out=a32(T2_O), in_=a32(T2_O)), ["T2"], ["T2"])
    tt(a32(DT_O), a32(T1_O), ARE, ALU.mult, ["T1", "PT"], ["DT"])
    tt(a32(ANG_O), LI1, AIM, ALU.mult, ["LPI", "PT"], ["ANG"])
    tt(a32(DT_O), a32(DT_O), a32(ANG_O), ALU.add, ["DT", "ANG"], ["DT"])
    tt(a32(CF_O, 32, 0), a32(DT_O), a32(T2_O), ALU.mult, ["DT", "T2"], ["CF"])
    tt(a32(DT_O), LI1, ARE, ALU.mult, ["LPI", "PT"], ["DT"])
    tt(a32(ANG_O), a32(T1_O), AIM, ALU.mult, ["T1", "PT"], ["ANG"])
    tt(a32(DT_O), a32(DT_O), a32(ANG_O), ALU.subtract, ["DT", "ANG"], ["DT"])
    tt(a32(CF_O, 32, 32), a32(DT_O), a32(T2_O), ALU.mult, ["DT", "T2"], ["CF"])
    P.op("sp", lambda e: e.dma_start(out=AV(BR_O, 0, 128, 0, [[16, 32], [1, 16]]),
                                     in_=bass.AP(b_re_d, 0, [[16, 128], [2048, 32], [1, 16]])), w=["BR"], dma=True)
    P.op("sp", lambda e: e.dma_start(out=AV(BI_O, 0, 128, 0, [[16, 32], [1, 16]]),
                                     in_=bass.AP(b_im_d, 0, [[16, 128], [2048, 32], [1, 16]])), w=["BI"], dma=True)
    for ri, cd in enumerate((c_re_d, c_im_d)):
        for pl in range(8):
            for i4 in range(4):
                P.op("sp", lambda e, ri=ri, cd=cd, pl=pl, i4=i4: e.dma_start(
                    out=AV(CIN_O, pl * 16, 16, ri * 512 + i4 * 128, [[64, 2], [1, 64]]),
                    in_=bass.AP(cd, pl * 2048 + i4 * 16384, [[64, 16], [1024, 2], [1, 64]])), w=["CIN"], dma=True)
    for ri, co in enumerate((CR_O, CI_O)):
        b, bt = bank()
        for i4 in range(4):
            pe(lambda e, b=b, ri=ri, i4=i4: e.transpose(
                out=b[:, i4 * 128:(i4 + 1) * 128], in_=AV(CIN_O, 0, 128, ri * 512 + i4 * 128, [[1, 128]]),
                identity=ct(CT_IDENT, 128)), ["CIN", "CTAB"], [bt])
        act(lambda e, b=b, co=co: e.copy(out=a32(co, 512), in_=b[:, 0:512]), [bt], ["CR" if ri == 0 else "CI"])

    def bc16(off, eoff):
        return AV(off, 0, 128, eoff, [[1, 32], [0, 16]])

    def v512(off):
        return AV(off, 0, 128, 0, [[16, 32], [1, 16]])

    def cmul(outr, outi, ar_, ai_, xr, xi, rtoks, wr, wi, neg_im=False):
        tt(v512(TA_O), ar_, xr, ALU.mult, rtoks, ["TA"])
        tt(v512(TB_O), ai_, xi, ALU.mult, rtoks, ["TB"])
        tt(outr, v512(TA_O), v512(TB_O), ALU.subtract, ["TA", "TB"], [wr])
        tt(v512(TA_O), ar_, xi, ALU.mult, rtoks, ["TA"])
        tt(v512(TB_O), ai_, xr, ALU.mult, rtoks, ["TB"])
        if neg_im:
            dve(lambda e: e.scalar_tensor_tensor(out=outi, in0=v512(TA_O), scalar=-1.0, in1=v512(TB_O),
                                                 op0=ALU.mult, op1=ALU.subtract), ["TA", "TB"], [wi])
        else:
            tt(outi, v512(TA_O), v512(TB_O), ALU.add, ["TA", "TB"], [wi])

    cmul(v512(E0R_O), v512(E0I_O), bc16(CF_O, 0), bc16(CF_O, 32), v512(BR_O), v512(BI_O),
         ["CF", "BR", "BI"], "E0R", "E0I")
    dve(lambda e: e.memset(a32(EBR_O, 1024), 0.0), [], ["EBR"])
    dve(lambda e: e.memset(a32(EBI_O, 1024), 0.0), [], ["EBI"])
    dve(lambda e: e.memset(a32(CBR_O, 1024), 0.0), [], ["CBR"])
    dve(lambda e: e.memset(a32(CBI_O, 1024), 0.0), [], ["CBI"])

    def expand(dst_o, src_o, rt, wt):
        for gp in range(2):
            dve(lambda e, gp=gp: e.tensor_copy(
                out=AV(dst_o, gp * 64, 64, gp * 16, [[32, 32], [1, 16]]),
                in_=AV(src_o, gp * 64, 64, 0, [[16, 32], [1, 16]])), [rt, wt], [wt])

    for r_ in range(8):
        cmul(v512(ER_O), v512(EI_O), bc16(LPR_O, 32 * (r_ + 1)), bc16(LPI_O, 32 * (r_ + 1)), v512(CR_O), v512(CI_O),
             ["LPR", "LPI", "CR", "CI"], "ER", "EI", neg_im=True)
        expand(CBR_O, ER_O, "ER", "CBR")
        expand(CBI_O, EI_O, "EI", "CBI")
        for ri, so in enumerate((CBR_O, CBI_O)):
            act(lambda e, r_=r_, ri=ri, so=so: e.copy(
                out=V(WOUT, 0, 128, (r_ * 2 + ri) * 32, [[512, 32], [1, 32]]),
                in_=AV(so, 0, 128, 0, [[32, 32], [1, 32]])), ["CBR" if ri == 0 else "CBI"], ["WOUT"])
    dve(lambda e: e.tensor_copy(out=v512(ER_O), in_=v512(CR_O)), ["CR", "ER"], ["ER"])
    ts(v512(EI_O), v512(CI_O), -1.0, None, ALU.mult, None, ["CI", "EI"], ["EI"])
    expand(CBR_O, ER_O, "ER", "CBR")
    expand(CBI_O, EI_O, "EI", "CBI")
    for tau in range(8):
        cmul(v512(ER_O), v512(EI_O), bc16(LPR_O, 32 * tau), bc16(LPI_O, 32 * tau), v512(E0R_O), v512(E0I_O),
             ["LPR", "LPI", "E0R", "E0I"], "ER", "EI")
        expand(EBR_O, ER_O, "ER", "EBR")
        expand(EBI_O, EI_O, "EI", "EBI")
        s_ = 7 - tau
        for ft in range(8):
            b, bt = bank()
            for ri, so in enumerate((EBR_O, EBI_O)):
                pe(lambda e, b=b, ri=ri, so=so, ft=ft: e.transpose(
                    out=b[:, ri * 128:(ri + 1) * 128], in_=AV(so, 0, 128, ft * 128, [[1, 128]]),
                    identity=ct(CT_IDENT, 128)), ["EBR" if ri == 0 else "EBI", "CTAB"], [bt])
            act(lambda e, b=b, ft=ft, s_=s_: e.copy(
                out=V(WIN, 0, 128, ((ft * 8 + s_) * 2) * 128, [[1, 256]]), in_=b[:, 0:256]), [bt], ["WIN"])
            pe(lambda e, b=b, ft=ft: e.matmul(b[:, 256:384], lhsT=AV(EBR_O, 0, 128, ft * 128, [[1, 128]]),
                                             rhs=AV(CBR_O, 0, 128, ft * 128, [[1, 128]]), start=True, stop=False),
               ["EBR", "CBR"], [bt])
            pe(lambda e, b=b, ft=ft: e.matmul(b[:, 256:384], lhsT=AV(EBI_O, 0, 128, ft * 128, [[1, 128]]),
                                             rhs=AV(CBI_O, 0, 128, ft * 128, [[1, 128]]), start=False, stop=True),
               ["EBI", "CBI"], [bt])
            kdst = V(KW, 0, 128, (ft * 8 + tau) * 128, [[1, 128]])
            if tau == 0:
                tt(AV(TA_O, 0, 128, 0, [[1, 128]]), b[:, 256:384], ct(CT_M16, 128), ALU.mult, [bt, "CTAB"], ["TA"])
                dve(lambda e, kdst=kdst, ft=ft: e.scalar_tensor_tensor(
                    out=kdst, in0=ct(CT_IDENT, 128), scalar=col(0, 16 + ft), in1=AV(TA_O, 0, 128, 0, [[1, 128]]),
                    op0=ALU.mult, op1=ALU.add), ["TA", "CTAB", "COLS"], ["KW"])
            else:
                tt(kdst, b[:, 256:384], ct(CT_M16, 128), ALU.mult, [bt, "CTAB"], ["KW"])
    P.barrier()
    pe(lambda e: e.matmul(PS[0][0:32, 0:32], lhsT=IDB[:, 0:32], rhs=IDB[:, 0:32], start=True, stop=True), ["IDB"], ["ps0"])
    P.barrier()
    if stop == 0:
        NT = 0

    wbn = [0]

    def load_w(dram, r0, c0):
        i = wbn[0] % 3
        wbn[0] += 1
        nm_ = [k for k, v in SRCW.items() if v is dram][0]
        src = WBF[nm_].ap()[r0:r0 + 1024, c0:c0 + 512].rearrange("(c p) f -> p c f", p=128)
        P.op("sp", lambda e, i=i, src=src: e.dma_start(out=V(WB[i], 0, 128, 0, [[512, 8], [1, 512]]), in_=src),
             r=[f"CV_{nm_}_{r0}_{c0}"], w=[f"WB{i}"], dma=True, nobar=True)
        return WB[i], f"WB{i}"

    def rms_to_ht(gl, XS_O, X, XT):
        import os
        ksub = int(os.environ.get("KSUB", "9"))
        if ksub < 1:
            return
        for s in range(2):
            actf(AV(XS_O, 0, 128, 0, [[1, 1024]], BF16), X[:, s * 1024:(s + 1) * 1024], AF.Square,
                 [XT], ["XS", "SMALL"], accum=SMALL[:, s:s + 1])
        if ksub < 2:
            return
        ts(SMALL[:, 2:4], SMALL[:, 0:2], 1.0 / 1024, EPS, ALU.mult, ALU.add, ["SMALL"], ["SMALL"])
        actf(SMALL[:, 2:4], SMALL[:, 2:4], AF.Sqrt, ["SMALL"], ["SMALL"])
        dve(lambda e: e.reciprocal(out=SMALL[:, 4:6], in_=SMALL[:, 2:4]), ["SMALL"], ["SMALL"])
        if ksub < 3:
            return
        for s in range(2):
            ts(AV(XS_O, 0, 128, s * 1024, [[1, 1024]], BF16), X[:, s * 1024:(s + 1) * 1024], SMALL[:, 4 + s:5 + s],
               None, ALU.mult, None, [XT, "SMALL"], ["XS"])
        if ksub < 4:
            return
        for c in range(8):
            b, bt = bank()
            for s in range(2):
                pe(lambda e, b=b, c=c, s=s: e.transpose(
                    out=V(b, 0, 128, s * 128, [[1, 128]], BF16),
                    in_=AV(XS_O, 0, 128, s * 1024 + c * 128, [[1, 128]], BF16), identity=IDB[:]), ["XS", "IDB"], [bt])
            ts(HT[:, c * TT:(c + 1) * TT], V(b, 0, 128, 0, [[1, TT]], BF16), col(0, gl * 8 + c), None, ALU.mult, None,
               [bt, "COLS"], ["HT"])

    def proj_fm(wb, wt, cc, rhs_t, rhs_tok, rhs_fn=None):
        b, bt = bank()
        for c in range(8):
            rhs = rhs_fn(c) if rhs_fn is not None else rhs_t[:, c * TT:(c + 1) * TT]
            pe(lambda e, b=b, c=c, rhs=rhs: e.matmul(b[:, 0:TT], lhsT=V(wb, 0, 128, c * 512 + cc, [[1, 128]]),
                                                     rhs=rhs, start=(c == 0), stop=(c == 7)),
               [wt, rhs_tok], [bt])
        return b, bt

    def proj_tm(wb, wt, s):
        b, bt = bank()
        for c in range(8):
            pe(lambda e, b=b, c=c: e.matmul(b[:, 0:512], lhsT=HT[:, c * TT + s * 128:c * TT + s * 128 + 128],
                                            rhs=V(wb, 0, 128, c * 512, [[1, 512]]), start=(c == 0), stop=(c == 7)),
               [wt, "HT"], [bt])
        return b, bt

    XS_O = 0
    ZT1_O = 0; ZT2_O = 1024; ZR_O = 2048; ZI_O = 3072; ZPR_O = 4096; ZPI_O = 5120
    U0_O = 6144
    SG0_O = 7168
    SAL_O = 8192
    GYB_O = 9280
    ZS_O = 10304
    assert ZS_O + 1024 <= ARENA_W

    def chk(k):
        if stop == k:
            P.barrier()
            raise StopBuild()

    try:
        def load_x(tt_):
            P.op("pool", lambda e, tt_=tt_: e.dma_start(
                out=V(XB[tt_ % 2], 0, 128, 0, [[1024, 2], [1, 1024]]),
                in_=x_d.ap()[tt_ * TT:(tt_ + 1) * TT, :].rearrange("(s p) d -> p s d", p=128)),
                w=[f"X{tt_ % 2}"], dma=True, nobar=True)

        prev_final = [None]

        def final_norm(tf, X, XT, junk_o, junk_tok):
            for s in range(2):
                actf(AV(junk_o, 0, 128, 0, [[1, 1024]], BF16), X[:, s * 1024:(s + 1) * 1024], AF.Square,
                     [XT], [junk_tok, "SMALL"], accum=SMALL[:, 8 + s:9 + s])
            ts(SMALL[:, 10:12], SMALL[:, 8:10], 1.0 / 1024, EPS, ALU.mult, ALU.add, ["SMALL"], ["SMALL"])
            actf(SMALL[:, 10:12], SMALL[:, 10:12], AF.Sqrt, ["SMALL"], ["SMALL"])
            dve(lambda e: e.reciprocal(out=SMALL[:, 12:14], in_=SMALL[:, 10:12]), ["SMALL"], ["SMALL"])
            for s in range(2):
                xs_ = X[:, s * 1024:(s + 1) * 1024]
                dve(lambda e, xs_=xs_, s=s: e.scalar_tensor_tensor(out=xs_, in0=xs_, scalar=SMALL[:, 12 + s:13 + s], in1=FG[:],
                                                                   op0=ALU.mult, op1=ALU.mult), [XT, "SMALL", "FG"], [XT])
            P.op("pool", lambda e, tf=tf, X=X: e.dma_start(
                out=out_d.ap()[tf * TT:(tf + 1) * TT, :].rearrange("(s p) d -> p s d", p=128),
                in_=V(X, 0, 128, 0, [[1024, 2], [1, 1024]])), r=[XT], w=["OUT"], dma=True, nobar=True)

        if NT > 0:
            load_x(0)
            rms_to_ht(0, XS_O, XB[0], "X0")
        for t in range(NT):
            tok0 = t * TT
            X = XB[t % 2]
            XT = f"X{t % 2}"
            rp = (t % 2) * 128
            for cs in range(2):
                P.op("pool", lambda e, cs=cs, t=t, rp=rp: e.dma_start(
                    out=ROPE[:, rp + cs * 64:rp + cs * 64 + 64],
                    in_=ctab_d.ap()[:, CT_ROPE + cs * NSUB * 32 + 2 * t * 32:CT_ROPE + cs * NSUB * 32 + 2 * t * 32 + 64]),
                    w=[f"ROPE{t % 2}"], dma=True, nobar=True)
            if prev_final[0] is not None:
                final_norm(*prev_final[0], GYB_O, "GYB")
                prev_final[0] = None
            if t + 1 < NT:
                load_x(t + 1)

            chk(10)
            U0 = lambda p0, pn, eoff, pat: AV(U0_O, p0, pn, eoff, pat, BF16)
            U0M = lambda p0, pn, eoff, pat: AV(ZPR_O, p0, pn, eoff, pat, BF16)
            for half in range(2):
                wb, wt = load_w(w_in_ab_d, 0, half * 512)
                for f in range(4):
                    ft = half * 4 + f
                    b, bt = proj_fm(wb, wt, f * 128, HT, "HT")
                    act(lambda e, b=b, ft=ft: e.copy(out=U0(0, 128, ft * TT, [[1, 32], [32, 8]]),
                                                     in_=V(b, 0, 128, 0, [[8, 32], [1, 8]])), [bt], ["U0"])
                    act(lambda e, b=b, ft=ft: e.copy(out=U0M(64, 64, ft * TT, [[1, 32], [32, 8]]),
                                                     in_=V(b, 64, 64, 0, [[8, 32], [1, 8]])), [bt], ["U0M"])
                    dve(lambda e, ft=ft: e.memset(U0M(64, 32, ft * TT, [[1, TT]]), 0.0), ["U0M"], ["U0M"])
            for half in range(2):
                wb, wt = load_w(w_in_ab_d, 0, 1024 + half * 512)
                for f in range(4):
                    ft = half * 4 + f
                    b, bt = proj_fm(wb, wt, f * 128, HT, "HT")
                    actf(AV(SG0_O, 0, 128, ft * TT, [[1, TT]], BF16), b[:, 0:TT], AF.Silu, [bt], ["SG0"])
            chk(11)
            for q in range(4):
                if q > int(os.environ.get("KQ", "3")):
                    continue
                b, bt = bank()
                for ft in range(8):
                    for ri in range(2):
                        for s in range(8):
                            if q < 3:
                                pe(lambda e, b=b, ri=ri, s=s, ft=ft, q=q: e.matmul(
                                    b[:, ri * 256 + ft * 32: ri * 256 + ft * 32 + 32],
                                    lhsT=V(WIN, 32 * q, 32, ((ft * 8 + s) * 2 + ri) * 128, [[1, 128]]),
                                    rhs=U0(32 * q, 32, ft * TT + s * 32, [[1, 32]]),
                                    start=(s == 0), stop=(s == 7), tile_position=(32 * q, 0)), ["WIN", "U0"], [bt])
                            else:
                                pe(lambda e, b=b, ri=ri, s=s, ft=ft: e.matmul(
                                    b[:, ri * 256 + ft * 32: ri * 256 + ft * 32 + 32],
                                    lhsT=V(WIN, 64, 64, ((ft * 8 + s) * 2 + ri) * 128, [[1, 128]]),
                                    rhs=U0M(64, 64, ft * TT + s * 32, [[1, 32]]),
                                    start=(s == 0), stop=(s == 7), tile_position=(64, 0)), ["WIN", "U0M"], [bt])
                act(lambda e, b=b, q=q: e.copy(out=AV(ZR_O, 0, 128, q * 32, [[128, 8], [1, 32]]),
                                               in_=V(b, 0, 128, 0, [[32, 8], [1, 32]])), [bt], ["ZR"])
                act(lambda e, b=b, q=q: e.copy(out=AV(ZI_O, 0, 128, q * 32, [[128, 8], [1, 32]]),
                                               in_=V(b, 0, 128, 256, [[32, 8], [1, 32]])), [bt], ["ZI"])
            chk(12)
            SAL = lambda ri, lo, n: AV(SAL_O, 0, 128, ri * 1056 + lo, [[33, 32], [1, n]], BF16)
            for ri in range(2):
                dve(lambda e, ri=ri: e.tensor_copy(out=SAL(ri, 0, 1), in_=V(W0, 0, 128, ri * 32, [[1, 32], [1, 1]])),
                    ["W0"], ["SAL"])
            f1k = lambda off: AV(off, 0, 128, 0, [[1, 1024]])
            TCf = V(TCS, 0, 128, 0, [[1, 1024]])
            TSf = V(TCS, 0, 128, 1024, [[1, 1024]])
            tt(f1k(ZT1_O), TCf, f1k(ZR_O), ALU.mult, ["TCS", "ZR"], ["ZT1", "XS"])
            tt(f1k(ZT2_O), TSf, f1k(ZI_O), ALU.mult, ["TCS", "ZI"], ["ZT2"])
            tt(f1k(ZPR_O), f1k(ZT1_O), f1k(ZT2_O), ALU.add, ["ZT1", "ZT2"], ["ZPR", "U0M"])
            tt(f1k(ZT1_O), TCf, f1k(ZI_O), ALU.mult, ["TCS", "ZI"], ["ZT1"])
            tt(f1k(ZT2_O), TSf, f1k(ZR_O), ALU.mult, ["TCS", "ZR"], ["ZT2"])
            tt(f1k(ZPI_O), f1k(ZT1_O), f1k(ZT2_O), ALU.subtract, ["ZT1", "ZT2"], ["ZPI"])
            for ri, zo, wo_, tk in ((0, ZPR_O, ZR_O, "ZPR"), (1, ZPI_O, ZI_O, "ZPI")):
                z0 = AV(zo, 0, 128, 0, [[32, 32], [1, 1]])
                tt(AV(ZT1_O, 0, 128, 0, [[1, 32], [1, 1]]), V(RHO, 0, 128, 0, [[1, 32], [1, 1]]),
                   V(W0, 0, 128, ri * 32, [[1, 32], [1, 1]]), ALU.mult, ["RHO", "W0"], ["ZT1"])
                tt(z0, z0, AV(ZT1_O, 0, 128, 0, [[1, 32], [1, 1]]), ALU.add, [tk, "ZT1"], [tk])
                wtk = "ZR" if ri == 0 else "ZI"
                dve(lambda e, zo=zo, wo_=wo_: e.tensor_tensor_scan(
                    out=f1k(wo_), data0=RHOT[:], data1=f1k(zo), initial=0.0, op0=ALU.mult, op1=ALU.add),
                    [tk, "RHOT"], [wtk])
            tt(f1k(ZT1_O), TCf, f1k(ZR_O), ALU.mult, ["TCS", "ZR"], ["ZT1"])
            tt(f1k(ZT2_O), TSf, f1k(ZI_O), ALU.mult, ["TCS", "ZI"], ["ZT2"])
            tt(f1k(ZPR_O), f1k(ZT1_O), f1k(ZT2_O), ALU.subtract, ["ZT1", "ZT2"], ["ZPR"])
            tt(f1k(ZT1_O), TCf, f1k(ZI_O), ALU.mult, ["TCS", "ZI"], ["ZT1"])
            tt(f1k(ZT2_O), TSf, f1k(ZR_O), ALU.mult, ["TCS", "ZR"], ["ZT2"])
            tt(f1k(ZPI_O), f1k(ZT1_O), f1k(ZT2_O), ALU.add, ["ZT1", "ZT2"], ["ZPI"])
            for ri, zo, tk in ((0, ZPR_O, "ZPR"), (1, ZPI_O, "ZPI")):
                dve(lambda e, ri=ri, zo=zo: e.tensor_copy(out=SAL(ri, 1, 32), in_=AV(zo, 0, 128, 0, [[32, 32], [1, 32]])),
                    [tk], ["SAL"])
                dve(lambda e, ri=ri, zo=zo: e.tensor_copy(out=V(W0, 0, 128, ri * 32, [[1, 32], [1, 1]]),
                                                          in_=AV(zo, 0, 128, 31, [[32, 32], [1, 1]])), [tk], ["W0"])
            chk(13)
            C_G = 0.7978845608028654
            for ft in range(8):
                b, bt = bank()
                bv = lambda p0, pn, lo, n, b=b: V(b, p0, pn, lo, [[8, 32], [1, n]])
                for tau in range(8):
                    nn = (8 - tau) * 32
                    pe(lambda e, b=b, ft=ft, tau=tau, nn=nn: e.matmul(
                        b[:, tau * 32:TT], lhsT=V(KW, 0, 128, (ft * 8 + tau) * 128, [[1, 128]]),
                        rhs=U0(0, 128, ft * TT, [[1, nn]]), start=(tau == 0), stop=False), ["KW", "U0"], [bt])
                for q in range(4):
                    pair = ft * 4 + q
                    for r_ in range(8):
                        for ri in range(2):
                            last = (r_ == 7 and ri == 1)
                            pe(lambda e, b=b, q=q, pair=pair, r_=r_, ri=ri, last=last, bv=bv: e.matmul(
                                V(b, 32 * q, 32, r_ * 32, [[1, 32]]), lhsT=V(WOUT, 0, 128, ((pair * 8 + r_) * 2 + ri) * 32, [[1, 32]]),
                                rhs=AV(SAL_O, 0, 128, ri * 1056 + pair * 33, [[1, 32]], BF16),
                                start=False, stop=last, tile_position=(0, 32 * q)), ["WOUT", "SAL"], [bt])
                t1 = AV(ZT1_O, 0, 128, (ft % 2) * TT, [[1, TT]])
                t2 = AV(ZT2_O, 0, 128, (ft % 2) * TT, [[1, TT]])
                k1, k2 = ("ZT1", "ZT2")
                actf(t1, b[:, 0:TT], AF.Square, [bt], [k1])
                ts(t1, t1, 0.044715, 1.0, ALU.mult, ALU.add, [k1], [k1])
                tt(t2, t1, b[:, 0:TT], ALU.mult, [k1, bt], [k2])
                actf(t2, t2, AF.Sigmoid, [k2], [k2], scale=2.0 * C_G)
                tt(AV(GYB_O, 0, 128, ft * TT, [[1, 8], [8, 32]], BF16), AV(ZT2_O, 0, 128, (ft % 2) * TT, [[32, 8], [1, 32]]),
                   V(b, 0, 128, 0, [[32, 8], [1, 32]]), ALU.mult, [k2, bt], ["GYB"])
            chk(14)
            GYB = AV(GYB_O, 0, 128, 0, [[1, 8 * TT]], BF16)
            for half in range(2):
                wb, wt = load_w(glu_w_d, 0, half * 512)
                for f in range(4):
                    ft = half * 4 + f
                    b, bt = proj_fm(wb, wt, f * 128, GYB, "GYB")
                    zs = AV(ZS_O, 0, 128, ft * TT, [[1, TT]], BF16)
                    actf(zs, b[:, 0:TT], AF.Sigmoid, [bt, "COLS"], ["ZS"], bias=col(0, 24 + ft))
                    tt(zs, zs, AV(GYB_O, 0, 128, ft * TT, [[1, TT]], BF16), ALU.mult, ["ZS", "GYB"], ["ZS"])
                    tt(YC[:, ft * TT:(ft + 1) * TT], zs, AV(SG0_O, 0, 128, ft * TT, [[1, TT]], BF16), ALU.mult,
                       ["ZS", "SG0"], ["YC"])
            P.barrier()
            if stop == 1:
                break

            QR_O = 0; KR_O = 512; KD0_O = 1024; KD1_O = 1536; QD_O = 2048
            QT0_O = 2560; QT1_O = 3072; KT_O = 3584; QDT0_O = 4096; QDT1_O = 4608
            VB_O = 5120
            SMF_O = 6144
            SB_O = 7168
            SGR_O = 8192
            RT_O = 9216
            OF_O = 10240
            ST_O = 9216
            QT_OS = (QT0_O, QT1_O)
            QDT_OS = (QDT0_O, QDT1_O)
            KD_OS = (KD0_O, KD1_O)
            for off, nm in ((QT0_O, "QT0"), (QDT0_O, "QDT0"), (KD0_O, "KD0")):
                pool(lambda e, off=off: e.memset(AV(off, 64, 64, 0, [[1, 1024]], BF16), 0.0), [], [nm])
            for off, nm in ((QT1_O, "QT1"), (QDT1_O, "QDT1"), (KD1_O, "KD1")):
                pool(lambda e, off=off: e.memset(AV(off, 0, 64, 0, [[1, 1024]], BF16), 0.0), [], [nm])
            pool(lambda e: e.memset(AV(SMF_O, 0, 128, 0, [[1, 2048]], BF16), 0.0), [], ["SMF"])
            ropeC = lambda s_: V(ROPE, 0, 128, rp + s_ * 32, [[0, 8], [1, 32]])
            ropeS = lambda s_: V(ROPE, 0, 128, rp + 64 + s_ * 32, [[0, 8], [1, 32]])
            RTK = f"ROPE{t % 2}"
            for qk, col0, dst in ((0, 2048, QR_O), (1, 2560, KR_O)):
                wb, wt = load_w(w_in_ab_d, 0, col0)
                for s in range(2):
                    b, bt = proj_tm(wb, wt, s)
                    sub = s
                    x1 = V(b, 0, 128, 0, [[64, 8], [1, 32]])
                    x2 = V(b, 0, 128, 32, [[64, 8], [1, 32]])
                    r4 = lambda k: AV(RT_O, 0, 128, k * 256, [[32, 8], [1, 32]])
                    tt(r4(0), x1, ropeC(sub), ALU.mult, [bt, RTK], ["RT0"])
                    tt(r4(1), x2, ropeS(sub), ALU.mult, [bt, RTK], ["RT1"])
                    tt(r4(2), x1, ropeS(sub), ALU.mult, [bt, RTK], ["RT2"])
                    tt(r4(3), x2, ropeC(sub), ALU.mult, [bt, RTK], ["RT3"])
                    o1 = AV(dst, 0, 128, s * 512, [[64, 8], [1, 32]], BF16)
                    o2 = AV(dst, 0, 128, s * 512 + 32, [[64, 8], [1, 32]], BF16)
                    tk = "QR" if qk == 0 else "KR"
                    tt(o1, r4(0), r4(1), ALU.subtract, ["RT0", "RT1"], [tk])
                    tt(o2, r4(2), r4(3), ALU.add, ["RT2", "RT3"], [tk])
                    if qk == 0:
                        tt(AV(QD_O, 0, 128, s * 512, [[64, 8], [1, 64]], BF16),
                           AV(dst, 0, 128, s * 512, [[64, 8], [1, 64]], BF16),
                           V(CTAB, 0, 128, CT_QDEC, [[1, 8], [0, 64]]), ALU.mult, [tk, "CTAB"], ["QD"])
                    else:
                        for hf in range(2):
                            tt(AV(KD_OS[hf], 64 * hf, 64, s * 512, [[64, 8], [1, 64]], BF16),
                               AV(dst, 64 * hf, 64, s * 512, [[64, 8], [1, 64]], BF16),
                               V(CTAB, 64 * hf, 64, CT_KDEC, [[1, 8], [0, 64]]), ALU.mult, [tk, "CTAB"], [f"KD{hf}"])
            for src, stk, dsts in ((QR_O, "QR", (QT0_O, QT1_O)), (KR_O, "KR", None), (QD_O, "QD", (QDT0_O, QDT1_O))):
                for s in range(2):
                    b, bt = bank()
                    for hp in range(4):
                        pe(lambda e, b=b, src=src, s=s, hp=hp: e.transpose(
                            out=V(b, 0, 128, hp * 128, [[1, 128]], BF16),
                            in_=AV(src, 0, 128, s * 512 + hp * 128, [[1, 128]], BF16), identity=IDB[:]), [stk, "IDB"], [bt])
                    if dsts is None:
                        act(lambda e, b=b, s=s: e.copy(
                            out=AV(KT_O, 0, 128, s * 128, [[TT, 4], [1, 128]], BF16),
                            in_=V(b, 0, 128, 0, [[128, 4], [1, 128]], BF16)), [bt], ["KT"])
                    else:
                        for hf in range(2):
                            nm = ("QT" if stk == "QR" else "QDT") + str(hf)
                            act(lambda e, b=b, s=s, hf=hf, dsts=dsts: e.copy(
                                out=AV(dsts[hf], 64 * hf, 64, s * 128, [[TT, 4], [1, 128]], BF16),
                                in_=V(b, 64 * hf, 64, 0, [[128, 4], [1, 128]], BF16)), [bt], [nm])
            for half in range(2):
                wb, wt = load_w(w_in_ab_d, 0, 3072 + half * 512)
                for s in range(2):
                    b, bt = proj_tm(wb, wt, s)
                    act(lambda e, b=b, s=s, half=half: e.copy(
                        out=AV(VB_O, 0, 128, s * 1024 + half * 512, [[1, 512]], BF16), in_=b[:, 0:512]), [bt], ["VB"])
            for half in range(2):
                wb, wt = load_w(w_in_ab_d, 0, 4096 + half * 512)
                for f in range(4):
                    ft = half * 4 + f
                    b, bt = proj_fm(wb, wt, f * 128, HT, "HT")
                    actf(AV(SGR_O, 0, 128, ft * TT, [[1, TT]], BF16), b[:, 0:TT], AF.Silu, [bt], ["SGR"])
            for h in range(8):
                hp, par = h // 2, h % 2
                b, bt = bank()
                for c in range(4):
                    s, cpar = c // 2, c % 2
                    tk0 = s * 128 + cpar * 64
                    pe(lambda e, b=b, hp=hp, par=par, s=s, cpar=cpar, tk0=tk0: e.matmul(
                        b[64 * cpar:64 * cpar + 64, s * 64:(s + 1) * 64],
                        lhsT=AV(KT_O, 0, 128, hp * TT + tk0, [[1, 64]], BF16),
                        rhs=AV(QT_OS[par], 0, 128, hp * TT + tk0, [[1, 64]], BF16), start=True, stop=True,
                        tile_position=(0, 64 * cpar)), ["KT", f"QT{par}"], [bt])
                for cpar in range(2):
                    tt(AV(SMF_O, 64 * cpar, 64, h * 256 + cpar * 64, [[128, 2], [1, 64]], BF16),
                       V(b, 64 * cpar, 64, 0, [[64, 2], [1, 64]]),
                       V(CTAB, 64 * cpar, 64, CT_MASK + h * 64, [[0, 2], [1, 64]]), ALU.mult, [bt, "CTAB"], ["SMF"])
            for hp in range(4):
                b, bt = bank()
                for c in range(4):
                    s, cpar = c // 2, c % 2
                    for par in range(2):
                        h = hp * 2 + par
                        pe(lambda e, b=b, c=c, s=s, cpar=cpar, par=par, h=h: e.matmul(
                            b[64 * par:64 * par + 64, c * 128:(c + 1) * 128],
                            lhsT=AV(KD_OS[cpar], 0, 128, s * 512 + h * 64, [[1, 64]], BF16),
                            rhs=AV(VB_O, 0, 128, s * 1024 + h * 128, [[1, 128]], BF16), start=True, stop=True,
                            tile_position=(0, 64 * par)), [f"KD{cpar}", "VB"], [bt])
                for c in range(4):
                    sst = SRET[:, hp * 128:(hp + 1) * 128]
                    dve(lambda e, hp=hp, c=c, sst=sst: e.tensor_copy(
                        out=AV(SB_O, 0, 128, (hp * 4 + c) * 128, [[1, 128]], BF16), in_=sst), ["SRET"], ["SB"])
                    dve(lambda e, b=b, hp=hp, c=c, sst=sst: e.scalar_tensor_tensor(
                        out=sst, in0=sst, scalar=V(CTAB, 0, 128, CT_G64 + hp, [[1, 1]]), in1=b[:, c * 128:(c + 1) * 128],
                        op0=ALU.mult, op1=ALU.add), ["SRET", "CTAB", bt], ["SRET"])
            for h in range(8):
                hp, par = h // 2, h % 2
                b, bt = bank()
                for s in range(2):
                    pe(lambda e, b=b, h=h, s=s: e.matmul(
                        b[:, s * 128:(s + 1) * 128], lhsT=AV(VB_O, 0, 128, s * 1024 + h * 128, [[1, 128]], BF16),
                        rhs=AV(SMF_O, 0, 128, h * 256 + s * 128, [[1, 128]], BF16), start=(s == 0), stop=False),
                       ["VB", "SMF"], [bt])
                for c in range(4):
                    tk0 = (c // 2) * 128 + (c % 2) * 64
                    pe(lambda e, b=b, hp=hp, par=par, c=c, tk0=tk0: e.matmul(
                        b[:, c * 64:(c + 1) * 64], lhsT=AV(SB_O, 0, 128, (hp * 4 + c) * 128, [[1, 128]], BF16),
                        rhs=AV(QDT_OS[par], 0, 128, hp * TT + tk0, [[1, 64]], BF16), start=False, stop=(c == 3)),
                       ["SB", f"QDT{par}"], [bt])
                ob = (h % 2) * 512
                OF = AV(OF_O, 0, 128, ob, [[1, TT]])
                OFB = AV(OF_O, 0, 128, 2 * (ob + 256), [[1, TT]], BF16)
                OSQ = AV(OF_O, 0, 128, 2 * (ob + 384), [[1, TT]], BF16)
                otk = f"OF{h % 2}"
                act(lambda e, b=b, OF=OF: e.copy(out=OF, in_=b[:, 0:TT]), [bt], [otk])
                act(lambda e, b=b, OFB=OFB: e.copy(out=OFB, in_=b[:, 0:TT]), [bt], [otk])
                actf(OSQ, b[:, 0:TT], AF.Square, [bt], [otk])
                b2, bt2 = bank()
                pe(lambda e, b2=b2, OFB=OFB: e.matmul(b2[:, 0:TT], lhsT=ONESB[:], rhs=OFB, start=True, stop=True),
                   [otk, "ONESB"], [bt2])
                pe(lambda e, b2=b2, OSQ=OSQ: e.matmul(b2[:, 256:256 + TT], lhsT=ONESB[:], rhs=OSQ, start=True, stop=True),
                   [otk, "ONESB"], [bt2])
                sm = AV(ST_O, 0, 128, 0, [[1, TT]])
                sv = AV(ST_O, 0, 128, 256, [[1, TT]])
                sx = AV(ST_O, 0, 128, 512, [[1, TT]])
                ts(sm, b2[:, 0:TT], 1.0 / 128, None, ALU.mult, None, [bt2], ["ST0", "RT0"])
                tt(sv, sm, sm, ALU.mult, ["ST0"], ["ST1", "RT1"])
                dve(lambda e, b2=b2, sv=sv: e.scalar_tensor_tensor(out=sv, in0=b2[:, 256:256 + TT], scalar=1.0 / 128, in1=sv,
                                                                   op0=ALU.mult, op1=ALU.subtract), [bt2, "ST1"], ["ST1"])
                ts(sv, sv, EPS, None, ALU.add, None, ["ST1"], ["ST1"])
                actf(sv, sv, AF.Sqrt, ["ST1"], ["ST1"])
                dve(lambda e, sv=sv: e.reciprocal(out=sv, in_=sv), ["ST1"], ["ST1"])
                tt(sx, OF, sm, ALU.subtract, [otk, "ST0"], ["ST2", "RT2"])
                tt(sx, sx, sv, ALU.mult, ["ST2", "ST1"], ["ST2"])
                tt(YC[:, (8 + h) * TT:(9 + h) * TT], sx, AV(SGR_O, 0, 128, h * TT, [[1, TT]], BF16), ALU.mult,
                   ["ST2", "SGR"], ["YC"])
            for ns in range(2):
                wa, wat = load_w(w_out_ab_d, 0, ns * 512)
                wb2, wbt = load_w(w_out_ab_d, 1024, ns * 512)
                for s in range(2):
                    b, bt = bank()
                    for kc in range(16):
                        w_, wt_ = (wa, wat) if kc < 8 else (wb2, wbt)
                        pe(lambda e, b=b, kc=kc, s=s, w_=w_: e.matmul(
                            b[:, 0:512], lhsT=YC[:, kc * TT + s * 128:kc * TT + s * 128 + 128],
                            rhs=V(w_, 0, 128, (kc % 8) * 512, [[1, 512]]), start=(kc == 0), stop=(kc == 15)),
                           ["YC", wt_], [bt])
                    xs_ = X[:, s * 1024 + ns * 512:s * 1024 + ns * 512 + 512]
                    tt(xs_, xs_, b[:, 0:512], ALU.add, [XT, bt], [XT])
            if debug:
                P.op("pool", lambda e, tok0=tok0, X=X: e.dma_start(
                    out=x1_d.ap()[tok0:tok0 + TT, :].rearrange("(s p) d -> p s d", p=128),
                    in_=V(X, 0, 128, 0, [[1024, 2], [1, 1024]])), r=[XT], w=["OUT"], dma=True)
            P.barrier()
            if stop == 2:
                break

            L1XS_O = 0
            SG1_O = 1024
            SIG_O = 2048
            VV_O = 2560
            VSQ_O = 4608
            LST_O = 5120
            Y1_O = 6144
            LT_O = 7168
            rms_to_ht(1, L1XS_O, X, XT)
            for ft in range(8):
                dve(lambda e, ft=ft: e.tensor_copy(out=U1[:, ft * 288 + 2:ft * 288 + 32],
                                                   in_=U1[:, ft * 288 + 258:ft * 288 + 288]), ["U1"], ["U1"])
            for half in range(2):
                wa, wat = load_w(w_in_c_d, 0, half * 512)
                wb2, wbt = load_w(w_in_c_d, 0, 1024 + half * 512)
                for f in range(4):
                    ft = half * 4 + f
                    ba, bat = proj_fm(wa, wat, f * 128, HT, "HT")
                    bb, bbt = proj_fm(wb2, wbt, f * 128, HT, "HT")
                    sg = AV(SIG_O, 0, 128, (ft % 2) * 256, [[1, TT]])
                    actf(sg, bb[:, 0:TT], AF.Sigmoid, [bbt], [f"SIG{ft % 2}"])
                    tt(U1[:, ft * 288 + 32:ft * 288 + 288], ba[:, 0:TT], sg, ALU.mult, [bat, f"SIG{ft % 2}"], ["U1"])
            for half in range(2):
                wb, wt = load_w(w_in_c_d, 0, 2048 + half * 512)
                for f in range(4):
                    ft = half * 4 + f
                    b, bt = proj_fm(wb, wt, f * 128, HT, "HT")
                    actf(AV(SG1_O, 0, 128, ft * TT, [[1, TT]], BF16), b[:, 0:TT], AF.Silu, [bt], ["SG1"])
            if t + 1 < NT:
                rms_to_ht(0, L1XS_O, XB[(t + 1) % 2], f"X{(t + 1) % 2}")
            dn = 0
            bsum, bsumt = PS[7], "ps7"
            for ft in range(8):
                b, bt = bank()
                di = wbn[0] % 3
                wbn[0] += 1
                P.op("sp", lambda e, di=di, ft=ft: e.dma_start(out=V(WB[di], 0, 128, 0, [[1, 31 * 128]]), in_=DG.ap()[ft]),
                     r=[f"DG{ft}"], w=[f"WB{di}"], dma=True, nobar=True)
                for k in range(31):
                    pe(lambda e, b=b, di=di, ft=ft, k=k: e.matmul(
                        b[:, 0:TT], lhsT=V(WB[di], 0, 128, k * 128, [[1, 128]]),
                        rhs=U1[:, ft * 288 + 2 + k:ft * 288 + 2 + k + TT],
                        start=(k == 0), stop=(k == 30)), [f"WB{di}", "U1"], [bt])
                vv = AV(VV_O, 0, 128, ft * TT, [[1, TT]])
                actf(vv, b[:, 0:TT], AF.Identity, [bt, "COLS"], ["VV"], bias=col(0, 32 + ft))
                vvb = AV(VSQ_O, 0, 128, 2 * ((ft % 2) * 256), [[1, TT]], BF16)
                vsq = AV(VSQ_O, 0, 128, 2 * ((ft % 2) * 256 + 128), [[1, TT]], BF16)
                dve(lambda e, vvb=vvb, vv=vv: e.tensor_copy(out=vvb, in_=vv), ["VV"], [f"VSQ{ft % 2}"])
                actf(vsq, vv, AF.Square, ["VV"], [f"VSQ{ft % 2}"])
                pe(lambda e, vvb=vvb, ft=ft: e.matmul(bsum[:, 0:TT], lhsT=ONESB[:], rhs=vvb,
                                                     start=(ft == 0), stop=False), [f"VSQ{ft % 2}", "ONESB"], [bsumt])
                pe(lambda e, vsq=vsq, ft=ft: e.matmul(bsum[:, 256:256 + TT], lhsT=ONESB[:], rhs=vsq,
                                                     start=False, stop=(ft == 7)), [f"VSQ{ft % 2}", "ONESB"], [bsumt])
            sm = AV(LST_O, 0, 128, 0, [[1, TT]])
            sv = AV(LST_O, 0, 128, 256, [[1, TT]])
            ts(sm, bsum[:, 0:TT], 1.0 / 1024, None, ALU.mult, None, [bsumt], ["LST0"])
            tt(sv, sm, sm, ALU.mult, ["LST0"], ["LST1"])
            dve(lambda e: e.scalar_tensor_tensor(out=sv, in0=bsum[:, 256:256 + TT], scalar=1.0 / 1024, in1=sv,
                                                 op0=ALU.mult, op1=ALU.subtract), [bsumt, "LST1"], ["LST1"])
            ts(sv, sv, EPS, None, ALU.add, None, ["LST1"], ["LST1"])
            actf(sv, sv, AF.Sqrt, ["LST1"], ["LST1"])
            dve(lambda e: e.reciprocal(out=sv, in_=sv), ["LST1"], ["LST1"])
            for ft in range(8):
                vv = AV(VV_O, 0, 128, ft * TT, [[1, TT]])
                lt = AV(LT_O, 0, 128, (ft % 2) * 256, [[1, TT]])
                ltk = f"LT{ft % 2}"
                tt(lt, vv, sm, ALU.subtract, ["VV", "LST0"], [ltk])
                tt(lt, lt, sv, ALU.mult, [ltk, "LST1"], [ltk])
                actf(lt, lt, AF.Silu, [ltk, "COLS"], [ltk], scale=col(0, 40 + ft), bias=col(0, 48 + ft))
                tt(AV(Y1_O, 0, 128, ft * TT, [[1, TT]], BF16), lt, AV(SG1_O, 0, 128, ft * TT, [[1, TT]], BF16), ALU.mult,
                   [ltk, "SG1"], ["Y1"])
            for ns in range(2):
                wb, wt = load_w(w_out_c_d, 0, ns * 512)
                for s in range(2):
                    b, bt = bank()
                    for kc in range(8):
                        pe(lambda e, b=b, kc=kc, s=s, wb=wb: e.matmul(
                            b[:, 0:512], lhsT=AV(Y1_O, 0, 128, kc * TT + s * 128, [[1, 128]], BF16),
                            rhs=V(wb, 0, 128, kc * 512, [[1, 512]]), start=(kc == 0), stop=(kc == 7)), ["Y1", wt], [bt])
                    xs_ = X[:, s * 1024 + ns * 512:s * 1024 + ns * 512 + 512]
                    tt(xs_, xs_, b[:, 0:512], ALU.add, [XT, bt], [XT])
            prev_final[0] = (t, X, XT)
            P.barrier()

        if prev_final[0] is not None:
            final_norm(*prev_final[0], GYB_O, "GYB")
    except StopBuild:
        pass
    P.op("sp", lambda e: e.nop(), r=["OUT"])
    P.barrier()
    P.emit()
    P.close()
    return nc


def prep_inputs(inp, b, NT):
    T = NT * TT
    f = lambda a: np.ascontiguousarray(np.asarray(a, dtype=np.float32))
    return {
        "x": f(inp["x"][b, :T]),
        "norm_g": f(inp["norm_g"]).reshape(16, 128),
        "final_g": f(inp["final_g"]),
        "w_in_ab": f(inp["w_in_ab"][0]),
        "a_re": f(inp["s5_a_re"][0]).reshape(32, 128),
        "a_im": f(inp["s5_a_im"][0]).reshape(32, 128),
        "log_dt": f(inp["s5_log_dt"][0]).reshape(32, 2),
        "b_re": f(inp["s5_b_re"][0]).reshape(-1),
        "b_im": f(inp["s5_b_im"][0]).reshape(-1),
        "c_re": f(inp["s5_c_re"][0]).reshape(-1),
        "c_im": f(inp["s5_c_im"][0]).reshape(-1),
        "s5_d": f(inp["s5_d"][0]).reshape(8, 128),
        "glu_w": f(inp["s5_glu_w"][0]),
        "glu_b": f(inp["s5_glu_b"][0]).reshape(8, 128),
        "w_out_ab": f(inp["w_out_ab"][0]),
        "w_in_c": f(inp["w_in_c"][0]),
        "conv_w": f(inp["conv_w"][0]).reshape(248, 128),
        "conv_b": f(inp["conv_b"][0]).reshape(8, 128),
        "ln_g": f(inp["conv_ln_g"][0]).reshape(8, 128),
        "ln_b": f(inp["conv_ln_b"][0]).reshape(8, 128),
        "w_out_c": f(inp["w_out_c"][0]),
        "ctab": make_ctab(NT * 2),
    }


def kernel(**inputs):
    NT = 16
    nc = build(NT)
    in_maps = [prep_inputs(inputs, b, NT) for b in range(8)]
    res = run_bass_kernel_spmd(nc, in_maps, core_ids=list(range(8)))
    return np.stack([np.asarray(r["out"]) for r in res.results], axis=0).astype(np.float32)
```

```python
import math
import os
import numpy as np
import concourse.bass as bass
import concourse.mybir as mybir
from concourse.bass_utils import run_bass_kernel_spmd
from contextlib import ExitStack

F32 = mybir.dt.float32
BF16 = mybir.dt.bfloat16
AF = mybir.ActivationFunctionType
ALU = mybir.AluOpType
AX = mybir.AxisListType

TT = 256
EPS = 1e-6


class StopBuild(Exception):
    pass


class Prog:
    COMPUTE = ("pe", "act", "dve", "pool")
    RING = 8

    def __init__(self, nc):
        self.nc = nc
        self.ops = []
        self.lw = {}
        self.rd = {}
        self.ndma = {"sp": 0, "act": 0, "pool": 0}
        self.stack = ExitStack()
        self.last = {}
        self.pending_dma = []

    def sb(self, name, shape, dt):
        return self.stack.enter_context(self.nc.sbuf_tensor(name, list(shape), dt))

    def ps(self, name, shape, dt):
        return self.stack.enter_context(self.nc.psum_tensor(name, list(shape), dt))

    def op(self, eng, fn, r=(), w=(), dma=False, nobar=False, extra=()):
        i = len(self.ops)
        raw = set()
        oth = set()
        for t in r:
            if t in self.lw:
                raw.add(self.lw[t])
        for t in w:
            if t in self.lw:
                oth.add(self.lw[t])
            for _, j in self.rd.get(t, {}).items():
                oth.add(j)
        deps = set(extra)
        for j in raw | oth:
            oj = self.ops[j]
            if (not dma) and (not oj["dma"]) and oj["eng"] == eng and eng == "pe" and j not in raw:
                continue
            deps.add(j)
        o = dict(eng=eng, fn=fn, deps=deps, dma=dma, sig=False, val=None, slot=None)
        if dma:
            o["slot"] = self.ndma[eng] % self.RING
            self.ndma[eng] += 1
            if not nobar:
                self.pending_dma.append(i)
        else:
            self.last[eng] = i
        self.ops.append(o)
        key = ("dma", eng, i) if dma else eng
        for t in r:
            self.rd.setdefault(t, {})[key] = i
        for t in w:
            self.lw[t] = i
            self.rd[t] = {}
        return i

    def barrier(self):
        deps = set(self.last.values()) | set(self.pending_dma)
        self.pending_dma = []
        for e in ("pe", "act", "dve", "pool", "sp"):
            self.op(e, lambda g: g.nop(), extra=deps)

    def emit(self):
        nc = self.nc
        ops = self.ops
        for o in ops:
            for j in o["deps"]:
                ops[j]["sig"] = True
        cnt = {e: 0 for e in self.COMPUTE + ("sp",)}
        dcnt = {}
        for o in ops:
            if o["dma"]:
                k = (o["eng"], o["slot"])
                dcnt[k] = dcnt.get(k, 0) + 16
                o["val"] = dcnt[k]
            elif o["sig"]:
                cnt[o["eng"]] += 1
                o["val"] = cnt[o["eng"]]
        st = self.stack
        csem = {e: st.enter_context(nc.semaphore(f"s_{e}")) for e in self.COMPUTE + ("sp",)}
        dsem = {}
        for q in ("sp", "act", "pool"):
            for s in range(min(self.RING, self.ndma[q])):
                dsem[(q, s)] = st.enter_context(nc.semaphore(f"d_{q}{s}"))
        block = st.enter_context(nc.Block())
        per = {e: [] for e in ("pe", "act", "dve", "pool", "sp")}
        for i, o in enumerate(ops):
            per[o["eng"]].append(i)

        def semof(o):
            if o["dma"]:
                return dsem[(o["eng"], o["slot"])]
            return csem[o["eng"]]

        def run(e, eng):
            waited = {}
            for i in per[e]:
                o = ops[i]
                need = {}
                for j in o["deps"]:
                    oj = ops[j]
                    s = semof(oj)
                    k = id(s)
                    if waited.get(k, 0) >= oj["val"]:
                        continue
                    if k not in need or need[k][1] < oj["val"]:
                        need[k] = (s, oj["val"])
                if o["dma"]:
                    s = semof(o)
                    prev = o["val"] - 16
                    if prev > 0 and waited.get(id(s), 0) < prev:
                        if id(s) not in need or need[id(s)][1] < prev:
                            need[id(s)] = (s, prev)
                for k, (s, v) in need.items():
                    eng.wait_ge(s, v)
                    waited[k] = v
                ins = o["fn"](eng)
                if o["dma"]:
                    ins.then_inc(semof(o), 16)
                elif o["sig"]:
                    ins.then_inc(semof(o), 1)

        if per["pe"]:
            @block.tensor
            def _(eng):
                run("pe", eng)
        if per["act"]:
            @block.scalar
            def _(eng):
                run("act", eng)
        if per["dve"]:
            @block.vector
            def _(eng):
                run("dve", eng)
        if per["pool"]:
            @block.gpsimd
            def _(eng):
                run("pool", eng)
        if per["sp"]:
            @block.sync
            def _(eng):
                run("sp", eng)

    def close(self):
        self.stack.close()


def V(t, p0, pn, off, pat, dt=None):
    a = t[:] if dt is None else t[:].bitcast(dt)
    base = a[p0:p0 + pn, off:off + 1]
    return bass.AP(base.tensor, base.offset, [list(base.ap[0])] + [list(x) for x in pat])


CT_IDENT = 0
CT_M16 = 128
CT_ONES = 256
CT_MASK = 384
CT_KDEC = CT_MASK + 512
CT_QDEC = CT_KDEC + 8
CT_G64 = CT_QDEC + 8
CT_ROPE = CT_G64 + 8


def make_ctab(nsub):
    n = CT_ROPE + 2 * nsub * 32
    c = np.zeros((128, n), np.float64)
    c[:, CT_IDENT:CT_IDENT + 128] = np.eye(128)
    r = np.arange(128)
    c[:, CT_M16:CT_M16 + 128] = (r[:, None] // 16 == r[None, :] // 16)
    c[:, CT_ONES:CT_ONES + 128] = 1.0
    gam = 1.0 - 2.0 ** (-5.0 - np.arange(8))
    j = r % 64
    i = np.arange(64)
    m = gam[None, :, None] ** np.abs(i[None, None, :] - j[:, None, None]) * (64 ** -0.5)
    c[:, CT_MASK:CT_MASK + 512] = m.reshape(128, 512)
    c[:, CT_KDEC:CT_KDEC + 8] = gam[None, :] ** (63 - j[:, None])
    c[:, CT_QDEC:CT_QDEC + 8] = gam[None, :] ** (j[:, None] + 1.0) * (64 ** -0.5)
    par = r // 64
    for hp in range(4):
        c[:, CT_G64 + hp] = gam[2 * hp + par] ** 64
    freqs = 10000.0 ** (-np.arange(32) / 32.0)
    pos = (np.arange(nsub)[None, :] * 128 + r[:, None]).astype(np.float64)
    ang = pos[:, :, None] * freqs[None, None, :]
    c[:, CT_ROPE:CT_ROPE + nsub * 32] = np.cos(ang).reshape(128, -1)
    c[:, CT_ROPE + nsub * 32:CT_ROPE + 2 * nsub * 32] = np.sin(ang).reshape(128, -1)
    return c.astype(np.float32)


def build(NT, debug=False, stop=99):
    nc = bass.Bass("TRN2", target_bir_lowering=False)
    P = Prog(nc)
    T = NT * TT
    NSUB = NT * 2
    NCT = CT_ROPE + 2 * NSUB * 32

    def din(name, shape):
        return nc.dram_tensor(name, list(shape), F32, kind="ExternalInput")

    x_d = din("x", [T, 1024])
    norm_g_d = din("norm_g", [16, 128])
    final_g_d = din("final_g", [1024])
    w_in_ab_d = din("w_in_ab", [1024, 5120])
    a_re_d = din("a_re", [32, 128])
    a_im_d = din("a_im", [32, 128])
    log_dt_d = din("log_dt", [32, 2])
    b_re_d = din("b_re", [64 * 64 * 16])
    b_im_d = din("b_im", [64 * 64 * 16])
    c_re_d = din("c_re", [64 * 16 * 64])
    c_im_d = din("c_im", [64 * 16 * 64])
    s5_d_d = din("s5_d", [8, 128])
    glu_w_d = din("glu_w", [1024, 1024])
    glu_b_d = din("glu_b", [8, 128])
    w_out_ab_d = din("w_out_ab", [2048, 1024])
    w_in_c_d = din("w_in_c", [1024, 3072])
    conv_w_d = din("conv_w", [248, 128])
    conv_b_d = din("conv_b", [8, 128])
    ln_g_d = din("ln_g", [8, 128])
    ln_b_d = din("ln_b", [8, 128])
    w_out_c_d = din("w_out_c", [1024, 1024])
    ctab_d = din("ctab", [128, NCT])
    out_d = nc.dram_tensor("out", [T, 1024], F32, kind="ExternalOutput")
    DG = nc.dram_tensor("dg_conv", [8, 128, 31 * 128], BF16, kind="Internal")
    WBF = {}
    for nm_, d_ in (("w_in_ab", w_in_ab_d), ("glu_w", glu_w_d), ("w_out_ab", w_out_ab_d),
                    ("w_in_c", w_in_c_d), ("w_out_c", w_out_c_d)):
        WBF[nm_] = nc.dram_tensor("bf_" + nm_, list(d_.shape), BF16, kind="Internal")
    if debug:
        x1_d = nc.dram_tensor("x1", [T, 1024], F32, kind="ExternalOutput")

    CTAB = P.sb("CTAB", [128, CT_ROPE], F32)
    ROPE = P.sb("ROPE", [128, 256], F32)
    IDB = P.sb("IDB", [128, 128], BF16)
    ONESB = P.sb("ONESB", [128, 128], BF16)
    COLS = P.sb("COLS", [128, 384], F32)
    FG = P.sb("FG", [128, 1024], F32)
    KW = P.sb("KW", [128, 64 * 128], BF16)
    WIN = P.sb("WIN", [128, 128 * 128], BF16)
    WOUT = P.sb("WOUT", [128, 512 * 32], BF16)
    TCS = P.sb("TCS", [128, 2 * 1024], F32)
    RHOT = P.sb("RHOT", [128, 1024], F32)
    RHO = P.sb("RHO", [128, 32], F32)
    W0 = P.sb("W0", [128, 64], F32)
    SRET = P.sb("SRET", [128, 512], F32)
    XB = [P.sb(f"X{i}", [128, 2 * 1024], F32) for i in range(2)]
    HT = P.sb("HT", [128, 8 * TT], BF16)
    WB = [P.sb(f"WB{i}", [128, 8 * 512], BF16) for i in range(3)]
    YC = P.sb("YC", [128, 16 * TT], BF16)
    U1 = P.sb("U1", [128, 8 * 288], BF16)
    SMALL = P.sb("SMALL", [128, 64], F32)
    DIAG = [P.sb(f"DIAG{i}", [128, 128], BF16) for i in range(4)]
    ARENA_W = 11520
    AR = P.sb("ARENA", [128, ARENA_W], F32)
    PS = [P.ps(f"ps{i}", [128, 512], F32) for i in range(8)]
    psn = [0]

    def bank():
        i = psn[0] % 7
        psn[0] += 1
        return PS[i], f"ps{i}"

    def ar(off_words, dt=F32):
        return off_words * (2 if dt == BF16 else 1)

    def AV(off_words, p0, pn, eoff, pat, dt=F32):
        if off_words >= 200000:
            return V(WB[1], p0, pn, (off_words - 200000) + eoff, pat, F32)
        if off_words >= 100000:
            return V(WB[0], p0, pn, (off_words - 100000) + eoff, pat, F32)
        return V(AR, p0, pn, ar(off_words, dt) + eoff, pat, dt if dt != F32 else None)

    def ct(off, n, p0=0, pn=128):
        return V(CTAB, p0, pn, off, [[1, n]])

    def col(chunk, c, pn=128):
        return V(COLS, 0, pn, chunk * 128 + c, [[1, 1]])

    dve = lambda fn, r, w: P.op("dve", fn, r, w)
    act = lambda fn, r, w: P.op("act", fn, r, w)
    pe = lambda fn, r, w: P.op("pe", fn, r, w)
    pool = lambda fn, r, w: P.op("pool", fn, r, w)

    def ptt(out, in0, in1, op, r, w):
        pool(lambda e: e.tensor_tensor(out=out, in0=in0, in1=in1, op=op), r, w)

    def tt(out, in0, in1, op, r, w):
        dve(lambda e: e.tensor_tensor(out=out, in0=in0, in1=in1, op=op), r, w)

    def ts(out, in0, s1, s2, op0, op1, r, w):
        if op1 is None:
            dve(lambda e: e.tensor_scalar(out=out, in0=in0, scalar1=s1, scalar2=None, op0=op0), r, w)
        else:
            dve(lambda e: e.tensor_scalar(out=out, in0=in0, scalar1=s1, scalar2=s2, op0=op0, op1=op1), r, w)

    def actf(out, in_, func, r, w, scale=None, bias=None, accum=None):
        kw = {}
        if scale is not None:
            kw["scale"] = scale
        if bias is not None:
            kw["bias"] = bias
        if accum is not None:
            kw["accum_out"] = accum
        act(lambda e: e.activation(out=out, in_=in_, func=func, **kw), r, w)

    conv_order = []
    for c0 in range(0, 5120, 512):
        conv_order.append(("w_in_ab", 0, c0))
    for c0 in (0, 512):
        conv_order.append(("glu_w", 0, c0))
    for c0 in (0, 512):
        conv_order.append(("w_out_ab", 0, c0))
        conv_order.append(("w_out_ab", 1024, c0))
    for c0 in range(0, 3072, 512):
        conv_order.append(("w_in_c", 0, c0))
    for c0 in (0, 512):
        conv_order.append(("w_out_c", 0, c0))
    SRCW = {"w_in_ab": w_in_ab_d, "glu_w": glu_w_d, "w_out_ab": w_out_ab_d, "w_in_c": w_in_c_d, "w_out_c": w_out_c_d}
    for (nm_, r0, c0) in conv_order:
        P.op("pool", lambda e, nm_=nm_, r0=r0, c0=c0: e.dma_start(
            out=WBF[nm_].ap()[r0:r0 + 1024, c0:c0 + 512], in_=SRCW[nm_].ap()[r0:r0 + 1024, c0:c0 + 512]),
            w=[f"CV_{nm_}_{r0}_{c0}"], dma=True, nobar=True)
    P.op("sp", lambda e: e.dma_start(out=CTAB[:], in_=ctab_d.ap()[:, 0:CT_ROPE]), w=["CTAB"], dma=True)
    dve(lambda e: e.tensor_copy(out=IDB[:], in_=ct(CT_IDENT, 128)), ["CTAB"], ["IDB"])
    dve(lambda e: e.tensor_copy(out=ONESB[:], in_=ct(CT_ONES, 128)), ["CTAB"], ["ONESB"])
    P.op("sp", lambda e: e.dma_start(out=FG[:], in_=bass.AP(final_g_d, 0, [[0, 128], [1, 1024]])), w=["FG"], dma=True)
    dve(lambda e: e.memset(W0[:], 0.0), [], ["W0"])
    dve(lambda e: e.memset(SRET[:], 0.0), [], ["SRET"])
    dve(lambda e: e.memset(U1[:], 0.0), [], ["U1"])
    dve(lambda e: e.memset(SMALL[:], 0.0), [], ["SMALL"])
    dve(lambda e: e.memset(SMALL[:, 60:61], EPS), ["SMALL"], ["SMALL"])
    dve(lambda e: e.memset(SMALL[:, 61:62], math.pi / 2), ["SMALL"], ["SMALL"])
    EPSC = SMALL[:, 60:61]
    HPIC = SMALL[:, 61:62]

    STG_O = 0
    dve(lambda e: e.memset(AV(STG_O, 0, 128, 0, [[1, 384]]), 0.0), [], ["STG"])
    pieces = [(norm_g_d, 16, 0, 0), (s5_d_d, 8, 0, 16), (glu_b_d, 8, 0, 24), (conv_b_d, 8, 0, 32),
              (ln_g_d, 8, 0, 40), (ln_b_d, 8, 0, 48)]
    for (d, n, ch, r0) in pieces:
        P.op("sp", lambda e, d=d, n=n, ch=ch, r0=r0: e.dma_start(
            out=AV(STG_O, r0, n, ch * 128, [[1, 128]]), in_=d.ap()), r=["STG"], w=["STG"], dma=True)
    P.op("sp", lambda e: e.dma_start(out=AV(STG_O, 0, 128, 128, [[1, 128]]), in_=conv_w_d.ap()[0:128, :]),
         r=["STG"], w=["STG"], dma=True)
    P.op("sp", lambda e: e.dma_start(out=AV(STG_O, 0, 120, 256, [[1, 128]]), in_=conv_w_d.ap()[128:248, :]),
         r=["STG"], w=["STG"], dma=True)
    for ch in range(3):
        b, bt = bank()
        pe(lambda e, b=b, ch=ch: e.transpose(out=b[:, 0:128], in_=AV(STG_O, 0, 128, ch * 128, [[1, 128]]),
                                             identity=ct(CT_IDENT, 128)), ["STG", "CTAB"], [bt])
        act(lambda e, b=b, ch=ch: e.copy(out=COLS[:, ch * 128:(ch + 1) * 128], in_=b[:, 0:128]), [bt], ["COLS"])

    def cw_col(k, c):
        row = k * 8 + c
        return col(1 + row // 128, row % 128)

    DGS_O = 8800
    for ft in range(8):
        for k in range(31):
            ts(AV(DGS_O, 0, 128, k * 128, [[1, 128]], BF16), IDB[:], cw_col(k, ft), None, ALU.mult, None,
               ["IDB", "COLS", "DGS"], ["DGS"])
        P.op("sp", lambda e, ft=ft: e.dma_start(out=DG.ap()[ft], in_=AV(DGS_O, 0, 128, 0, [[1, 31 * 128]], BF16)),
             r=["DGS"], w=[f"DG{ft}"], dma=True)

    o = 384
    PRM_O = o; o += 384
    LDT_O = o; o += 2
    PT_O = o; o += 96
    DT_O = o; o += 32
    ADT_O = o; o += 32
    ANG_O = o; o += 32
    SC_O = o; o += 64
    T1_O = o; o += 32
    T2_O = o; o += 32
    UPC_O = o; o += 288
    UPS_O = o; o += 288
    MG_O = o; o += 288
    LPR_O = o; o += 288
    LPI_O = o; o += 288
    CF_O = o; o += 64
    BR_O = o; o += 512
    BI_O = o; o += 512
    E0R_O = o; o += 512
    E0I_O = o; o += 512
    CIN_O = o; o += 1024
    CR_O = o; o += 512
    CI_O = o; o += 512
    ER_O = o; o += 512
    EI_O = o; o += 512
    EBR_O = 100000
    EBI_O = 101024
    CBR_O = 200000
    CBI_O = 201024
    TA_O = o; o += 512
    TB_O = o; o += 512
    assert o <= ARENA_W, o

    def a32(off, n=32, eoff=0, p0=0, pn=128):
        return AV(off, p0, pn, eoff, [[1, n]])

    P.op("sp", lambda e: e.dma_start(out=AV(PRM_O, 0, 32, 0, [[1, 128]]), in_=a_re_d.ap()), w=["PRM"], dma=True)
    P.op("sp", lambda e: e.dma_start(out=AV(PRM_O, 0, 32, 128, [[1, 128]]), in_=a_im_d.ap()), w=["PRM"], dma=True)
    P.op("sp", lambda e: e.dma_start(out=AV(LDT_O, 0, 32, 0, [[1, 2]]), in_=log_dt_d.ap()), w=["LDT"], dma=True)
    dve(lambda e: e.tensor_copy(out=AV(PRM_O, 0, 32, 256, [[64, 2], [1, 64]]),
                                in_=AV(LDT_O, 0, 32, 0, [[1, 2], [0, 64]])), ["LDT", "PRM"], ["PRM"])
    b, bt = bank()
    for k in range(3):
        pe(lambda e, b=b, k=k: e.transpose(out=b[:, k * 32:(k + 1) * 32], in_=AV(PRM_O, 0, 32, k * 128, [[1, 128]]),
                                           identity=V(CTAB, 0, 32, CT_IDENT, [[1, 32]])), ["PRM", "CTAB"], [bt])
    act(lambda e, b=b: e.copy(out=a32(PT_O, 96), in_=b[:, 0:96]), [bt], ["PT"])
    ARE = a32(PT_O, 32, 0)
    AIM = a32(PT_O, 32, 32)
    actf(a32(DT_O), a32(PT_O, 32, 64), AF.Exp, ["PT"], ["DT"])
    tt(a32(ADT_O), ARE, a32(DT_O), ALU.mult, ["PT", "DT"], ["ADT"])
    tt(a32(ANG_O), AIM, a32(DT_O), ALU.mult, ["PT", "DT"], ["ANG"])
    actf(a32(SC_O, 32, 0), a32(ANG_O), AF.Sin, ["ANG"], ["SC"], scale=1.0 / 16)
    actf(a32(SC_O, 32, 32), a32(ANG_O), AF.Sin, ["ANG", "SMALL"], ["SC"], scale=1.0 / 16, bias=HPIC)
    for _ in range(4):
        tt(a32(T1_O), a32(SC_O, 32, 0), a32(SC_O, 32, 32), ALU.mult, ["SC"], ["T1"])
        tt(a32(T2_O), a32(SC_O, 32, 0), a32(SC_O, 32, 0), ALU.mult, ["SC"], ["T2"])
        ts(a32(SC_O, 32, 0), a32(T1_O), 2.0, None, ALU.mult, None, ["T1"], ["SC"])
        ts(a32(SC_O, 32, 32), a32(T2_O), -2.0, 1.0, ALU.mult, ALU.add, ["T2"], ["SC"])
    dve(lambda e: e.memset(a32(UPC_O, 32, 0), 1.0), [], ["UPC"])
    dve(lambda e: e.memset(a32(UPS_O, 32, 0), 0.0), [], ["UPS"])
    dve(lambda e: e.tensor_copy(out=a32(UPC_O, 32, 32), in_=a32(SC_O, 32, 32)), ["SC"], ["UPC"])
    dve(lambda e: e.tensor_copy(out=a32(UPS_O, 32, 32), in_=a32(SC_O, 32, 0)), ["SC"], ["UPS"])
    C1 = a32(SC_O, 32, 32)
    S1 = a32(SC_O, 32, 0)
    for e_ in range(1, 8):
        ce = a32(UPC_O, 32, 32 * e_)
        se = a32(UPS_O, 32, 32 * e_)
        tt(a32(T1_O), ce, C1, ALU.mult, ["UPC", "SC"], ["T1"])
        tt(a32(T2_O), se, S1, ALU.mult, ["UPS", "SC"], ["T2"])
        tt(a32(UPC_O, 32, 32 * (e_ + 1)), a32(T1_O), a32(T2_O), ALU.subtract, ["T1", "T2"], ["UPC"])
        tt(a32(T1_O), se, C1, ALU.mult, ["UPS", "SC"], ["T1"])
        tt(a32(T2_O), ce, S1, ALU.mult, ["UPC", "SC"], ["T2"])
        tt(a32(UPS_O, 32, 32 * (e_ + 1)), a32(T1_O), a32(T2_O), ALU.add, ["T1", "T2"], ["UPS"])
    for e_ in range(9):
        actf(a32(MG_O, 32, 32 * e_), a32(ADT_O), AF.Exp, ["ADT"], ["MG"], scale=float(e_))
    tt(a32(LPR_O, 288), a32(MG_O, 288), a32(UPC_O, 288), ALU.mult, ["MG", "UPC"], ["LPR"])
    tt(a32(LPI_O, 288), a32(MG_O, 288), a32(UPS_O, 288), ALU.mult, ["MG", "UPS"], ["LPI"])
    dve(lambda e: e.tensor_copy(out=RHO[:], in_=a32(MG_O, 32, 256)), ["MG"], ["RHO"])
    dve(lambda e: e.tensor_copy(out=V(RHOT, 0, 128, 0, [[32, 32], [1, 32]]),
                                in_=V(RHO, 0, 128, 0, [[1, 32], [0, 32]])), ["RHO"], ["RHOT"])
    dve(lambda e: e.memset(V(RHOT, 0, 128, 0, [[32, 32], [1, 1]]), 0.0), ["RHOT"], ["RHOT"])
    TC = lambda lo, n: V(TCS, 0, 128, lo, [[32, 32], [1, n]])
    TS_ = lambda lo, n: V(TCS, 0, 128, 1024 + lo, [[32, 32], [1, n]])
    dve(lambda e: e.tensor_copy(out=TC(0, 1), in_=AV(UPC_O, 0, 128, 256, [[1, 32], [1, 1]])), ["UPC"], ["TCS"])
    dve(lambda e: e.tensor_copy(out=TS_(0, 1), in_=AV(UPS_O, 0, 128, 256, [[1, 32], [1, 1]])), ["UPS"], ["TCS"])
    n_ = 1
    while n_ < 32:
        pc = V(TCS, 0, 128, n_ - 1, [[32, 32], [0, n_]])
        psn_ = V(TCS, 0, 128, 1024 + n_ - 1, [[32, 32], [0, n_]])
        ta = AV(TA_O, 0, 128, 0, [[n_, 32], [1, n_]])
        tb = AV(TB_O, 0, 128, 0, [[n_, 32], [1, n_]])
        tt(ta, TC(0, n_), pc, ALU.mult, ["TCS"], ["TA"])
        tt(tb, TS_(0, n_), psn_, ALU.mult, ["TCS"], ["TB"])
        tt(TC(n_, n_), ta, tb, ALU.subtract, ["TA", "TB", "TCS"], ["TCS"])
        tt(ta, TC(0, n_), psn_, ALU.mult, ["TCS"], ["TA"])
        tt(tb, TS_(0, n_), pc, ALU.mult, ["TCS"], ["TB"])
        tt(TS_(n_, n_), ta, tb, ALU.add, ["TA", "TB", "TCS"], ["TCS"])
        n_ *= 2
    LR1 = a32(LPR_O, 32, 32)
    LI1 = a32(LPI_O, 32, 32)
    ts(a32(T1_O), LR1, -1.0, None, ALU.add, None, ["LPR"], ["T1"])
    tt(a32(T2_O), ARE, ARE, ALU.mult, ["PT"], ["T2"])
    tt(a32(DT_O), AIM, AIM, ALU.mult, ["PT"], ["DT"])
    tt(a32(T2_O), a32(T2_O), a32(DT_O), ALU.add, ["T2", "DT"], ["T2"])
    dve(lambda e: e.reciprocal(out=a32(T2_O), in_=a32(T2_O)), ["T2"], ["T2"])
    tt(a32(DT_O), a32(T1_O), ARE, ALU.mult, ["T1", "PT"], ["DT"])
    tt(a32(ANG_O), LI1, AIM, ALU.mult, ["LPI", "PT"], ["ANG"])
    tt(a32(DT_O), a32(DT_O), a32(ANG_O), ALU.add, ["DT", "ANG"], ["DT"])
    tt(a32(CF_O, 32, 0), a32(DT_O), a32(T2_O), ALU.mult, ["DT", "T2"], ["CF"])
    tt(a32(DT_O), LI1, ARE, ALU.mult, ["LPI", "PT"], ["DT"])
    tt(a32(ANG_O), a32(T1_O), AIM, ALU.mult, ["T1", "PT"], ["ANG"])
    tt(a32(DT_O), a32(DT_O), a32(ANG_O), ALU.subtract, ["DT", "ANG"], ["DT"])
    tt(a32(CF_O, 32, 32), a32(DT_O), a32(T2_O), ALU.mult, ["DT", "T2"], ["CF"])
    P.op("sp", lambda e: e.dma_start(out=AV(BR_O, 0, 128, 0, [[16, 32], [1, 16]]),
                                     in_=bass.AP(b_re_d, 0, [[16, 128], [2048, 32], [1, 16]])), w=["BR"], dma=True)
    P.op("sp", lambda e: e.dma_start(out=AV(BI_O, 0, 128, 0, [[16, 32], [1, 16]]),
                                     in_=bass.AP(b_im_d, 0, [[16, 128], [2048, 32], [1, 16]])), w=["BI"], dma=True)
    for ri, cd in enumerate((c_re_d, c_im_d)):
        for pl in range(8):
            for i4 in range(4):
                P.op("sp", lambda e, ri=ri, cd=cd, pl=pl, i4=i4: e.dma_start(
                    out=AV(CIN_O, pl * 16, 16, ri * 512 + i4 * 128, [[64, 2], [1, 64]]),
                    in_=bass.AP(cd, pl * 2048 + i4 * 16384, [[64, 16], [1024, 2], [1, 64]])), w=["CIN"], dma=True)
    for ri, co in enumerate((CR_O, CI_O)):
        b, bt = bank()
        for i4 in range(4):
            pe(lambda e, b=b, ri=ri, i4=i4: e.transpose(
                out=b[:, i4 * 128:(i4 + 1) * 128], in_=AV(CIN_O, 0, 128, ri * 512 + i4 * 128, [[1, 128]]),
                identity=ct(CT_IDENT, 128)), ["CIN", "CTAB"], [bt])
        act(lambda e, b=b, co=co: e.copy(out=a32(co, 512), in_=b[:, 0:512]), [bt], ["CR" if ri == 0 else "CI"])

    def bc16(off, eoff):
        return AV(off, 0, 128, eoff, [[1, 32], [0, 16]])

    def v512(off):
        return AV(off, 0, 128, 0, [[16, 32], [1, 16]])

    def cmul(outr, outi, ar_, ai_, xr, xi, rtoks, wr, wi, neg_im=False):
        tt(v512(TA_O), ar_, xr, ALU.mult, rtoks, ["TA"])
        tt(v512(TB_O), ai_, xi, ALU.mult, rtoks, ["TB"])
        tt(outr, v512(TA_O), v512(TB_O), ALU.subtract, ["TA", "TB"], [wr])
        tt(v512(TA_O), ar_, xi, ALU.mult, rtoks, ["TA"])
        tt(v512(TB_O), ai_, xr, ALU.mult, rtoks, ["TB"])
        if neg_im:
            dve(lambda e: e.scalar_tensor_tensor(out=outi, in0=v512(TA_O), scalar=-1.0, in1=v512(TB_O),
                                                 op0=ALU.mult, op1=ALU.subtract), ["TA", "TB"], [wi])
        else:
            tt(outi, v512(TA_O), v512(TB_O), ALU.add, ["TA", "TB"], [wi])

    cmul(v512(E0R_O), v512(E0I_O), bc16(CF_O, 0), bc16(CF_O, 32), v512(BR_O), v512(BI_O),
         ["CF", "BR", "BI"], "E0R", "E0I")
    dve(lambda e: e.memset(a32(EBR_O, 1024), 0.0), [], ["EBR"])
    dve(lambda e: e.memset(a32(EBI_O, 1024), 0.0), [], ["EBI"])
    dve(lambda e: e.memset(a32(CBR_O, 1024), 0.0), [], ["CBR"])
    dve(lambda e: e.memset(a32(CBI_O, 1024), 0.0), [], ["CBI"])

    def expand(dst_o, src_o, rt, wt):
        for gp in range(2):
            dve(lambda e, gp=gp: e.tensor_copy(
                out=AV(dst_o, gp * 64, 64, gp * 16, [[32, 32], [1, 16]]),
                in_=AV(src_o, gp * 64, 64, 0, [[16, 32], [1, 16]])), [rt, wt], [wt])

    for r_ in range(8):
        cmul(v512(ER_O), v512(EI_O), bc16(LPR_O, 32 * (r_ + 1)), bc16(LPI_O, 32 * (r_ + 1)), v512(CR_O), v512(CI_O),
             ["LPR", "LPI", "CR", "CI"], "ER", "EI", neg_im=True)
        expand(CBR_O, ER_O, "ER", "CBR")
        expand(CBI_O, EI_O, "EI", "CBI")
        for ri, so in enumerate((CBR_O, CBI_O)):
            act(lambda e, r_=r_, ri=ri, so=so: e.copy(
                out=V(WOUT, 0, 128, (r_ * 2 + ri) * 32, [[512, 32], [1, 32]]),
                in_=AV(so, 0, 128, 0, [[32, 32], [1, 32]])), ["CBR" if ri == 0 else "CBI"], ["WOUT"])
    dve(lambda e: e.tensor_copy(out=v512(ER_O), in_=v512(CR_O)), ["CR", "ER"], ["ER"])
    ts(v512(EI_O), v512(CI_O), -1.0, None, ALU.mult, None, ["CI", "EI"], ["EI"])
    expand(CBR_O, ER_O, "ER", "CBR")
    expand(CBI_O, EI_O, "EI", "CBI")
    for tau in range(8):
        cmul(v512(ER_O), v512(EI_O), bc16(LPR_O, 32 * tau), bc16(LPI_O, 32 * tau), v512(E0R_O), v512(E0I_O),
             ["LPR", "LPI", "E0R", "E0I"], "ER", "EI")
        expand(EBR_O, ER_O, "ER", "EBR")
        expand(EBI_O, EI_O, "EI", "EBI")
        s_ = 7 - tau
        for ft in range(8):
            b, bt = bank()
            for ri, so in enumerate((EBR_O, EBI_O)):
                pe(lambda e, b=b, ri=ri, so=so, ft=ft: e.transpose(
                    out=b[:, ri * 128:(ri + 1) * 128], in_=AV(so, 0, 128, ft * 128, [[1, 128]]),
                    identity=ct(CT_IDENT, 128)), ["EBR" if ri == 0 else "EBI", "CTAB"], [bt])
            act(lambda e, b=b, ft=ft, s_=s_: e.copy(
                out=V(WIN, 0, 128, ((ft * 8 + s_) * 2) * 128, [[1, 256]]), in_=b[:, 0:256]), [bt], ["WIN"])
            pe(lambda e, b=b, ft=ft: e.matmul(b[:, 256:384], lhsT=AV(EBR_O, 0, 128, ft * 128, [[1, 128]]),
                                             rhs=AV(CBR_O, 0, 128, ft * 128, [[1, 128]]), start=True, stop=False),
               ["EBR", "CBR"], [bt])
            pe(lambda e, b=b, ft=ft: e.matmul(b[:, 256:384], lhsT=AV(EBI_O, 0, 128, ft * 128, [[1, 128]]),
                                             rhs=AV(CBI_O, 0, 128, ft * 128, [[1, 128]]), start=False, stop=True),
               ["EBI", "CBI"], [bt])
            kdst = V(KW, 0, 128, (ft * 8 + tau) * 128, [[1, 128]])
            if tau == 0:
                tt(AV(TA_O, 0, 128, 0, [[1, 128]]), b[:, 256:384], ct(CT_M16, 128), ALU.mult, [bt, "CTAB"], ["TA"])
                dve(lambda e, kdst=kdst, ft=ft: e.scalar_tensor_tensor(
                    out=kdst, in0=ct(CT_IDENT, 128), scalar=col(0, 16 + ft), in1=AV(TA_O, 0, 128, 0, [[1, 128]]),
                    op0=ALU.mult, op1=ALU.add), ["TA", "CTAB", "COLS"], ["KW"])
            else:
                tt(kdst, b[:, 256:384], ct(CT_M16, 128), ALU.mult, [bt, "CTAB"], ["KW"])
    P.barrier()
    pe(lambda e: e.matmul(PS[0][0:32, 0:32], lhsT=IDB[:, 0:32], rhs=IDB[:, 0:32], start=True, stop=True), ["IDB"], ["ps0"])
    P.barrier()
    if stop == 0:
        NT = 0

    wbn = [0]

    def load_w(dram, r0, c0):
        i = wbn[0] % 3
        wbn[0] += 1
        nm_ = [k for k, v in SRCW.items() if v is dram][0]
        src = WBF[nm_].ap()[r0:r0 + 1024, c0:c0 + 512].rearrange("(c p) f -> p c f", p=128)
        P.op("sp", lambda e, i=i, src=src: e.dma_start(out=V(WB[i], 0, 128, 0, [[512, 8], [1, 512]]), in_=src),
             r=[f"CV_{nm_}_{r0}_{c0}"], w=[f"WB{i}"], dma=True, nobar=True)
        return WB[i], f"WB{i}"

    def rms_to_ht(gl, XS_O, X, XT):
        import os
        ksub = int(os.environ.get("KSUB", "9"))
        if ksub < 1:
            return
        for s in range(2):
            actf(AV(XS_O, 0, 128, 0, [[1, 1024]], BF16), X[:, s * 1024:(s + 1) * 1024], AF.Square,
                 [XT], ["XS", "SMALL"], accum=SMALL[:, s:s + 1])
        if ksub < 2:
            return
        ts(SMALL[:, 2:4], SMALL[:, 0:2], 1.0 / 1024, EPS, ALU.mult, ALU.add, ["SMALL"], ["SMALL"])
        actf(SMALL[:, 2:4], SMALL[:, 2:4], AF.Sqrt, ["SMALL"], ["SMALL"])
        dve(lambda e: e.reciprocal(out=SMALL[:, 4:6], in_=SMALL[:, 2:4]), ["SMALL"], ["SMALL"])
        if ksub < 3:
            return
        for s in range(2):
            ts(AV(XS_O, 0, 128, s * 1024, [[1, 1024]], BF16), X[:, s * 1024:(s + 1) * 1024], SMALL[:, 4 + s:5 + s],
               None, ALU.mult, None, [XT, "SMALL"], ["XS"])
        if ksub < 4:
            return
        for c in range(8):
            b, bt = bank()
            for s in range(2):
                pe(lambda e, b=b, c=c, s=s: e.transpose(
                    out=V(b, 0, 128, s * 128, [[1, 128]], BF16),
                    in_=AV(XS_O, 0, 128, s * 1024 + c * 128, [[1, 128]], BF16), identity=IDB[:]), ["XS", "IDB"], [bt])
            ts(HT[:, c * TT:(c + 1) * TT], V(b, 0, 128, 0, [[1, TT]], BF16), col(0, gl * 8 + c), None, ALU.mult, None,
               [bt, "COLS"], ["HT"])

    def proj_fm(wb, wt, cc, rhs_t, rhs_tok, rhs_fn=None):
        b, bt = bank()
        for c in range(8):
            rhs = rhs_fn(c) if rhs_fn is not None else rhs_t[:, c * TT:(c + 1) * TT]
            pe(lambda e, b=b, c=c, rhs=rhs: e.matmul(b[:, 0:TT], lhsT=V(wb, 0, 128, c * 512 + cc, [[1, 128]]),
                                                     rhs=rhs, start=(c == 0), stop=(c == 7)),
               [wt, rhs_tok], [bt])
        return b, bt

    def proj_tm(wb, wt, s):
        b, bt = bank()
        for c in range(8):
            pe(lambda e, b=b, c=c: e.matmul(b[:, 0:512], lhsT=HT[:, c * TT + s * 128:c * TT + s * 128 + 128],
                                            rhs=V(wb, 0, 128, c * 512, [[1, 512]]), start=(c == 0), stop=(c == 7)),
               [wt, "HT"], [bt])
        return b, bt

    XS_O = 0
    ZT1_O = 0; ZT2_O = 1024; ZR_O = 2048; ZI_O = 3072; ZPR_O = 4096; ZPI_O = 5120
    U0_O = 6144
    SG0_O = 7168
    SAL_O = 8192
    GYB_O = 9280
    ZS_O = 10304
    assert ZS_O + 1024 <= ARENA_W

    def chk(k):
        if stop == k:
            P.barrier()
            raise StopBuild()

    try:
        def load_x(tt_):
            P.op("pool", lambda e, tt_=tt_: e.dma_start(
                out=V(XB[tt_ % 2], 0, 128, 0, [[1024, 2], [1, 1024]]),
                in_=x_d.ap()[tt_ * TT:(tt_ + 1) * TT, :].rearrange("(s p) d -> p s d", p=128)),
                w=[f"X{tt_ % 2}"], dma=True, nobar=True)

        prev_final = [None]

        def final_norm(tf, X, XT, junk_o, junk_tok):
            for s in range(2):
                actf(AV(junk_o, 0, 128, 0, [[1, 1024]], BF16), X[:, s * 1024:(s + 1) * 1024], AF.Square,
                     [XT], [junk_tok, "SMALL"], accum=SMALL[:, 8 + s:9 + s])
            ts(SMALL[:, 10:12], SMALL[:, 8:10], 1.0 / 1024, EPS, ALU.mult, ALU.add, ["SMALL"], ["SMALL"])
            actf(SMALL[:, 10:12], SMALL[:, 10:12], AF.Sqrt, ["SMALL"], ["SMALL"])
            dve(lambda e: e.reciprocal(out=SMALL[:, 12:14], in_=SMALL[:, 10:12]), ["SMALL"], ["SMALL"])
            for s in range(2):
                xs_ = X[:, s * 1024:(s + 1) * 1024]
                dve(lambda e, xs_=xs_, s=s: e.scalar_tensor_tensor(out=xs_, in0=xs_, scalar=SMALL[:, 12 + s:13 + s], in1=FG[:],
                                                                   op0=ALU.mult, op1=ALU.mult), [XT, "SMALL", "FG"], [XT])
            P.op("pool", lambda e, tf=tf, X=X: e.dma_start(
                out=out_d.ap()[tf * TT:(tf + 1) * TT, :].rearrange("(s p) d -> p s d", p=128),
                in_=V(X, 0, 128, 0, [[1024, 2], [1, 1024]])), r=[XT], w=["OUT"], dma=True, nobar=True)

        if NT > 0:
            load_x(0)
            rms_to_ht(0, XS_O, XB[0], "X0")
        for t in range(NT):
            tok0 = t * TT
            X = XB[t % 2]
            XT = f"X{t % 2}"
            rp = (t % 2) * 128
            for cs in range(2):
                P.op("pool", lambda e, cs=cs, t=t, rp=rp: e.dma_start(
                    out=ROPE[:, rp + cs * 64:rp + cs * 64 + 64],
                    in_=ctab_d.ap()[:, CT_ROPE + cs * NSUB * 32 + 2 * t * 32:CT_ROPE + cs * NSUB * 32 + 2 * t * 32 + 64]),
                    w=[f"ROPE{t % 2}"], dma=True, nobar=True)
            if prev_final[0] is not None:
                final_norm(*prev_final[0], GYB_O, "GYB")
                prev_final[0] = None
            if t + 1 < NT:
                load_x(t + 1)

            chk(10)
            U0 = lambda p0, pn, eoff, pat: AV(U0_O, p0, pn, eoff, pat, BF16)
            U0M = lambda p0, pn, eoff, pat: AV(ZPR_O, p0, pn, eoff, pat, BF16)
            for half in range(2):
                wb, wt = load_w(w_in_ab_d, 0, half * 512)
                for f in range(4):
                    ft = half * 4 + f
                    b, bt = proj_fm(wb, wt, f * 128, HT, "HT")
                    act(lambda e, b=b, ft=ft: e.copy(out=U0(0, 128, ft * TT, [[1, 32], [32, 8]]),
                                                     in_=V(b, 0, 128, 0, [[8, 32], [1, 8]])), [bt], [f"U0_{ft}"])
                    act(lambda e, b=b, ft=ft: e.copy(out=U0M(64, 64, ft * TT, [[1, 32], [32, 8]]),
                                                     in_=V(b, 64, 64, 0, [[8, 32], [1, 8]])), [bt], [f"U0M_{ft}"])
                    dve(lambda e, ft=ft: e.memset(U0M(64, 32, ft * TT, [[1, TT]]), 0.0), [f"U0M_{ft}"], [f"U0M_{ft}"])
            chk(11)
            zb = [bank() for _ in range(4)]
            for ft in range(8):
                for q in range(4):
                    b, bt = zb[q]
                    for ri in range(2):
                        for s in range(8):
                            if q < 3:
                                pe(lambda e, b=b, ri=ri, s=s, ft=ft, q=q: e.matmul(
                                    b[:, ri * 256 + ft * 32: ri * 256 + ft * 32 + 32],
                                    lhsT=V(WIN, 32 * q, 32, ((ft * 8 + s) * 2 + ri) * 128, [[1, 128]]),
                                    rhs=U0(32 * q, 32, ft * TT + s * 32, [[1, 32]]),
                                    start=(s == 0), stop=(s == 7), tile_position=(32 * q, 0)), ["WIN", f"U0_{ft}"], [bt])
                            else:
                                pe(lambda e, b=b, ri=ri, s=s, ft=ft: e.matmul(
                                    b[:, ri * 256 + ft * 32: ri * 256 + ft * 32 + 32],
                                    lhsT=V(WIN, 64, 64, ((ft * 8 + s) * 2 + ri) * 128, [[1, 128]]),
                                    rhs=U0M(64, 64, ft * TT + s * 32, [[1, 32]]),
                                    start=(s == 0), stop=(s == 7), tile_position=(64, 0)), ["WIN", f"U0M_{ft}"], [bt])
            for q in range(4):
                b, bt = zb[q]
                act(lambda e, b=b, q=q: e.copy(out=AV(ZR_O, 0, 128, q * 32, [[128, 8], [1, 32]]),
                                               in_=V(b, 0, 128, 0, [[32, 8], [1, 32]])), [bt], ["ZR"])
                act(lambda e, b=b, q=q: e.copy(out=AV(ZI_O, 0, 128, q * 32, [[128, 8], [1, 32]]),
                                               in_=V(b, 0, 128, 256, [[32, 8], [1, 32]])), [bt], ["ZI"])
            for half in range(2):
                wb, wt = load_w(w_in_ab_d, 0, 1024 + half * 512)
                for f in range(4):
                    ft = half * 4 + f
                    b, bt = proj_fm(wb, wt, f * 128, HT, "HT")
                    actf(AV(SG0_O, 0, 128, ft * TT, [[1, TT]], BF16), b[:, 0:TT], AF.Silu, [bt], ["SG0"])
            chk(12)
            SAL = lambda ri, lo, n: AV(SAL_O, 0, 128, ri * 1056 + lo, [[33, 32], [1, n]], BF16)
            for ri in range(2):
                dve(lambda e, ri=ri: e.tensor_copy(out=SAL(ri, 0, 1), in_=V(W0, 0, 128, ri * 32, [[1, 32], [1, 1]])),
                    ["W0"], ["SAL"])
            f1k = lambda off: AV(off, 0, 128, 0, [[1, 1024]])
            TCf = V(TCS, 0, 128, 0, [[1, 1024]])
            TSf = V(TCS, 0, 128, 1024, [[1, 1024]])
            tt(f1k(ZT1_O), TCf, f1k(ZR_O), ALU.mult, ["TCS", "ZR"], ["ZT1", "XS"])
            tt(f1k(ZT2_O), TSf, f1k(ZI_O), ALU.mult, ["TCS", "ZI"], ["ZT2"])
            tt(f1k(ZPR_O), f1k(ZT1_O), f1k(ZT2_O), ALU.add, ["ZT1", "ZT2"], ["ZPR"] + [f"U0M_{k}" for k in range(8)])
            tt(f1k(ZT1_O), TCf, f1k(ZI_O), ALU.mult, ["TCS", "ZI"], ["ZT1"])
            tt(f1k(ZT2_O), TSf, f1k(ZR_O), ALU.mult, ["TCS", "ZR"], ["ZT2"])
            tt(f1k(ZPI_O), f1k(ZT1_O), f1k(ZT2_O), ALU.subtract, ["ZT1", "ZT2"], ["ZPI"])
            for ri, zo, wo_, tk in ((0, ZPR_O, ZR_O, "ZPR"), (1, ZPI_O, ZI_O, "ZPI")):
                z0 = AV(zo, 0, 128, 0, [[32, 32], [1, 1]])
                tt(AV(ZT1_O, 0, 128, 0, [[1, 32], [1, 1]]), V(RHO, 0, 128, 0, [[1, 32], [1, 1]]),
                   V(W0, 0, 128, ri * 32, [[1, 32], [1, 1]]), ALU.mult, ["RHO", "W0"], ["ZT1"])
                tt(z0, z0, AV(ZT1_O, 0, 128, 0, [[1, 32], [1, 1]]), ALU.add, [tk, "ZT1"], [tk])
                wtk = "ZR" if ri == 0 else "ZI"
                dve(lambda e, zo=zo, wo_=wo_: e.tensor_tensor_scan(
                    out=f1k(wo_), data0=RHOT[:], data1=f1k(zo), initial=0.0, op0=ALU.mult, op1=ALU.add),
                    [tk, "RHOT"], [wtk])
            tt(f1k(ZT1_O), TCf, f1k(ZR_O), ALU.mult, ["TCS", "ZR"], ["ZT1"])
            tt(f1k(ZT2_O), TSf, f1k(ZI_O), ALU.mult, ["TCS", "ZI"], ["ZT2"])
            tt(f1k(ZPR_O), f1k(ZT1_O), f1k(ZT2_O), ALU.subtract, ["ZT1", "ZT2"], ["ZPR"])
            tt(f1k(ZT1_O), TCf, f1k(ZI_O), ALU.mult, ["TCS", "ZI"], ["ZT1"])
            tt(f1k(ZT2_O), TSf, f1k(ZR_O), ALU.mult, ["TCS", "ZR"], ["ZT2"])
            tt(f1k(ZPI_O), f1k(ZT1_O), f1k(ZT2_O), ALU.add, ["ZT1", "ZT2"], ["ZPI"])
            for ri, zo, tk in ((0, ZPR_O, "ZPR"), (1, ZPI_O, "ZPI")):
                dve(lambda e, ri=ri, zo=zo: e.tensor_copy(out=SAL(ri, 1, 32), in_=AV(zo, 0, 128, 0, [[32, 32], [1, 32]])),
                    [tk], ["SAL"])
                dve(lambda e, ri=ri, zo=zo: e.tensor_copy(out=V(W0, 0, 128, ri * 32, [[1, 32], [1, 1]]),
                                                          in_=AV(zo, 0, 128, 31, [[32, 32], [1, 1]])), [tk], ["W0"])
            chk(13)
            C_G = 0.7978845608028654
            for ft in range(8):
                b, bt = bank()
                bv = lambda p0, pn, lo, n, b=b: V(b, p0, pn, lo, [[8, 32], [1, n]])
                for tau in range(8):
                    nn = (8 - tau) * 32
                    pe(lambda e, b=b, ft=ft, tau=tau, nn=nn: e.matmul(
                        b[:, tau * 32:TT], lhsT=V(KW, 0, 128, (ft * 8 + tau) * 128, [[1, 128]]),
                        rhs=U0(0, 128, ft * TT, [[1, nn]]), start=(tau == 0), stop=False), ["KW", f"U0_{ft}"], [bt])
                for q in range(4):
                    pair = ft * 4 + q
                    for r_ in range(8):
                        for ri in range(2):
                            last = (r_ == 7 and ri == 1)
                            pe(lambda e, b=b, q=q, pair=pair, r_=r_, ri=ri, last=last, bv=bv: e.matmul(
                                V(b, 32 * q, 32, r_ * 32, [[1, 32]]), lhsT=V(WOUT, 0, 128, ((pair * 8 + r_) * 2 + ri) * 32, [[1, 32]]),
                                rhs=AV(SAL_O, 0, 128, ri * 1056 + pair * 33, [[1, 32]], BF16),
                                start=False, stop=last, tile_position=(0, 32 * q)), ["WOUT", "SAL"], [bt])
                t1 = AV(ZT1_O, 0, 128, (ft % 2) * TT, [[1, TT]])
                t2 = AV(ZT2_O, 0, 128, (ft % 2) * TT, [[1, TT]])
                k1, k2 = ("ZT1", "ZT2")
                actf(t1, b[:, 0:TT], AF.Square, [bt], [k1])
                ts(t1, t1, 0.044715, 1.0, ALU.mult, ALU.add, [k1], [k1])
                tt(t2, t1, b[:, 0:TT], ALU.mult, [k1, bt], [k2])
                actf(t2, t2, AF.Sigmoid, [k2], [k2], scale=2.0 * C_G)
                tt(AV(GYB_O, 0, 128, ft * TT, [[1, 8], [8, 32]], BF16), AV(ZT2_O, 0, 128, (ft % 2) * TT, [[32, 8], [1, 32]]),
                   V(b, 0, 128, 0, [[32, 8], [1, 32]]), ALU.mult, [k2, bt], ["GYB"])
            chk(14)
            GYB = AV(GYB_O, 0, 128, 0, [[1, 8 * TT]], BF16)
            for half in range(2):
                wb, wt = load_w(glu_w_d, 0, half * 512)
                for f in range(4):
                    ft = half * 4 + f
                    b, bt = proj_fm(wb, wt, f * 128, GYB, "GYB")
                    zs = AV(ZS_O, 0, 128, ft * TT, [[1, TT]], BF16)
                    actf(zs, b[:, 0:TT], AF.Sigmoid, [bt, "COLS"], ["ZS"], bias=col(0, 24 + ft))
                    tt(zs, zs, AV(GYB_O, 0, 128, ft * TT, [[1, TT]], BF16), ALU.mult, ["ZS", "GYB"], ["ZS"])
                    tt(YC[:, ft * TT:(ft + 1) * TT], zs, AV(SG0_O, 0, 128, ft * TT, [[1, TT]], BF16), ALU.mult,
                       ["ZS", "SG0"], ["YC"])
            P.barrier()
            if stop == 1:
                break

            QR_O = 0; KR_O = 512; KD0_O = 1024; KD1_O = 1536; QD_O = 2048
            QT0_O = 2560; QT1_O = 3072; KT_O = 3584; QDT0_O = 4096; QDT1_O = 4608
            VB_O = 5120
            SMF_O = 6144
            SB_O = 7168
            SGR_O = 8192
            RT_O = 9216
            OF_O = 10240
            ST_O = 9216
            QT_OS = (QT0_O, QT1_O)
            QDT_OS = (QDT0_O, QDT1_O)
            KD_OS = (KD0_O, KD1_O)
            for off, nm in ((QT0_O, "QT0"), (QDT0_O, "QDT0"), (KD0_O, "KD0")):
                pool(lambda e, off=off: e.memset(AV(off, 64, 64, 0, [[1, 1024]], BF16), 0.0), [], [nm])
            for off, nm in ((QT1_O, "QT1"), (QDT1_O, "QDT1"), (KD1_O, "KD1")):
                pool(lambda e, off=off: e.memset(AV(off, 0, 64, 0, [[1, 1024]], BF16), 0.0), [], [nm])
            pool(lambda e: e.memset(AV(SMF_O, 0, 128, 0, [[1, 2048]], BF16), 0.0), [], ["SMF"])
            ropeC = lambda s_: V(ROPE, 0, 128, rp + s_ * 32, [[0, 8], [1, 32]])
            ropeS = lambda s_: V(ROPE, 0, 128, rp + 64 + s_ * 32, [[0, 8], [1, 32]])
            RTK = f"ROPE{t % 2}"
            for qk, col0, dst in ((0, 2048, QR_O), (1, 2560, KR_O)):
                wb, wt = load_w(w_in_ab_d, 0, col0)
                for s in range(2):
                    b, bt = proj_tm(wb, wt, s)
                    sub = s
                    x1 = V(b, 0, 128, 0, [[64, 8], [1, 32]])
                    x2 = V(b, 0, 128, 32, [[64, 8], [1, 32]])
                    r4 = lambda k: AV(RT_O, 0, 128, k * 256, [[32, 8], [1, 32]])
                    tt(r4(0), x1, ropeC(sub), ALU.mult, [bt, RTK], ["RT0"])
                    tt(r4(1), x2, ropeS(sub), ALU.mult, [bt, RTK], ["RT1"])
                    tt(r4(2), x1, ropeS(sub), ALU.mult, [bt, RTK], ["RT2"])
                    tt(r4(3), x2, ropeC(sub), ALU.mult, [bt, RTK], ["RT3"])
                    o1 = AV(dst, 0, 128, s * 512, [[64, 8], [1, 32]], BF16)
                    o2 = AV(dst, 0, 128, s * 512 + 32, [[64, 8], [1, 32]], BF16)
                    tk = "QR" if qk == 0 else "KR"
                    tt(o1, r4(0), r4(1), ALU.subtract, ["RT0", "RT1"], [tk])
                    tt(o2, r4(2), r4(3), ALU.add, ["RT2", "RT3"], [tk])
                    if qk == 0:
                        tt(AV(QD_O, 0, 128, s * 512, [[64, 8], [1, 64]], BF16),
                           AV(dst, 0, 128, s * 512, [[64, 8], [1, 64]], BF16),
                           V(CTAB, 0, 128, CT_QDEC, [[1, 8], [0, 64]]), ALU.mult, [tk, "CTAB"], ["QD"])
                    else:
                        for hf in range(2):
                            tt(AV(KD_OS[hf], 64 * hf, 64, s * 512, [[64, 8], [1, 64]], BF16),
                               AV(dst, 64 * hf, 64, s * 512, [[64, 8], [1, 64]], BF16),
                               V(CTAB, 64 * hf, 64, CT_KDEC, [[1, 8], [0, 64]]), ALU.mult, [tk, "CTAB"], [f"KD{hf}"])
            for src, stk, dsts in ((QR_O, "QR", (QT0_O, QT1_O)), (KR_O, "KR", None), (QD_O, "QD", (QDT0_O, QDT1_O))):
                for s in range(2):
                    b, bt = bank()
                    for hp in range(4):
                        pe(lambda e, b=b, src=src, s=s, hp=hp: e.transpose(
                            out=V(b, 0, 128, hp * 128, [[1, 128]], BF16),
                            in_=AV(src, 0, 128, s * 512 + hp * 128, [[1, 128]], BF16), identity=IDB[:]), [stk, "IDB"], [bt])
                    if dsts is None:
                        act(lambda e, b=b, s=s: e.copy(
                            out=AV(KT_O, 0, 128, s * 128, [[TT, 4], [1, 128]], BF16),
                            in_=V(b, 0, 128, 0, [[128, 4], [1, 128]], BF16)), [bt], ["KT"])
                    else:
                        for hf in range(2):
                            nm = ("QT" if stk == "QR" else "QDT") + str(hf)
                            act(lambda e, b=b, s=s, hf=hf, dsts=dsts: e.copy(
                                out=AV(dsts[hf], 64 * hf, 64, s * 128, [[TT, 4], [1, 128]], BF16),
                                in_=V(b, 64 * hf, 64, 0, [[128, 4], [1, 128]], BF16)), [bt], [nm])
            for half in range(2):
                wb, wt = load_w(w_in_ab_d, 0, 3072 + half * 512)
                for s in range(2):
                    b, bt = proj_tm(wb, wt, s)
                    act(lambda e, b=b, s=s, half=half: e.copy(
                        out=AV(VB_O, 0, 128, s * 1024 + half * 512, [[1, 512]], BF16), in_=b[:, 0:512]), [bt], ["VB"])
            for half in range(2):
                wb, wt = load_w(w_in_ab_d, 0, 4096 + half * 512)
                for f in range(4):
                    ft = half * 4 + f
                    b, bt = proj_fm(wb, wt, f * 128, HT, "HT")
                    actf(AV(SGR_O, 0, 128, ft * TT, [[1, TT]], BF16), b[:, 0:TT], AF.Silu, [bt], ["SGR"])
            for h in range(8):
                hp, par = h // 2, h % 2
                b, bt = bank()
                for c in range(4):
                    s, cpar = c // 2, c % 2
                    tk0 = s * 128 + cpar * 64
                    pe(lambda e, b=b, hp=hp, par=par, s=s, cpar=cpar, tk0=tk0: e.matmul(
                        b[64 * cpar:64 * cpar + 64, s * 64:(s + 1) * 64],
                        lhsT=AV(KT_O, 0, 128, hp * TT + tk0, [[1, 64]], BF16),
                        rhs=AV(QT_OS[par], 0, 128, hp * TT + tk0, [[1, 64]], BF16), start=True, stop=True,
                        tile_position=(0, 64 * cpar)), ["KT", f"QT{par}"], [bt])
                for cpar in range(2):
                    tt(AV(SMF_O, 64 * cpar, 64, h * 256 + cpar * 64, [[128, 2], [1, 64]], BF16),
                       V(b, 64 * cpar, 64, 0, [[64, 2], [1, 64]]),
                       V(CTAB, 64 * cpar, 64, CT_MASK + h * 64, [[0, 2], [1, 64]]), ALU.mult, [bt, "CTAB"], ["SMF"])
            for hp in range(4):
                b, bt = bank()
                for c in range(4):
                    s, cpar = c // 2, c % 2
                    for par in range(2):
                        h = hp * 2 + par
                        pe(lambda e, b=b, c=c, s=s, cpar=cpar, par=par, h=h: e.matmul(
                            b[64 * par:64 * par + 64, c * 128:(c + 1) * 128],
                            lhsT=AV(KD_OS[cpar], 0, 128, s * 512 + h * 64, [[1, 64]], BF16),
                            rhs=AV(VB_O, 0, 128, s * 1024 + h * 128, [[1, 128]], BF16), start=True, stop=True,
                            tile_position=(0, 64 * par)), [f"KD{cpar}", "VB"], [bt])
                for c in range(4):
                    sst = SRET[:, hp * 128:(hp + 1) * 128]
                    dve(lambda e, hp=hp, c=c, sst=sst: e.tensor_copy(
                        out=AV(SB_O, 0, 128, (hp * 4 + c) * 128, [[1, 128]], BF16), in_=sst), ["SRET"], ["SB"])
                    dve(lambda e, b=b, hp=hp, c=c, sst=sst: e.scalar_tensor_tensor(
                        out=sst, in0=sst, scalar=V(CTAB, 0, 128, CT_G64 + hp, [[1, 1]]), in1=b[:, c * 128:(c + 1) * 128],
                        op0=ALU.mult, op1=ALU.add), ["SRET", "CTAB", bt], ["SRET"])
            for h in range(8):
                hp, par = h // 2, h % 2
                b, bt = bank()
                for s in range(2):
                    pe(lambda e, b=b, h=h, s=s: e.matmul(
                        b[:, s * 128:(s + 1) * 128], lhsT=AV(VB_O, 0, 128, s * 1024 + h * 128, [[1, 128]], BF16),
                        rhs=AV(SMF_O, 0, 128, h * 256 + s * 128, [[1, 128]], BF16), start=(s == 0), stop=False),
                       ["VB", "SMF"], [bt])
                for c in range(4):
                    tk0 = (c // 2) * 128 + (c % 2) * 64
                    pe(lambda e, b=b, hp=hp, par=par, c=c, tk0=tk0: e.matmul(
                        b[:, c * 64:(c + 1) * 64], lhsT=AV(SB_O, 0, 128, (hp * 4 + c) * 128, [[1, 128]], BF16),
                        rhs=AV(QDT_OS[par], 0, 128, hp * TT + tk0, [[1, 64]], BF16), start=False, stop=(c == 3)),
                       ["SB", f"QDT{par}"], [bt])
                ob = (h % 2) * 512
                OF = AV(OF_O, 0, 128, ob, [[1, TT]])
                OFB = AV(OF_O, 0, 128, 2 * (ob + 256), [[1, TT]], BF16)
                OSQ = AV(OF_O, 0, 128, 2 * (ob + 384), [[1, TT]], BF16)
                otk = f"OF{h % 2}"
                act(lambda e, b=b, OF=OF: e.copy(out=OF, in_=b[:, 0:TT]), [bt], [otk])
                act(lambda e, b=b, OFB=OFB: e.copy(out=OFB, in_=b[:, 0:TT]), [bt], [otk])
                actf(OSQ, b[:, 0:TT], AF.Square, [bt], [otk])
                b2, bt2 = bank()
                pe(lambda e, b2=b2, OFB=OFB: e.matmul(b2[:, 0:TT], lhsT=ONESB[:], rhs=OFB, start=True, stop=True),
                   [otk, "ONESB"], [bt2])
                pe(lambda e, b2=b2, OSQ=OSQ: e.matmul(b2[:, 256:256 + TT], lhsT=ONESB[:], rhs=OSQ, start=True, stop=True),
                   [otk, "ONESB"], [bt2])
                sm = AV(ST_O, 0, 128, 0, [[1, TT]])
                sv = AV(ST_O, 0, 128, 256, [[1, TT]])
                sx = AV(ST_O, 0, 128, 512, [[1, TT]])
                actf(sm, b2[:, 0:TT], AF.Copy, [bt2], ["ST0", "RT0"], scale=1.0 / 128)
                actf(sv, b2[:, 0:TT], AF.Square, [bt2], ["ST1", "RT1"], scale=1.0 / 128)
                dve(lambda e, b2=b2, sv=sv: e.scalar_tensor_tensor(out=sv, in0=b2[:, 256:256 + TT], scalar=1.0 / 128, in1=sv,
                                                                   op0=ALU.mult, op1=ALU.subtract), [bt2, "ST1"], ["ST1"])
                ts(sv, sv, EPS, None, ALU.add, None, ["ST1"], ["ST1"])
                actf(sv, sv, AF.Sqrt, ["ST1"], ["ST1"])
                dve(lambda e, sv=sv: e.reciprocal(out=sv, in_=sv), ["ST1"], ["ST1"])
                tt(sx, OF, sm, ALU.subtract, [otk, "ST0"], ["ST2", "RT2"])
                tt(sx, sx, sv, ALU.mult, ["ST2", "ST1"], ["ST2"])
                tt(YC[:, (8 + h) * TT:(9 + h) * TT], sx, AV(SGR_O, 0, 128, h * TT, [[1, TT]], BF16), ALU.mult,
                   ["ST2", "SGR"], ["YC"])
            for ns in range(2):
                wa, wat = load_w(w_out_ab_d, 0, ns * 512)
                wb2, wbt = load_w(w_out_ab_d, 1024, ns * 512)
                for s in range(2):
                    b, bt = bank()
                    for kc in range(16):
                        w_, wt_ = (wa, wat) if kc < 8 else (wb2, wbt)
                        pe(lambda e, b=b, kc=kc, s=s, w_=w_: e.matmul(
                            b[:, 0:512], lhsT=YC[:, kc * TT + s * 128:kc * TT + s * 128 + 128],
                            rhs=V(w_, 0, 128, (kc % 8) * 512, [[1, 512]]), start=(kc == 0), stop=(kc == 15)),
                           ["YC", wt_], [bt])
                    xs_ = X[:, s * 1024 + ns * 512:s * 1024 + ns * 512 + 512]
                    tt(xs_, xs_, b[:, 0:512], ALU.add, [XT, bt], [XT])
            if debug:
                P.op("pool", lambda e, tok0=tok0, X=X: e.dma_start(
                    out=x1_d.ap()[tok0:tok0 + TT, :].rearrange("(s p) d -> p s d", p=128),
                    in_=V(X, 0, 128, 0, [[1024, 2], [1, 1024]])), r=[XT], w=["OUT"], dma=True)
            P.barrier()
            if stop == 2:
                break

            L1XS_O = 0
            SG1_O = 1024
            SIG_O = 2048
            VV_O = 2560
            VSQ_O = 4608
            LST_O = 5120
            Y1_O = 6144
            LT_O = 7168
            rms_to_ht(1, L1XS_O, X, XT)
            for ft in range(8):
                dve(lambda e, ft=ft: e.tensor_copy(out=U1[:, ft * 288 + 2:ft * 288 + 32],
                                                   in_=U1[:, ft * 288 + 258:ft * 288 + 288]), ["U1"], ["U1"])
            for half in range(2):
                wa, wat = load_w(w_in_c_d, 0, half * 512)
                wb2, wbt = load_w(w_in_c_d, 0, 1024 + half * 512)
                for f in range(4):
                    ft = half * 4 + f
                    ba, bat = proj_fm(wa, wat, f * 128, HT, "HT")
                    bb, bbt = proj_fm(wb2, wbt, f * 128, HT, "HT")
                    sg = AV(SIG_O, 0, 128, (ft % 2) * 256, [[1, TT]])
                    actf(sg, bb[:, 0:TT], AF.Sigmoid, [bbt], [f"SIG{ft % 2}"])
                    tt(U1[:, ft * 288 + 32:ft * 288 + 288], ba[:, 0:TT], sg, ALU.mult, [bat, f"SIG{ft % 2}"], ["U1"])
            for half in range(2):
                wb, wt = load_w(w_in_c_d, 0, 2048 + half * 512)
                for f in range(4):
                    ft = half * 4 + f
                    b, bt = proj_fm(wb, wt, f * 128, HT, "HT")
                    actf(AV(SG1_O, 0, 128, ft * TT, [[1, TT]], BF16), b[:, 0:TT], AF.Silu, [bt], ["SG1"])
            if t + 1 < NT:
                rms_to_ht(0, L1XS_O, XB[(t + 1) % 2], f"X{(t + 1) % 2}")
            dn = 0
            bsum, bsumt = PS[7], "ps7"
            for ft in range(8):
                b, bt = bank()
                di = wbn[0] % 3
                wbn[0] += 1
                P.op("sp", lambda e, di=di, ft=ft: e.dma_start(out=V(WB[di], 0, 128, 0, [[1, 31 * 128]]), in_=DG.ap()[ft]),
                     r=[f"DG{ft}"], w=[f"WB{di}"], dma=True, nobar=True)
                for k in range(31):
                    pe(lambda e, b=b, di=di, ft=ft, k=k: e.matmul(
                        b[:, 0:TT], lhsT=V(WB[di], 0, 128, k * 128, [[1, 128]]),
                        rhs=U1[:, ft * 288 + 2 + k:ft * 288 + 2 + k + TT],
                        start=(k == 0), stop=(k == 30)), [f"WB{di}", "U1"], [bt])
                vv = AV(VV_O, 0, 128, ft * TT, [[1, TT]])
                actf(vv, b[:, 0:TT], AF.Identity, [bt, "COLS"], ["VV"], bias=col(0, 32 + ft))
                vvb = AV(VSQ_O, 0, 128, 2 * ((ft % 2) * 256), [[1, TT]], BF16)
                vsq = AV(VSQ_O, 0, 128, 2 * ((ft % 2) * 256 + 128), [[1, TT]], BF16)
                dve(lambda e, vvb=vvb, vv=vv: e.tensor_copy(out=vvb, in_=vv), ["VV"], [f"VSQ{ft % 2}"])
                actf(vsq, vv, AF.Square, ["VV"], [f"VSQ{ft % 2}"])
                pe(lambda e, vvb=vvb, ft=ft: e.matmul(bsum[:, 0:TT], lhsT=ONESB[:], rhs=vvb,
                                                     start=(ft == 0), stop=False), [f"VSQ{ft % 2}", "ONESB"], [bsumt])
                pe(lambda e, vsq=vsq, ft=ft: e.matmul(bsum[:, 256:256 + TT], lhsT=ONESB[:], rhs=vsq,
                                                     start=False, stop=(ft == 7)), [f"VSQ{ft % 2}", "ONESB"], [bsumt])
            sm = AV(LST_O, 0, 128, 0, [[1, TT]])
            sv = AV(LST_O, 0, 128, 256, [[1, TT]])
            actf(sm, bsum[:, 0:TT], AF.Copy, [bsumt], ["LST0"], scale=1.0 / 1024)
            actf(sv, bsum[:, 0:TT], AF.Square, [bsumt], ["LST1"], scale=1.0 / 1024)
            dve(lambda e: e.scalar_tensor_tensor(out=sv, in0=bsum[:, 256:256 + TT], scalar=1.0 / 1024, in1=sv,
                                                 op0=ALU.mult, op1=ALU.subtract), [bsumt, "LST1"], ["LST1"])
            ts(sv, sv, EPS, None, ALU.add, None, ["LST1"], ["LST1"])
            actf(sv, sv, AF.Sqrt, ["LST1"], ["LST1"])
            dve(lambda e: e.reciprocal(out=sv, in_=sv), ["LST1"], ["LST1"])
            for ft in range(8):
                vv = AV(VV_O, 0, 128, ft * TT, [[1, TT]])
                lt = AV(LT_O, 0, 128, (ft % 2) * 256, [[1, TT]])
                ltk = f"LT{ft % 2}"
                tt(lt, vv, sm, ALU.subtract, ["VV", "LST0"], [ltk])
                tt(lt, lt, sv, ALU.mult, [ltk, "LST1"], [ltk])
                actf(lt, lt, AF.Silu, [ltk, "COLS"], [ltk], scale=col(0, 40 + ft), bias=col(0, 48 + ft))
                tt(AV(Y1_O, 0, 128, ft * TT, [[1, TT]], BF16), lt, AV(SG1_O, 0, 128, ft * TT, [[1, TT]], BF16), ALU.mult,
                   [ltk, "SG1"], [f"Y1_{ft}"])
            wcs = [load_w(w_out_c_d, 0, ns * 512) for ns in range(2)]
            obk = {(ns, s): bank() for ns in range(2) for s in range(2)}
            for kc in range(8):
                for ns in range(2):
                    wb, wt = wcs[ns]
                    for s in range(2):
                        b, bt = obk[(ns, s)]
                        pe(lambda e, b=b, kc=kc, s=s, wb=wb: e.matmul(
                            b[:, 0:512], lhsT=AV(Y1_O, 0, 128, kc * TT + s * 128, [[1, 128]], BF16),
                            rhs=V(wb, 0, 128, kc * 512, [[1, 512]]), start=(kc == 0), stop=(kc == 7)), [f"Y1_{kc}", wt], [bt])
            for ns in range(2):
                for s in range(2):
                    b, bt = obk[(ns, s)]
                    xs_ = X[:, s * 1024 + ns * 512:s * 1024 + ns * 512 + 512]
                    tt(xs_, xs_, b[:, 0:512], ALU.add, [XT, bt], [XT])
            prev_final[0] = (t, X, XT)
            P.barrier()

        if prev_final[0] is not None:
            final_norm(*prev_final[0], GYB_O, "GYB")
    except StopBuild:
        pass
    P.op("sp", lambda e: e.nop(), r=["OUT"])
    P.barrier()
    P.emit()
    P.close()
    return nc


def prep_inputs(inp, b, NT):
    T = NT * TT
    f = lambda a: np.ascontiguousarray(np.asarray(a, dtype=np.float32))
    return {
        "x": f(inp["x"][b, :T]),
        "norm_g": f(inp["norm_g"]).reshape(16, 128),
        "final_g": f(inp["final_g"]),
        "w_in_ab": f(inp["w_in_ab"][0]),
        "a_re": f(inp["s5_a_re"][0]).reshape(32, 128),
        "a_im": f(inp["s5_a_im"][0]).reshape(32, 128),
        "log_dt": f(inp["s5_log_dt"][0]).reshape(32, 2),
        "b_re": f(inp["s5_b_re"][0]).reshape(-1),
        "b_im": f(inp["s5_b_im"][0]).reshape(-1),
        "c_re": f(inp["s5_c_re"][0]).reshape(-1),
        "c_im": f(inp["s5_c_im"][0]).reshape(-1),
        "s5_d": f(inp["s5_d"][0]).reshape(8, 128),
        "glu_w": f(inp["s5_glu_w"][0]),
        "glu_b": f(inp["s5_glu_b"][0]).reshape(8, 128),
        "w_out_ab": f(inp["w_out_ab"][0]),
        "w_in_c": f(inp["w_in_c"][0]),
        "conv_w": f(inp["conv_w"][0]).reshape(248, 128),
        "conv_b": f(inp["conv_b"][0]).reshape(8, 128),
        "ln_g": f(inp["conv_ln_g"][0]).reshape(8, 128),
        "ln_b": f(inp["conv_ln_b"][0]).reshape(8, 128),
        "w_out_c": f(inp["w_out_c"][0]),
        "ctab": make_ctab(NT * 2),
    }


def kernel(**inputs):
    NT = 16
    nc = build(NT)
    in_maps = [prep_inputs(inputs, b, NT) for b in range(8)]
    res = run_bass_kernel_spmd(nc, in_maps, core_ids=list(range(8)))
    return np.stack([np.asarray(r["out"]) for r in res.results], axis=0).astype(np.float32)
```

```python
import math
import os
import numpy as np
import concourse.bass as bass
import concourse.mybir as mybir
from concourse.bass_utils import run_bass_kernel_spmd
from contextlib import ExitStack

F32 = mybir.dt.float32
BF16 = mybir.dt.bfloat16
AF = mybir.ActivationFunctionType
ALU = mybir.AluOpType
AX = mybir.AxisListType

TT = 256
EPS = 1e-6


class StopBuild(Exception):
    pass


class Prog:
    COMPUTE = ("pe", "act", "dve", "pool")
    RING = 8

    def __init__(self, nc):
        self.nc = nc
        self.ops = []
        self.lw = {}
        self.rd = {}
        self.ndma = {"sp": 0, "act": 0, "pool": 0}
        self.stack = ExitStack()
        self.last = {}
        self.pending_dma = []

    def sb(self, name, shape, dt):
        return self.stack.enter_context(self.nc.sbuf_tensor(name, list(shape), dt))

    def ps(self, name, shape, dt):
        return self.stack.enter_context(self.nc.psum_tensor(name, list(shape), dt))

    def op(self, eng, fn, r=(), w=(), dma=False, nobar=False, extra=()):
        i = len(self.ops)
        raw = set()
        oth = set()
        for t in r:
            if t in self.lw:
                raw.add(self.lw[t])
        for t in w:
            if t in self.lw:
                oth.add(self.lw[t])
            for _, j in self.rd.get(t, {}).items():
                oth.add(j)
        deps = set(extra)
        for j in raw | oth:
            oj = self.ops[j]
            if (not dma) and (not oj["dma"]) and oj["eng"] == eng and eng == "pe" and j not in raw:
                continue
            deps.add(j)
        o = dict(eng=eng, fn=fn, deps=deps, dma=dma, sig=False, val=None, slot=None)
        if dma:
            o["slot"] = self.ndma[eng] % self.RING
            self.ndma[eng] += 1
            if not nobar:
                self.pending_dma.append(i)
        else:
            self.last[eng] = i
        self.ops.append(o)
        key = ("dma", eng, i) if dma else eng
        for t in r:
            self.rd.setdefault(t, {})[key] = i
        for t in w:
            self.lw[t] = i
            self.rd[t] = {}
        return i

    def barrier(self):
        deps = set(self.last.values()) | set(self.pending_dma)
        self.pending_dma = []
        for e in ("pe", "act", "dve", "pool", "sp"):
            self.op(e, lambda g: g.nop(), extra=deps)

    def emit(self):
        nc = self.nc
        ops = self.ops
        for o in ops:
            for j in o["deps"]:
                ops[j]["sig"] = True
        cnt = {e: 0 for e in self.COMPUTE + ("sp",)}
        dcnt = {}
        for o in ops:
            if o["dma"]:
                k = (o["eng"], o["slot"])
                dcnt[k] = dcnt.get(k, 0) + 16
                o["val"] = dcnt[k]
            elif o["sig"]:
                cnt[o["eng"]] += 1
                o["val"] = cnt[o["eng"]]
        st = self.stack
        csem = {e: st.enter_context(nc.semaphore(f"s_{e}")) for e in self.COMPUTE + ("sp",)}
        dsem = {}
        for q in ("sp", "act", "pool"):
            for s in range(min(self.RING, self.ndma[q])):
                dsem[(q, s)] = st.enter_context(nc.semaphore(f"d_{q}{s}"))
        block = st.enter_context(nc.Block())
        per = {e: [] for e in ("pe", "act", "dve", "pool", "sp")}
        for i, o in enumerate(ops):
            per[o["eng"]].append(i)

        def semof(o):
            if o["dma"]:
                return dsem[(o["eng"], o["slot"])]
            return csem[o["eng"]]

        def run(e, eng):
            waited = {}
            for i in per[e]:
                o = ops[i]
                need = {}
                for j in o["deps"]:
                    oj = ops[j]
                    s = semof(oj)
                    k = id(s)
                    if waited.get(k, 0) >= oj["val"]:
                        continue
                    if k not in need or need[k][1] < oj["val"]:
                        need[k] = (s, oj["val"])
                if o["dma"]:
                    s = semof(o)
                    prev = o["val"] - 16
                    if prev > 0 and waited.get(id(s), 0) < prev:
                        if id(s) not in need or need[id(s)][1] < prev:
                            need[id(s)] = (s, prev)
                for k, (s, v) in need.items():
                    eng.wait_ge(s, v)
                    waited[k] = v
                ins = o["fn"](eng)
                if o["dma"]:
                    ins.then_inc(semof(o), 16)
                elif o["sig"]:
                    ins.then_inc(semof(o), 1)

        if per["pe"]:
            @block.tensor
            def _(eng):
                run("pe", eng)
        if per["act"]:
            @block.scalar
            def _(eng):
                run("act", eng)
        if per["dve"]:
            @block.vector
            def _(eng):
                run("dve", eng)
        if per["pool"]:
            @block.gpsimd
            def _(eng):
                run("pool", eng)
        if per["sp"]:
            @block.sync
            def _(eng):
                run("sp", eng)

    def close(self):
        self.stack.close()


def V(t, p0, pn, off, pat, dt=None):
    a = t[:] if dt is None else t[:].bitcast(dt)
    base = a[p0:p0 + pn, off:off + 1]
    return bass.AP(base.tensor, base.offset, [list(base.ap[0])] + [list(x) for x in pat])


CT_IDENT = 0
CT_M16 = 128
CT_ONES = 256
CT_MASK = 384
CT_KDEC = CT_MASK + 512
CT_QDEC = CT_KDEC + 8
CT_G64 = CT_QDEC + 8
CT_ROPE = CT_G64 + 8


def make_ctab(nsub):
    n = CT_ROPE + 2 * nsub * 32
    c = np.zeros((128, n), np.float64)
    c[:, CT_IDENT:CT_IDENT + 128] = np.eye(128)
    r = np.arange(128)
    c[:, CT_M16:CT_M16 + 128] = (r[:, None] // 16 == r[None, :] // 16)
    c[:, CT_ONES:CT_ONES + 128] = 1.0
    gam = 1.0 - 2.0 ** (-5.0 - np.arange(8))
    j = r % 64
    i = np.arange(64)
    m = gam[None, :, None] ** np.abs(i[None, None, :] - j[:, None, None]) * (64 ** -0.5)
    c[:, CT_MASK:CT_MASK + 512] = m.reshape(128, 512)
    c[:, CT_KDEC:CT_KDEC + 8] = gam[None, :] ** (63 - j[:, None])
    c[:, CT_QDEC:CT_QDEC + 8] = gam[None, :] ** (j[:, None] + 1.0) * (64 ** -0.5)
    par = r // 64
    for hp in range(4):
        c[:, CT_G64 + hp] = gam[2 * hp + par] ** 64
    freqs = 10000.0 ** (-np.arange(32) / 32.0)
    pos = (np.arange(nsub)[None, :] * 128 + r[:, None]).astype(np.float64)
    ang = pos[:, :, None] * freqs[None, None, :]
    c[:, CT_ROPE:CT_ROPE + nsub * 32] = np.cos(ang).reshape(128, -1)
    c[:, CT_ROPE + nsub * 32:CT_ROPE + 2 * nsub * 32] = np.sin(ang).reshape(128, -1)
    return c.astype(np.float32)


def build(NT, debug=False, stop=99):
    nc = bass.Bass("TRN2", target_bir_lowering=False)
    P = Prog(nc)
    T = NT * TT
    NSUB = NT * 2
    NCT = CT_ROPE + 2 * NSUB * 32

    def din(name, shape):
        return nc.dram_tensor(name, list(shape), F32, kind="ExternalInput")

    x_d = din("x", [T, 1024])
    norm_g_d = din("norm_g", [16, 128])
    final_g_d = din("final_g", [1024])
    w_in_ab_d = din("w_in_ab", [1024, 5120])
    a_re_d = din("a_re", [32, 128])
    a_im_d = din("a_im", [32, 128])
    log_dt_d = din("log_dt", [32, 2])
    b_re_d = din("b_re", [64 * 64 * 16])
    b_im_d = din("b_im", [64 * 64 * 16])
    c_re_d = din("c_re", [64 * 16 * 64])
    c_im_d = din("c_im", [64 * 16 * 64])
    s5_d_d = din("s5_d", [8, 128])
    glu_w_d = din("glu_w", [1024, 1024])
    glu_b_d = din("glu_b", [8, 128])
    w_out_ab_d = din("w_out_ab", [2048, 1024])
    w_in_c_d = din("w_in_c", [1024, 3072])
    conv_w_d = din("conv_w", [248, 128])
    conv_b_d = din("conv_b", [8, 128])
    ln_g_d = din("ln_g", [8, 128])
    ln_b_d = din("ln_b", [8, 128])
    w_out_c_d = din("w_out_c", [1024, 1024])
    ctab_d = din("ctab", [128, NCT])
    out_d = nc.dram_tensor("out", [T, 1024], F32, kind="ExternalOutput")
    DG = nc.dram_tensor("dg_conv", [8, 128, 31 * 128], BF16, kind="Internal")
    WBF = {}
    for nm_, d_ in (("w_in_ab", w_in_ab_d), ("glu_w", glu_w_d), ("w_out_ab", w_out_ab_d),
                    ("w_in_c", w_in_c_d), ("w_out_c", w_out_c_d)):
        WBF[nm_] = nc.dram_tensor("bf_" + nm_, list(d_.shape), BF16, kind="Internal")
    if debug:
        x1_d = nc.dram_tensor("x1", [T, 1024], F32, kind="ExternalOutput")

    CTAB = P.sb("CTAB", [128, CT_ROPE], F32)
    ROPE = P.sb("ROPE", [128, 256], F32)
    IDB = P.sb("IDB", [128, 128], BF16)
    ONESB = P.sb("ONESB", [128, 128], BF16)
    COLS = P.sb("COLS", [128, 384], F32)
    FG = P.sb("FG", [128, 1024], F32)
    KW = P.sb("KW", [128, 64 * 128], BF16)
    WIN = P.sb("WIN", [128, 128 * 128], BF16)
    WOUT = P.sb("WOUT", [128, 512 * 32], BF16)
    TCS = P.sb("TCS", [128, 2 * 1024], F32)
    RHOT = P.sb("RHOT", [128, 1024], F32)
    RHO = P.sb("RHO", [128, 32], F32)
    W0 = P.sb("W0", [128, 64], F32)
    SRET = P.sb("SRET", [128, 512], F32)
    XB = [P.sb(f"X{i}", [128, 2 * 1024], F32) for i in range(2)]
    HT = P.sb("HT", [128, 8 * TT], BF16)
    WB = [P.sb(f"WB{i}", [128, 8 * 512], BF16) for i in range(3)]
    YC = P.sb("YC", [128, 16 * TT], BF16)
    U1 = P.sb("U1", [128, 8 * 288], BF16)
    SMALL = P.sb("SMALL", [128, 64], F32)
    DIAG = [P.sb(f"DIAG{i}", [128, 128], BF16) for i in range(4)]
    ARENA_W = 11520
    AR = P.sb("ARENA", [128, ARENA_W], F32)
    PS = [P.ps(f"ps{i}", [128, 512], F32) for i in range(8)]
    psn = [0]

    def bank():
        i = psn[0] % 7
        psn[0] += 1
        return PS[i], f"ps{i}"

    def ar(off_words, dt=F32):
        return off_words * (2 if dt == BF16 else 1)

    def AV(off_words, p0, pn, eoff, pat, dt=F32):
        if off_words >= 200000:
            return V(WB[1], p0, pn, (off_words - 200000) + eoff, pat, F32)
        if off_words >= 100000:
            return V(WB[0], p0, pn, (off_words - 100000) + eoff, pat, F32)
        return V(AR, p0, pn, ar(off_words, dt) + eoff, pat, dt if dt != F32 else None)

    def ct(off, n, p0=0, pn=128):
        return V(CTAB, p0, pn, off, [[1, n]])

    def col(chunk, c, pn=128):
        return V(COLS, 0, pn, chunk * 128 + c, [[1, 1]])

    dve = lambda fn, r, w: P.op("dve", fn, r, w)
    act = lambda fn, r, w: P.op("act", fn, r, w)
    pe = lambda fn, r, w: P.op("pe", fn, r, w)
    pool = lambda fn, r, w: P.op("pool", fn, r, w)

    def ptt(out, in0, in1, op, r, w):
        pool(lambda e: e.tensor_tensor(out=out, in0=in0, in1=in1, op=op), r, w)

    def tt(out, in0, in1, op, r, w):
        dve(lambda e: e.tensor_tensor(out=out, in0=in0, in1=in1, op=op), r, w)

    def ts(out, in0, s1, s2, op0, op1, r, w):
        if op1 is None:
            dve(lambda e: e.tensor_scalar(out=out, in0=in0, scalar1=s1, scalar2=None, op0=op0), r, w)
        else:
            dve(lambda e: e.tensor_scalar(out=out, in0=in0, scalar1=s1, scalar2=s2, op0=op0, op1=op1), r, w)

    def actf(out, in_, func, r, w, scale=None, bias=None, accum=None):
        kw = {}
        if scale is not None:
            kw["scale"] = scale
        if bias is not None:
            kw["bias"] = bias
        if accum is not None:
            kw["accum_out"] = accum
        act(lambda e: e.activation(out=out, in_=in_, func=func, **kw), r, w)

    conv_order = []
    for c0 in range(0, 5120, 512):
        conv_order.append(("w_in_ab", 0, c0))
    for c0 in (0, 512):
        conv_order.append(("glu_w", 0, c0))
    for c0 in (0, 512):
        conv_order.append(("w_out_ab", 0, c0))
        conv_order.append(("w_out_ab", 1024, c0))
    for c0 in range(0, 3072, 512):
        conv_order.append(("w_in_c", 0, c0))
    for c0 in (0, 512):
        conv_order.append(("w_out_c", 0, c0))
    SRCW = {"w_in_ab": w_in_ab_d, "glu_w": glu_w_d, "w_out_ab": w_out_ab_d, "w_in_c": w_in_c_d, "w_out_c": w_out_c_d}
    for (nm_, r0, c0) in conv_order:
        P.op("pool", lambda e, nm_=nm_, r0=r0, c0=c0: e.dma_start(
            out=WBF[nm_].ap()[r0:r0 + 1024, c0:c0 + 512], in_=SRCW[nm_].ap()[r0:r0 + 1024, c0:c0 + 512]),
            w=[f"CV_{nm_}_{r0}_{c0}"], dma=True, nobar=True)
    P.op("sp", lambda e: e.dma_start(out=CTAB[:], in_=ctab_d.ap()[:, 0:CT_ROPE]), w=["CTAB"], dma=True)
    dve(lambda e: e.tensor_copy(out=IDB[:], in_=ct(CT_IDENT, 128)), ["CTAB"], ["IDB"])
    dve(lambda e: e.tensor_copy(out=ONESB[:], in_=ct(CT_ONES, 128)), ["CTAB"], ["ONESB"])
    P.op("sp", lambda e: e.dma_start(out=FG[:], in_=bass.AP(final_g_d, 0, [[0, 128], [1, 1024]])), w=["FG"], dma=True)
    dve(lambda e: e.memset(W0[:], 0.0), [], ["W0"])
    dve(lambda e: e.memset(SRET[:], 0.0), [], ["SRET"])
    dve(lambda e: e.memset(U1[:], 0.0), [], ["U1"])
    dve(lambda e: e.memset(SMALL[:], 0.0), [], ["SMALL"])
    dve(lambda e: e.memset(SMALL[:, 60:61], EPS), ["SMALL"], ["SMALL"])
    dve(lambda e: e.memset(SMALL[:, 61:62], math.pi / 2), ["SMALL"], ["SMALL"])
    EPSC = SMALL[:, 60:61]
    HPIC = SMALL[:, 61:62]

    STG_O = 0
    dve(lambda e: e.memset(AV(STG_O, 0, 128, 0, [[1, 384]]), 0.0), [], ["STG"])
    pieces = [(norm_g_d, 16, 0, 0), (s5_d_d, 8, 0, 16), (glu_b_d, 8, 0, 24), (conv_b_d, 8, 0, 32),
              (ln_g_d, 8, 0, 40), (ln_b_d, 8, 0, 48)]
    for (d, n, ch, r0) in pieces:
        P.op("sp", lambda e, d=d, n=n, ch=ch, r0=r0: e.dma_start(
            out=AV(STG_O, r0, n, ch * 128, [[1, 128]]), in_=d.ap()), r=["STG"], w=["STG"], dma=True)
    P.op("sp", lambda e: e.dma_start(out=AV(STG_O, 0, 128, 128, [[1, 128]]), in_=conv_w_d.ap()[0:128, :]),
         r=["STG"], w=["STG"], dma=True)
    P.op("sp", lambda e: e.dma_start(out=AV(STG_O, 0, 120, 256, [[1, 128]]), in_=conv_w_d.ap()[128:248, :]),
         r=["STG"], w=["STG"], dma=True)
    for ch in range(3):
        b, bt = bank()
        pe(lambda e, b=b, ch=ch: e.transpose(out=b[:, 0:128], in_=AV(STG_O, 0, 128, ch * 128, [[1, 128]]),
                                             identity=ct(CT_IDENT, 128)), ["STG", "CTAB"], [bt])
        act(lambda e, b=b, ch=ch: e.copy(out=COLS[:, ch * 128:(ch + 1) * 128], in_=b[:, 0:128]), [bt], ["COLS"])

    def cw_col(k, c):
        row = k * 8 + c
        return col(1 + row // 128, row % 128)

    DGS_O = 8800
    for ft in range(8):
        for k in range(31):
            ts(AV(DGS_O, 0, 128, k * 128, [[1, 128]], BF16), IDB[:], cw_col(k, ft), None, ALU.mult, None,
               ["IDB", "COLS", "DGS"], ["DGS"])
        P.op("sp", lambda e, ft=ft: e.dma_start(out=DG.ap()[ft], in_=AV(DGS_O, 0, 128, 0, [[1, 31 * 128]], BF16)),
             r=["DGS"], w=[f"DG{ft}"], dma=True)

    o = 384
    PRM_O = o; o += 384
    LDT_O = o; o += 2
    PT_O = o; o += 96
    DT_O = o; o += 32
    ADT_O = o; o += 32
    ANG_O = o; o += 32
    SC_O = o; o += 64
    T1_O = o; o += 32
    T2_O = o; o += 32
    UPC_O = o; o += 288
    UPS_O = o; o += 288
    MG_O = o; o += 288
    LPR_O = o; o += 288
    LPI_O = o; o += 288
    CF_O = o; o += 64
    BR_O = o; o += 512
    BI_O = o; o += 512
    E0R_O = o; o += 512
    E0I_O = o; o += 512
    CIN_O = o; o += 1024
    CR_O = o; o += 512
    CI_O = o; o += 512
    ER_O = o; o += 512
    EI_O = o; o += 512
    EBR_O = 100000
    EBI_O = 101024
    CBR_O = 200000
    CBI_O = 201024
    TA_O = o; o += 512
    TB_O = o; o += 512
    assert o <= ARENA_W, o

    def a32(off, n=32, eoff=0, p0=0, pn=128):
        return AV(off, p0, pn, eoff, [[1, n]])

    P.op("sp", lambda e: e.dma_start(out=AV(PRM_O, 0, 32, 0, [[1, 128]]), in_=a_re_d.ap()), w=["PRM"], dma=True)
    P.op("sp", lambda e: e.dma_start(out=AV(PRM_O, 0, 32, 128, [[1, 128]]), in_=a_im_d.ap()), w=["PRM"], dma=True)
    P.op("sp", lambda e: e.dma_start(out=AV(LDT_O, 0, 32, 0, [[1, 2]]), in_=log_dt_d.ap()), w=["LDT"], dma=True)
    dve(lambda e: e.tensor_copy(out=AV(PRM_O, 0, 32, 256, [[64, 2], [1, 64]]),
                                in_=AV(LDT_O, 0, 32, 0, [[1, 2], [0, 64]])), ["LDT", "PRM"], ["PRM"])
    b, bt = bank()
    for k in range(3):
        pe(lambda e, b=b, k=k: e.transpose(out=b[:, k * 32:(k + 1) * 32], in_=AV(PRM_O, 0, 32, k * 128, [[1, 128]]),
                                           identity=V(CTAB, 0, 32, CT_IDENT, [[1, 32]])), ["PRM", "CTAB"], [bt])
    act(lambda e, b=b: e.copy(out=a32(PT_O, 96), in_=b[:, 0:96]), [bt], ["PT"])
    ARE = a32(PT_O, 32, 0)
    AIM = a32(PT_O, 32, 32)
    actf(a32(DT_O), a32(PT_O, 32, 64), AF.Exp, ["PT"], ["DT"])
    tt(a32(ADT_O), ARE, a32(DT_O), ALU.mult, ["PT", "DT"], ["ADT"])
    tt(a32(ANG_O), AIM, a32(DT_O), ALU.mult, ["PT", "DT"], ["ANG"])
    actf(a32(SC_O, 32, 0), a32(ANG_O), AF.Sin, ["ANG"], ["SC"], scale=1.0 / 16)
    actf(a32(SC_O, 32, 32), a32(ANG_O), AF.Sin, ["ANG", "SMALL"], ["SC"], scale=1.0 / 16, bias=HPIC)
    for _ in range(4):
        tt(a32(T1_O), a32(SC_O, 32, 0), a32(SC_O, 32, 32), ALU.mult, ["SC"], ["T1"])
        tt(a32(T2_O), a32(SC_O, 32, 0), a32(SC_O, 32, 0), ALU.mult, ["SC"], ["T2"])
        ts(a32(SC_O, 32, 0), a32(T1_O), 2.0, None, ALU.mult, None, ["T1"], ["SC"])
        ts(a32(SC_O, 32, 32), a32(T2_O), -2.0, 1.0, ALU.mult, ALU.add, ["T2"], ["SC"])
    dve(lambda e: e.memset(a32(UPC_O, 32, 0), 1.0), [], ["UPC"])
    dve(lambda e: e.memset(a32(UPS_O, 32, 0), 0.0), [], ["UPS"])
    dve(lambda e: e.tensor_copy(out=a32(UPC_O, 32, 32), in_=a32(SC_O, 32, 32)), ["SC"], ["UPC"])
    dve(lambda e: e.tensor_copy(out=a32(UPS_O, 32, 32), in_=a32(SC_O, 32, 0)), ["SC"], ["UPS"])
    C1 = a32(SC_O, 32, 32)
    S1 = a32(SC_O, 32, 0)
    for e_ in range(1, 8):
        ce = a32(UPC_O, 32, 32 * e_)
        se = a32(UPS_O, 32, 32 * e_)
        tt(a32(T1_O), ce, C1, ALU.mult, ["UPC", "SC"], ["T1"])
        tt(a32(T2_O), se, S1, ALU.mult, ["UPS", "SC"], ["T2"])
        tt(a32(UPC_O, 32, 32 * (e_ + 1)), a32(T1_O), a32(T2_O), ALU.subtract, ["T1", "T2"], ["UPC"])
        tt(a32(T1_O), se, C1, ALU.mult, ["UPS", "SC"], ["T1"])
        tt(a32(T2_O), ce, S1, ALU.mult, ["UPC", "SC"], ["T2"])
        tt(a32(UPS_O, 32, 32 * (e_ + 1)), a32(T1_O), a32(T2_O), ALU.add, ["T1", "T2"], ["UPS"])
    for e_ in range(9):
        actf(a32(MG_O, 32, 32 * e_), a32(ADT_O), AF.Exp, ["ADT"], ["MG"], scale=float(e_))
    tt(a32(LPR_O, 288), a32(MG_O, 288), a32(UPC_O, 288), ALU.mult, ["MG", "UPC"], ["LPR"])
    tt(a32(LPI_O, 288), a32(MG_O, 288), a32(UPS_O, 288), ALU.mult, ["MG", "UPS"], ["LPI"])
    dve(lambda e: e.tensor_copy(out=RHO[:], in_=a32(MG_O, 32, 256)), ["MG"], ["RHO"])
    dve(lambda e: e.tensor_copy(out=V(RHOT, 0, 128, 0, [[32, 32], [1, 32]]),
                                in_=V(RHO, 0, 128, 0, [[1, 32], [0, 32]])), ["RHO"], ["RHOT"])
    dve(lambda e: e.memset(V(RHOT, 0, 128, 0, [[32, 32], [1, 1]]), 0.0), ["RHOT"], ["RHOT"])
    TC = lambda lo, n: V(TCS, 0, 128, lo, [[32, 32], [1, n]])
    TS_ = lambda lo, n: V(TCS, 0, 128, 1024 + lo, [[32, 32], [1, n]])
    dve(lambda e: e.tensor_copy(out=TC(0, 1), in_=AV(UPC_O, 0, 128, 256, [[1, 32], [1, 1]])), ["UPC"], ["TCS"])
    dve(lambda e: e.tensor_copy(out=TS_(0, 1), in_=AV(UPS_O, 0, 128, 256, [[1, 32], [1, 1]])), ["UPS"], ["TCS"])
    n_ = 1
    while n_ < 32:
        pc = V(TCS, 0, 128, n_ - 1, [[32, 32], [0, n_]])
        psn_ = V(TCS, 0, 128, 1024 + n_ - 1, [[32, 32], [0, n_]])
        ta = AV(TA_O, 0, 128, 0, [[n_, 32], [1, n_]])
        tb = AV(TB_O, 0, 128, 0, [[n_, 32], [1, n_]])
        tt(ta, TC(0, n_), pc, ALU.mult, ["TCS"], ["TA"])
        tt(tb, TS_(0, n_), psn_, ALU.mult, ["TCS"], ["TB"])
        tt(TC(n_, n_), ta, tb, ALU.subtract, ["TA", "TB", "TCS"], ["TCS"])
        tt(ta, TC(0, n_), psn_, ALU.mult, ["TCS"], ["TA"])
        tt(tb, TS_(0, n_), pc, ALU.mult, ["TCS"], ["TB"])
        tt(TS_(n_, n_), ta, tb, ALU.add, ["TA", "TB", "TCS"], ["TCS"])
        n_ *= 2
    LR1 = a32(LPR_O, 32, 32)
    LI1 = a32(LPI_O, 32, 32)
    ts(a32(T1_O), LR1, -1.0, None, ALU.add, None, ["LPR"], ["T1"])
    tt(a32(T2_O), ARE, ARE, ALU.mult, ["PT"], ["T2"])
    tt(a32(DT_O), AIM, AIM, ALU.mult, ["PT"], ["DT"])
    tt(a32(T2_O), a32(T2_O), a32(DT_O), ALU.add, ["T2", "DT"], ["T2"])
    dve(lambda e: e.reciprocal(out=a32(T2_O), in_=a32(T2_O)), ["T2"], ["T2"])
    tt(a32(DT_O), a32(T1_O), ARE, ALU.mult, ["T1", "PT"], ["DT"])
    tt(a32(ANG_O), LI1, AIM, ALU.mult, ["LPI", "PT"], ["ANG"])
    tt(a32(DT_O), a32(DT_O), a32(ANG_O), ALU.add, ["DT", "ANG"], ["DT"])
    tt(a32(CF_O, 32, 0), a32(DT_O), a32(T2_O), ALU.mult, ["DT", "T2"], ["CF"])
    tt(a32(DT_O), LI1, ARE, ALU.mult, ["LPI", "PT"], ["DT"])
    tt(a32(ANG_O), a32(T1_O), AIM, ALU.mult, ["T1", "PT"], ["ANG"])
    tt(a32(DT_O), a32(DT_O), a32(ANG_O), ALU.subtract, ["DT", "ANG"], ["DT"])
    tt(a32(CF_O, 32, 32), a32(DT_O), a32(T2_O), ALU.mult, ["DT", "T2"], ["CF"])
    P.op("sp", lambda e: e.dma_start(out=AV(BR_O, 0, 128, 0, [[16, 32], [1, 16]]),
                                     in_=bass.AP(b_re_d, 0, [[16, 128], [2048, 32], [1, 16]])), w=["BR"], dma=True)
    P.op("sp", lambda e: e.dma_start(out=AV(BI_O, 0, 128, 0, [[16, 32], [1, 16]]),
                                     in_=bass.AP(b_im_d, 0, [[16, 128], [2048, 32], [1, 16]])), w=["BI"], dma=True)
    for ri, cd in enumerate((c_re_d, c_im_d)):
        for pl in range(8):
            for i4 in range(4):
                P.op("sp", lambda e, ri=ri, cd=cd, pl=pl, i4=i4: e.dma_start(
                    out=AV(CIN_O, pl * 16, 16, ri * 512 + i4 * 128, [[64, 2], [1, 64]]),
                    in_=bass.AP(cd, pl * 2048 + i4 * 16384, [[64, 16], [1024, 2], [1, 64]])), w=["CIN"], dma=True)
    for ri, co in enumerate((CR_O, CI_O)):
        b, bt = bank()
        for i4 in range(4):
            pe(lambda e, b=b, ri=ri, i4=i4: e.transpose(
                out=b[:, i4 * 128:(i4 + 1) * 128], in_=AV(CIN_O, 0, 128, ri * 512 + i4 * 128, [[1, 128]]),
                identity=ct(CT_IDENT, 128)), ["CIN", "CTAB"], [bt])
        act(lambda e, b=b, co=co: e.copy(out=a32(co, 512), in_=b[:, 0:512]), [bt], ["CR" if ri == 0 else "CI"])

    def bc16(off, eoff):
        return AV(off, 0, 128, eoff, [[1, 32], [0, 16]])

    def v512(off):
        return AV(off, 0, 128, 0, [[16, 32], [1, 16]])

    def cmul(outr, outi, ar_, ai_, xr, xi, rtoks, wr, wi, neg_im=False):
        tt(v512(TA_O), ar_, xr, ALU.mult, rtoks, ["TA"])
        tt(v512(TB_O), ai_, xi, ALU.mult, rtoks, ["TB"])
        tt(outr, v512(TA_O), v512(TB_O), ALU.subtract, ["TA", "TB"], [wr])
        tt(v512(TA_O), ar_, xi, ALU.mult, rtoks, ["TA"])
        tt(v512(TB_O), ai_, xr, ALU.mult, rtoks, ["TB"])
        if neg_im:
            dve(lambda e: e.scalar_tensor_tensor(out=outi, in0=v512(TA_O), scalar=-1.0, in1=v512(TB_O),
                                                 op0=ALU.mult, op1=ALU.subtract), ["TA", "TB"], [wi])
        else:
            tt(outi, v512(TA_O), v512(TB_O), ALU.add, ["TA", "TB"], [wi])

    cmul(v512(E0R_O), v512(E0I_O), bc16(CF_O, 0), bc16(CF_O, 32), v512(BR_O), v512(BI_O),
         ["CF", "BR", "BI"], "E0R", "E0I")
    dve(lambda e: e.memset(a32(EBR_O, 1024), 0.0), [], ["EBR"])
    dve(lambda e: e.memset(a32(EBI_O, 1024), 0.0), [], ["EBI"])
    dve(lambda e: e.memset(a32(CBR_O, 1024), 0.0), [], ["CBR"])
    dve(lambda e: e.memset(a32(CBI_O, 1024), 0.0), [], ["CBI"])

    def expand(dst_o, src_o, rt, wt):
        for gp in range(2):
            dve(lambda e, gp=gp: e.tensor_copy(
                out=AV(dst_o, gp * 64, 64, gp * 16, [[32, 32], [1, 16]]),
                in_=AV(src_o, gp * 64, 64, 0, [[16, 32], [1, 16]])), [rt, wt], [wt])

    for r_ in range(8):
        cmul(v512(ER_O), v512(EI_O), bc16(LPR_O, 32 * (r_ + 1)), bc16(LPI_O, 32 * (r_ + 1)), v512(CR_O), v512(CI_O),
             ["LPR", "LPI", "CR", "CI"], "ER", "EI", neg_im=True)
        expand(CBR_O, ER_O, "ER", "CBR")
        expand(CBI_O, EI_O, "EI", "CBI")
        for ri, so in enumerate((CBR_O, CBI_O)):
            act(lambda e, r_=r_, ri=ri, so=so: e.copy(
                out=V(WOUT, 0, 128, (r_ * 2 + ri) * 32, [[512, 32], [1, 32]]),
                in_=AV(so, 0, 128, 0, [[32, 32], [1, 32]])), ["CBR" if ri == 0 else "CBI"], ["WOUT"])
    dve(lambda e: e.tensor_copy(out=v512(ER_O), in_=v512(CR_O)), ["CR", "ER"], ["ER"])
    ts(v512(EI_O), v512(CI_O), -1.0, None, ALU.mult, None, ["CI", "EI"], ["EI"])
    expand(CBR_O, ER_O, "ER", "CBR")
    expand(CBI_O, EI_O, "EI", "CBI")
    for tau in range(8):
        cmul(v512(ER_O), v512(EI_O), bc16(LPR_O, 32 * tau), bc16(LPI_O, 32 * tau), v512(E0R_O), v512(E0I_O),
             ["LPR", "LPI", "E0R", "E0I"], "ER", "EI")
        expand(EBR_O, ER_O, "ER", "EBR")
        expand(EBI_O, EI_O, "EI", "EBI")
        s_ = 7 - tau
        for ft in range(8):
            b, bt = bank()
            for ri, so in enumerate((EBR_O, EBI_O)):
                pe(lambda e, b=b, ri=ri, so=so, ft=ft: e.transpose(
                    out=b[:, ri * 128:(ri + 1) * 128], in_=AV(so, 0, 128, ft * 128, [[1, 128]]),
                    identity=ct(CT_IDENT, 128)), ["EBR" if ri == 0 else "EBI", "CTAB"], [bt])
            act(lambda e, b=b, ft=ft, s_=s_: e.copy(
                out=V(WIN, 0, 128, ((ft * 8 + s_) * 2) * 128, [[1, 256]]), in_=b[:, 0:256]), [bt], ["WIN"])
            pe(lambda e, b=b, ft=ft: e.matmul(b[:, 256:384], lhsT=AV(EBR_O, 0, 128, ft * 128, [[1, 128]]),
                                             rhs=AV(CBR_O, 0, 128, ft * 128, [[1, 128]]), start=True, stop=False),
               ["EBR", "CBR"], [bt])
            pe(lambda e, b=b, ft=ft: e.matmul(b[:, 256:384], lhsT=AV(EBI_O, 0, 128, ft * 128, [[1, 128]]),
                                             rhs=AV(CBI_O, 0, 128, ft * 128, [[1, 128]]), start=False, stop=True),
               ["EBI", "CBI"], [bt])
            kdst = V(KW, 0, 128, (ft * 8 + tau) * 128, [[1, 128]])
            if tau == 0:
                tt(AV(TA_O, 0, 128, 0, [[1, 128]]), b[:, 256:384], ct(CT_M16, 128), ALU.mult, [bt, "CTAB"], ["TA"])
                dve(lambda e, kdst=kdst, ft=ft: e.scalar_tensor_tensor(
                    out=kdst, in0=ct(CT_IDENT, 128), scalar=col(0, 16 + ft), in1=AV(TA_O, 0, 128, 0, [[1, 128]]),
                    op0=ALU.mult, op1=ALU.add), ["TA", "CTAB", "COLS"], ["KW"])
            else:
                tt(kdst, b[:, 256:384], ct(CT_M16, 128), ALU.mult, [bt, "CTAB"], ["KW"])
    P.barrier()
    pe(lambda e: e.matmul(PS[0][0:32, 0:32], lhsT=IDB[:, 0:32], rhs=IDB[:, 0:32], start=True, stop=True), ["IDB"], ["ps0"])
    P.barrier()
    if stop == 0:
        NT = 0

    wbn = [0]

    def load_w(dram, r0, c0):
        i = wbn[0] % 3
        wbn[0] += 1
        nm_ = [k for k, v in SRCW.items() if v is dram][0]
        src = WBF[nm_].ap()[r0:r0 + 1024, c0:c0 + 512].rearrange("(c p) f -> p c f", p=128)
        P.op("sp", lambda e, i=i, src=src: e.dma_start(out=V(WB[i], 0, 128, 0, [[512, 8], [1, 512]]), in_=src),
             r=[f"CV_{nm_}_{r0}_{c0}"], w=[f"WB{i}"], dma=True, nobar=True)
        return WB[i], f"WB{i}"

    def rms_to_ht(gl, XS_O, X, XT):
        import os
        ksub = int(os.environ.get("KSUB", "9"))
        if ksub < 1:
            return
        for s in range(2):
            actf(AV(XS_O, 0, 128, 0, [[1, 1024]], BF16), X[:, s * 1024:(s + 1) * 1024], AF.Square,
                 [XT], ["XS", "SMALL"], accum=SMALL[:, s:s + 1])
        if ksub < 2:
            return
        ts(SMALL[:, 2:4], SMALL[:, 0:2], 1.0 / 1024, EPS, ALU.mult, ALU.add, ["SMALL"], ["SMALL"])
        actf(SMALL[:, 2:4], SMALL[:, 2:4], AF.Sqrt, ["SMALL"], ["SMALL"])
        dve(lambda e: e.reciprocal(out=SMALL[:, 4:6], in_=SMALL[:, 2:4]), ["SMALL"], ["SMALL"])
        if ksub < 3:
            return
        for s in range(2):
            ts(AV(XS_O, 0, 128, s * 1024, [[1, 1024]], BF16), X[:, s * 1024:(s + 1) * 1024], SMALL[:, 4 + s:5 + s],
               None, ALU.mult, None, [XT, "SMALL"], ["XS"])
        if ksub < 4:
            return
        for c in range(8):
            b, bt = bank()
            for s in range(2):
                pe(lambda e, b=b, c=c, s=s: e.transpose(
                    out=V(b, 0, 128, s * 128, [[1, 128]], BF16),
                    in_=AV(XS_O, 0, 128, s * 1024 + c * 128, [[1, 128]], BF16), identity=IDB[:]), ["XS", "IDB"], [bt])
            ts(HT[:, c * TT:(c + 1) * TT], V(b, 0, 128, 0, [[1, TT]], BF16), col(0, gl * 8 + c), None, ALU.mult, None,
               [bt, "COLS"], ["HT"])

    def proj_fm(wb, wt, cc, rhs_t, rhs_tok, rhs_fn=None):
        b, bt = bank()
        for c in range(8):
            rhs = rhs_fn(c) if rhs_fn is not None else rhs_t[:, c * TT:(c + 1) * TT]
            pe(lambda e, b=b, c=c, rhs=rhs: e.matmul(b[:, 0:TT], lhsT=V(wb, 0, 128, c * 512 + cc, [[1, 128]]),
                                                     rhs=rhs, start=(c == 0), stop=(c == 7)),
               [wt, rhs_tok], [bt])
        return b, bt

    def proj_tm(wb, wt, s):
        b, bt = bank()
        for c in range(8):
            pe(lambda e, b=b, c=c: e.matmul(b[:, 0:512], lhsT=HT[:, c * TT + s * 128:c * TT + s * 128 + 128],
                                            rhs=V(wb, 0, 128, c * 512, [[1, 512]]), start=(c == 0), stop=(c == 7)),
               [wt, "HT"], [bt])
        return b, bt

    XS_O = 0
    ZT1_O = 0; ZT2_O = 1024; ZR_O = 2048; ZI_O = 3072; ZPR_O = 4096; ZPI_O = 5120
    U0_O = 6144
    SG0_O = 7168
    SAL_O = 8192
    GYB_O = 9280
    ZS_O = 10304
    assert ZS_O + 1024 <= ARENA_W

    def chk(k):
        if stop == k:
            P.barrier()
            raise StopBuild()

    try:
        def load_x(tt_):
            P.op("pool", lambda e, tt_=tt_: e.dma_start(
                out=V(XB[tt_ % 2], 0, 128, 0, [[1024, 2], [1, 1024]]),
                in_=x_d.ap()[tt_ * TT:(tt_ + 1) * TT, :].rearrange("(s p) d -> p s d", p=128)),
                w=[f"X{tt_ % 2}"], dma=True, nobar=True)

        prev_final = [None]
        pre_u = []

        def final_norm(tf, X, XT, junk_o, junk_tok):
            for s in range(2):
                actf(AV(junk_o, 0, 128, 0, [[1, 1024]], BF16), X[:, s * 1024:(s + 1) * 1024], AF.Square,
                     [XT], [junk_tok, "SMALL"], accum=SMALL[:, 8 + s:9 + s])
            ts(SMALL[:, 10:12], SMALL[:, 8:10], 1.0 / 1024, EPS, ALU.mult, ALU.add, ["SMALL"], ["SMALL"])
            actf(SMALL[:, 10:12], SMALL[:, 10:12], AF.Sqrt, ["SMALL"], ["SMALL"])
            dve(lambda e: e.reciprocal(out=SMALL[:, 12:14], in_=SMALL[:, 10:12]), ["SMALL"], ["SMALL"])
            for s in range(2):
                xs_ = X[:, s * 1024:(s + 1) * 1024]
                dve(lambda e, xs_=xs_, s=s: e.scalar_tensor_tensor(out=xs_, in0=xs_, scalar=SMALL[:, 12 + s:13 + s], in1=FG[:],
                                                                   op0=ALU.mult, op1=ALU.mult), [XT, "SMALL", "FG"], [XT])
            P.op("pool", lambda e, tf=tf, X=X: e.dma_start(
                out=out_d.ap()[tf * TT:(tf + 1) * TT, :].rearrange("(s p) d -> p s d", p=128),
                in_=V(X, 0, 128, 0, [[1024, 2], [1, 1024]])), r=[XT], w=["OUT"], dma=True, nobar=True)

        if NT > 0:
            load_x(0)
            rms_to_ht(0, XS_O, XB[0], "X0")
        for t in range(NT):
            tok0 = t * TT
            X = XB[t % 2]
            XT = f"X{t % 2}"
            rp = (t % 2) * 128
            for cs in range(2):
                P.op("pool", lambda e, cs=cs, t=t, rp=rp: e.dma_start(
                    out=ROPE[:, rp + cs * 64:rp + cs * 64 + 64],
                    in_=ctab_d.ap()[:, CT_ROPE + cs * NSUB * 32 + 2 * t * 32:CT_ROPE + cs * NSUB * 32 + 2 * t * 32 + 64]),
                    w=[f"ROPE{t % 2}"], dma=True, nobar=True)
            if prev_final[0] is not None:
                final_norm(*prev_final[0], GYB_O, "GYB")
                prev_final[0] = None
            if t + 1 < NT:
                load_x(t + 1)

            chk(10)
            U0 = lambda p0, pn, eoff, pat: AV(U0_O, p0, pn, eoff, pat, BF16)
            U0M = lambda p0, pn, eoff, pat: AV(ZPR_O, p0, pn, eoff, pat, BF16)
            for half in range(2):
                if not (half == 0 and pre_u):
                    wb, wt = load_w(w_in_ab_d, 0, half * 512)
                for f in range(4):
                    ft = half * 4 + f
                    if half == 0 and pre_u:
                        b, bt = pre_u[f]
                    else:
                        b, bt = proj_fm(wb, wt, f * 128, HT, "HT")
                    act(lambda e, b=b, ft=ft: e.copy(out=U0(0, 128, ft * TT, [[1, 32], [32, 8]]),
                                                     in_=V(b, 0, 128, 0, [[8, 32], [1, 8]])), [bt], [f"U0_{ft}"])
                    act(lambda e, b=b, ft=ft: e.copy(out=U0M(64, 64, ft * TT, [[1, 32], [32, 8]]),
                                                     in_=V(b, 64, 64, 0, [[8, 32], [1, 8]])), [bt], [f"U0M_{ft}"])
                    dve(lambda e, ft=ft: e.memset(U0M(64, 32, ft * TT, [[1, TT]]), 0.0), [f"U0M_{ft}"], [f"U0M_{ft}"])
            pre_u = []
            chk(11)
            zb = [bank() for _ in range(4)]
            for ft in range(8):
                for q in range(4):
                    b, bt = zb[q]
                    for ri in range(2):
                        for s in range(8):
                            if q < 3:
                                pe(lambda e, b=b, ri=ri, s=s, ft=ft, q=q: e.matmul(
                                    b[:, ri * 256 + ft * 32: ri * 256 + ft * 32 + 32],
                                    lhsT=V(WIN, 32 * q, 32, ((ft * 8 + s) * 2 + ri) * 128, [[1, 128]]),
                                    rhs=U0(32 * q, 32, ft * TT + s * 32, [[1, 32]]),
                                    start=(s == 0), stop=(s == 7), tile_position=(32 * q, 0)), ["WIN", f"U0_{ft}"], [bt])
                            else:
                                pe(lambda e, b=b, ri=ri, s=s, ft=ft: e.matmul(
                                    b[:, ri * 256 + ft * 32: ri * 256 + ft * 32 + 32],
                                    lhsT=V(WIN, 64, 64, ((ft * 8 + s) * 2 + ri) * 128, [[1, 128]]),
                                    rhs=U0M(64, 64, ft * TT + s * 32, [[1, 32]]),
                                    start=(s == 0), stop=(s == 7), tile_position=(64, 0)), ["WIN", f"U0M_{ft}"], [bt])
            for q in range(4):
                b, bt = zb[q]
                act(lambda e, b=b, q=q: e.copy(out=AV(ZR_O, 0, 128, q * 32, [[128, 8], [1, 32]]),
                                               in_=V(b, 0, 128, 0, [[32, 8], [1, 32]])), [bt], ["ZR"])
                act(lambda e, b=b, q=q: e.copy(out=AV(ZI_O, 0, 128, q * 32, [[128, 8], [1, 32]]),
                                               in_=V(b, 0, 128, 256, [[32, 8], [1, 32]])), [bt], ["ZI"])
            for half in range(2):
                wb, wt = load_w(w_in_ab_d, 0, 1024 + half * 512)
                for f in range(4):
                    ft = half * 4 + f
                    b, bt = proj_fm(wb, wt, f * 128, HT, "HT")
                    actf(AV(SG0_O, 0, 128, ft * TT, [[1, TT]], BF16), b[:, 0:TT], AF.Silu, [bt], ["SG0"])
            chk(12)
            SAL = lambda ri, lo, n: AV(SAL_O, 0, 128, ri * 1056 + lo, [[33, 32], [1, n]], BF16)
            for ri in range(2):
                dve(lambda e, ri=ri: e.tensor_copy(out=SAL(ri, 0, 1), in_=V(W0, 0, 128, ri * 32, [[1, 32], [1, 1]])),
                    ["W0"], ["SAL"])
            f1k = lambda off: AV(off, 0, 128, 0, [[1, 1024]])
            TCf = V(TCS, 0, 128, 0, [[1, 1024]])
            TSf = V(TCS, 0, 128, 1024, [[1, 1024]])
            tt(f1k(ZT1_O), TCf, f1k(ZR_O), ALU.mult, ["TCS", "ZR"], ["ZT1", "XS"])
            tt(f1k(ZT2_O), TSf, f1k(ZI_O), ALU.mult, ["TCS", "ZI"], ["ZT2"])
            tt(f1k(ZPR_O), f1k(ZT1_O), f1k(ZT2_O), ALU.add, ["ZT1", "ZT2"], ["ZPR"] + [f"U0M_{k}" for k in range(8)])
            tt(f1k(ZT1_O), TCf, f1k(ZI_O), ALU.mult, ["TCS", "ZI"], ["ZT1"])
            tt(f1k(ZT2_O), TSf, f1k(ZR_O), ALU.mult, ["TCS", "ZR"], ["ZT2"])
            tt(f1k(ZPI_O), f1k(ZT1_O), f1k(ZT2_O), ALU.subtract, ["ZT1", "ZT2"], ["ZPI"])
            for ri, zo, wo_, tk in ((0, ZPR_O, ZR_O, "ZPR"), (1, ZPI_O, ZI_O, "ZPI")):
                z0 = AV(zo, 0, 128, 0, [[32, 32], [1, 1]])
                tt(AV(ZT1_O, 0, 128, 0, [[1, 32], [1, 1]]), V(RHO, 0, 128, 0, [[1, 32], [1, 1]]),
                   V(W0, 0, 128, ri * 32, [[1, 32], [1, 1]]), ALU.mult, ["RHO", "W0"], ["ZT1"])
                tt(z0, z0, AV(ZT1_O, 0, 128, 0, [[1, 32], [1, 1]]), ALU.add, [tk, "ZT1"], [tk])
                wtk = "ZR" if ri == 0 else "ZI"
                dve(lambda e, zo=zo, wo_=wo_: e.tensor_tensor_scan(
                    out=f1k(wo_), data0=RHOT[:], data1=f1k(zo), initial=0.0, op0=ALU.mult, op1=ALU.add),
                    [tk, "RHOT"], [wtk])
            tt(f1k(ZT1_O), TCf, f1k(ZR_O), ALU.mult, ["TCS", "ZR"], ["ZT1"])
            tt(f1k(ZT2_O), TSf, f1k(ZI_O), ALU.mult, ["TCS", "ZI"], ["ZT2"])
            tt(f1k(ZPR_O), f1k(ZT1_O), f1k(ZT2_O), ALU.subtract, ["ZT1", "ZT2"], ["ZPR"])
            tt(f1k(ZT1_O), TCf, f1k(ZI_O), ALU.mult, ["TCS", "ZI"], ["ZT1"])
            tt(f1k(ZT2_O), TSf, f1k(ZR_O), ALU.mult, ["TCS", "ZR"], ["ZT2"])
            tt(f1k(ZPI_O), f1k(ZT1_O), f1k(ZT2_O), ALU.add, ["ZT1", "ZT2"], ["ZPI"])
            for ri, zo, tk in ((0, ZPR_O, "ZPR"), (1, ZPI_O, "ZPI")):
                dve(lambda e, ri=ri, zo=zo: e.tensor_copy(out=SAL(ri, 1, 32), in_=AV(zo, 0, 128, 0, [[32, 32], [1, 32]])),
                    [tk], ["SAL"])
                dve(lambda e, ri=ri, zo=zo: e.tensor_copy(out=V(W0, 0, 128, ri * 32, [[1, 32], [1, 1]]),
                                                          in_=AV(zo, 0, 128, 31, [[32, 32], [1, 1]])), [tk], ["W0"])
            chk(13)
            C_G = 0.7978845608028654
            for ft in range(8):
                b, bt = bank()
                bv = lambda p0, pn, lo, n, b=b: V(b, p0, pn, lo, [[8, 32], [1, n]])
                for tau in range(8):
                    nn = (8 - tau) * 32
                    pe(lambda e, b=b, ft=ft, tau=tau, nn=nn: e.matmul(
                        b[:, tau * 32:TT], lhsT=V(KW, 0, 128, (ft * 8 + tau) * 128, [[1, 128]]),
                        rhs=U0(0, 128, ft * TT, [[1, nn]]), start=(tau == 0), stop=False), ["KW", f"U0_{ft}"], [bt])
                for q in range(4):
                    pair = ft * 4 + q
                    for r_ in range(8):
                        for ri in range(2):
                            last = (r_ == 7 and ri == 1)
                            pe(lambda e, b=b, q=q, pair=pair, r_=r_, ri=ri, last=last, bv=bv: e.matmul(
                                V(b, 32 * q, 32, r_ * 32, [[1, 32]]), lhsT=V(WOUT, 0, 128, ((pair * 8 + r_) * 2 + ri) * 32, [[1, 32]]),
                                rhs=AV(SAL_O, 0, 128, ri * 1056 + pair * 33, [[1, 32]], BF16),
                                start=False, stop=last, tile_position=(0, 32 * q)), ["WOUT", "SAL"], [bt])
                t1 = AV(ZT1_O, 0, 128, (ft % 2) * TT, [[1, TT]])
                t2 = AV(ZT2_O, 0, 128, (ft % 2) * TT, [[1, TT]])
                k1, k2 = ("ZT1", "ZT2")
                actf(t1, b[:, 0:TT], AF.Square, [bt], [k1])
                ts(t1, t1, 0.044715, 1.0, ALU.mult, ALU.add, [k1], [k1])
                tt(t2, t1, b[:, 0:TT], ALU.mult, [k1, bt], [k2])
                actf(t2, t2, AF.Sigmoid, [k2], [k2], scale=2.0 * C_G)
                tt(AV(GYB_O, 0, 128, ft * TT, [[1, 8], [8, 32]], BF16), AV(ZT2_O, 0, 128, (ft % 2) * TT, [[32, 8], [1, 32]]),
                   V(b, 0, 128, 0, [[32, 8], [1, 32]]), ALU.mult, [k2, bt], ["GYB"])
            chk(14)
            GYB = AV(GYB_O, 0, 128, 0, [[1, 8 * TT]], BF16)
            for half in range(2):
                wb, wt = load_w(glu_w_d, 0, half * 512)
                for f in range(4):
                    ft = half * 4 + f
                    b, bt = proj_fm(wb, wt, f * 128, GYB, "GYB")
                    zs = AV(ZS_O, 0, 128, ft * TT, [[1, TT]], BF16)
                    actf(zs, b[:, 0:TT], AF.Sigmoid, [bt, "COLS"], ["ZS"], bias=col(0, 24 + ft))
                    tt(zs, zs, AV(GYB_O, 0, 128, ft * TT, [[1, TT]], BF16), ALU.mult, ["ZS", "GYB"], ["ZS"])
                    tt(YC[:, ft * TT:(ft + 1) * TT], zs, AV(SG0_O, 0, 128, ft * TT, [[1, TT]], BF16), ALU.mult,
                       ["ZS", "SG0"], ["YC"])
            qk_banks = {}
            for qk, col0 in ((0, 2048), (1, 2560)):
                wb, wt = load_w(w_in_ab_d, 0, col0)
                for s in range(2):
                    qk_banks[(qk, s)] = proj_tm(wb, wt, s)
            P.barrier()
            if stop == 1:
                break

            QR_O = 0; KR_O = 512; KD0_O = 1024; KD1_O = 1536; QD_O = 2048
            QT0_O = 2560; QT1_O = 3072; KT_O = 3584; QDT0_O = 4096; QDT1_O = 4608
            VB_O = 5120
            SMF_O = 6144
            SB_O = 7168
            SGR_O = 8192
            RT_O = 9216
            OF_O = 10240
            ST_O = 9216
            QT_OS = (QT0_O, QT1_O)
            QDT_OS = (QDT0_O, QDT1_O)
            KD_OS = (KD0_O, KD1_O)
            for off, nm in ((QT0_O, "QT0"), (QDT0_O, "QDT0"), (KD0_O, "KD0")):
                pool(lambda e, off=off: e.memset(AV(off, 64, 64, 0, [[1, 1024]], BF16), 0.0), [], [nm])
            for off, nm in ((QT1_O, "QT1"), (QDT1_O, "QDT1"), (KD1_O, "KD1")):
                pool(lambda e, off=off: e.memset(AV(off, 0, 64, 0, [[1, 1024]], BF16), 0.0), [], [nm])
            pool(lambda e: e.memset(AV(SMF_O, 0, 128, 0, [[1, 2048]], BF16), 0.0), [], ["SMF"])
            ropeC = lambda s_: V(ROPE, 0, 128, rp + s_ * 32, [[0, 8], [1, 32]])
            ropeS = lambda s_: V(ROPE, 0, 128, rp + 64 + s_ * 32, [[0, 8], [1, 32]])
            RTK = f"ROPE{t % 2}"
            for qk, col0, dst in ((0, 2048, QR_O), (1, 2560, KR_O)):
                for s in range(2):
                    b, bt = qk_banks[(qk, s)]
                    sub = s
                    x1 = V(b, 0, 128, 0, [[64, 8], [1, 32]])
                    x2 = V(b, 0, 128, 32, [[64, 8], [1, 32]])
                    r4 = lambda k: AV(RT_O, 0, 128, k * 256, [[32, 8], [1, 32]])
                    tt(r4(0), x1, ropeC(sub), ALU.mult, [bt, RTK], ["RT0"])
                    tt(r4(1), x2, ropeS(sub), ALU.mult, [bt, RTK], ["RT1"])
                    tt(r4(2), x1, ropeS(sub), ALU.mult, [bt, RTK], ["RT2"])
                    tt(r4(3), x2, ropeC(sub), ALU.mult, [bt, RTK], ["RT3"])
                    o1 = AV(dst, 0, 128, s * 512, [[64, 8], [1, 32]], BF16)
                    o2 = AV(dst, 0, 128, s * 512 + 32, [[64, 8], [1, 32]], BF16)
                    tk = "QR" if qk == 0 else "KR"
                    tt(o1, r4(0), r4(1), ALU.subtract, ["RT0", "RT1"], [tk])
                    tt(o2, r4(2), r4(3), ALU.add, ["RT2", "RT3"], [tk])
                    if qk == 0:
                        tt(AV(QD_O, 0, 128, s * 512, [[64, 8], [1, 64]], BF16),
                           AV(dst, 0, 128, s * 512, [[64, 8], [1, 64]], BF16),
                           V(CTAB, 0, 128, CT_QDEC, [[1, 8], [0, 64]]), ALU.mult, [tk, "CTAB"], ["QD"])
                    else:
                        for hf in range(2):
                            tt(AV(KD_OS[hf], 64 * hf, 64, s * 512, [[64, 8], [1, 64]], BF16),
                               AV(dst, 64 * hf, 64, s * 512, [[64, 8], [1, 64]], BF16),
                               V(CTAB, 64 * hf, 64, CT_KDEC, [[1, 8], [0, 64]]), ALU.mult, [tk, "CTAB"], [f"KD{hf}"])
            for src, stk, dsts in ((QR_O, "QR", (QT0_O, QT1_O)), (KR_O, "KR", None), (QD_O, "QD", (QDT0_O, QDT1_O))):
                for s in range(2):
                    b, bt = bank()
                    for hp in range(4):
                        pe(lambda e, b=b, src=src, s=s, hp=hp: e.transpose(
                            out=V(b, 0, 128, hp * 128, [[1, 128]], BF16),
                            in_=AV(src, 0, 128, s * 512 + hp * 128, [[1, 128]], BF16), identity=IDB[:]), [stk, "IDB"], [bt])
                    if dsts is None:
                        act(lambda e, b=b, s=s: e.copy(
                            out=AV(KT_O, 0, 128, s * 128, [[TT, 4], [1, 128]], BF16),
                            in_=V(b, 0, 128, 0, [[128, 4], [1, 128]], BF16)), [bt], ["KT"])
                    else:
                        for hf in range(2):
                            nm = ("QT" if stk == "QR" else "QDT") + str(hf)
                            act(lambda e, b=b, s=s, hf=hf, dsts=dsts: e.copy(
                                out=AV(dsts[hf], 64 * hf, 64, s * 128, [[TT, 4], [1, 128]], BF16),
                                in_=V(b, 64 * hf, 64, 0, [[128, 4], [1, 128]], BF16)), [bt], [nm])
            for half in range(2):
                wb, wt = load_w(w_in_ab_d, 0, 3072 + half * 512)
                for s in range(2):
                    b, bt = proj_tm(wb, wt, s)
                    act(lambda e, b=b, s=s, half=half: e.copy(
                        out=AV(VB_O, 0, 128, s * 1024 + half * 512, [[1, 512]], BF16), in_=b[:, 0:512]), [bt], ["VB"])
            for half in range(2):
                wb, wt = load_w(w_in_ab_d, 0, 4096 + half * 512)
                for f in range(4):
                    ft = half * 4 + f
                    b, bt = proj_fm(wb, wt, f * 128, HT, "HT")
                    actf(AV(SGR_O, 0, 128, ft * TT, [[1, TT]], BF16), b[:, 0:TT], AF.Silu, [bt], ["SGR"])
            for h in range(8):
                hp, par = h // 2, h % 2
                b, bt = bank()
                for c in range(4):
                    s, cpar = c // 2, c % 2
                    tk0 = s * 128 + cpar * 64
                    pe(lambda e, b=b, hp=hp, par=par, s=s, cpar=cpar, tk0=tk0: e.matmul(
                        b[64 * cpar:64 * cpar + 64, s * 64:(s + 1) * 64],
                        lhsT=AV(KT_O, 0, 128, hp * TT + tk0, [[1, 64]], BF16),
                        rhs=AV(QT_OS[par], 0, 128, hp * TT + tk0, [[1, 64]], BF16), start=True, stop=True,
                        tile_position=(0, 64 * cpar)), ["KT", f"QT{par}"], [bt])
                for cpar in range(2):
                    tt(AV(SMF_O, 64 * cpar, 64, h * 256 + cpar * 64, [[128, 2], [1, 64]], BF16),
                       V(b, 64 * cpar, 64, 0, [[64, 2], [1, 64]]),
                       V(CTAB, 64 * cpar, 64, CT_MASK + h * 64, [[0, 2], [1, 64]]), ALU.mult, [bt, "CTAB"], ["SMF"])
            for hp in range(4):
                b, bt = bank()
                for c in range(4):
                    s, cpar = c // 2, c % 2
                    for par in range(2):
                        h = hp * 2 + par
                        pe(lambda e, b=b, c=c, s=s, cpar=cpar, par=par, h=h: e.matmul(
                            b[64 * par:64 * par + 64, c * 128:(c + 1) * 128],
                            lhsT=AV(KD_OS[cpar], 0, 128, s * 512 + h * 64, [[1, 64]], BF16),
                            rhs=AV(VB_O, 0, 128, s * 1024 + h * 128, [[1, 128]], BF16), start=True, stop=True,
                            tile_position=(0, 64 * par)), [f"KD{cpar}", "VB"], [bt])
                for c in range(4):
                    sst = SRET[:, hp * 128:(hp + 1) * 128]
                    dve(lambda e, hp=hp, c=c, sst=sst: e.tensor_copy(
                        out=AV(SB_O, 0, 128, (hp * 4 + c) * 128, [[1, 128]], BF16), in_=sst), ["SRET"], ["SB"])
                    dve(lambda e, b=b, hp=hp, c=c, sst=sst: e.scalar_tensor_tensor(
                        out=sst, in0=sst, scalar=V(CTAB, 0, 128, CT_G64 + hp, [[1, 1]]), in1=b[:, c * 128:(c + 1) * 128],
                        op0=ALU.mult, op1=ALU.add), ["SRET", "CTAB", bt], ["SRET"])
            for h in range(8):
                hp, par = h // 2, h % 2
                b, bt = bank()
                for s in range(2):
                    pe(lambda e, b=b, h=h, s=s: e.matmul(
                        b[:, s * 128:(s + 1) * 128], lhsT=AV(VB_O, 0, 128, s * 1024 + h * 128, [[1, 128]], BF16),
                        rhs=AV(SMF_O, 0, 128, h * 256 + s * 128, [[1, 128]], BF16), start=(s == 0), stop=False),
                       ["VB", "SMF"], [bt])
                for c in range(4):
                    tk0 = (c // 2) * 128 + (c % 2) * 64
                    pe(lambda e, b=b, hp=hp, par=par, c=c, tk0=tk0: e.matmul(
                        b[:, c * 64:(c + 1) * 64], lhsT=AV(SB_O, 0, 128, (hp * 4 + c) * 128, [[1, 128]], BF16),
                        rhs=AV(QDT_OS[par], 0, 128, hp * TT + tk0, [[1, 64]], BF16), start=False, stop=(c == 3)),
                       ["SB", f"QDT{par}"], [bt])
                ob = (h % 2) * 512
                OF = AV(OF_O, 0, 128, ob, [[1, TT]])
                OFB = AV(OF_O, 0, 128, 2 * (ob + 256), [[1, TT]], BF16)
                OSQ = AV(OF_O, 0, 128, 2 * (ob + 384), [[1, TT]], BF16)
                otk = f"OF{h % 2}"
                act(lambda e, b=b, OF=OF: e.copy(out=OF, in_=b[:, 0:TT]), [bt], [otk])
                act(lambda e, b=b, OFB=OFB: e.copy(out=OFB, in_=b[:, 0:TT]), [bt], [otk])
                actf(OSQ, b[:, 0:TT], AF.Square, [bt], [otk])
                b2, bt2 = bank()
                pe(lambda e, b2=b2, OFB=OFB: e.matmul(b2[:, 0:TT], lhsT=ONESB[:], rhs=OFB, start=True, stop=True),
                   [otk, "ONESB"], [bt2])
                pe(lambda e, b2=b2, OSQ=OSQ: e.matmul(b2[:, 256:256 + TT], lhsT=ONESB[:], rhs=OSQ, start=True, stop=True),
                   [otk, "ONESB"], [bt2])
                sm = AV(ST_O, 0, 128, 0, [[1, TT]])
                sv = AV(ST_O, 0, 128, 256, [[1, TT]])
                sx = AV(ST_O, 0, 128, 512, [[1, TT]])
                actf(sm, b2[:, 0:TT], AF.Copy, [bt2], ["ST0", "RT0"], scale=1.0 / 128)
                actf(sv, b2[:, 0:TT], AF.Square, [bt2], ["ST1", "RT1"], scale=1.0 / 128)
                dve(lambda e, b2=b2, sv=sv: e.scalar_tensor_tensor(out=sv, in0=b2[:, 256:256 + TT], scalar=1.0 / 128, in1=sv,
                                                                   op0=ALU.mult, op1=ALU.subtract), [bt2, "ST1"], ["ST1"])
                ts(sv, sv, EPS, None, ALU.add, None, ["ST1"], ["ST1"])
                actf(sv, sv, AF.Sqrt, ["ST1"], ["ST1"])
                dve(lambda e, sv=sv: e.reciprocal(out=sv, in_=sv), ["ST1"], ["ST1"])
                tt(sx, OF, sm, ALU.subtract, [otk, "ST0"], ["ST2", "RT2"])
                tt(sx, sx, sv, ALU.mult, ["ST2", "ST1"], ["ST2"])
                tt(YC[:, (8 + h) * TT:(9 + h) * TT], sx, AV(SGR_O, 0, 128, h * TT, [[1, TT]], BF16), ALU.mult,
                   ["ST2", "SGR"], ["YC"])
            for ns in range(2):
                wa, wat = load_w(w_out_ab_d, 0, ns * 512)
                wb2, wbt = load_w(w_out_ab_d, 1024, ns * 512)
                for s in range(2):
                    b, bt = bank()
                    for kc in range(16):
                        w_, wt_ = (wa, wat) if kc < 8 else (wb2, wbt)
                        pe(lambda e, b=b, kc=kc, s=s, w_=w_: e.matmul(
                            b[:, 0:512], lhsT=YC[:, kc * TT + s * 128:kc * TT + s * 128 + 128],
                            rhs=V(w_, 0, 128, (kc % 8) * 512, [[1, 512]]), start=(kc == 0), stop=(kc == 15)),
                           ["YC", wt_], [bt])
                    xs_ = X[:, s * 1024 + ns * 512:s * 1024 + ns * 512 + 512]
                    tt(xs_, xs_, b[:, 0:512], ALU.add, [XT, bt], [XT])
            if debug:
                P.op("pool", lambda e, tok0=tok0, X=X: e.dma_start(
                    out=x1_d.ap()[tok0:tok0 + TT, :].rearrange("(s p) d -> p s d", p=128),
                    in_=V(X, 0, 128, 0, [[1024, 2], [1, 1024]])), r=[XT], w=["OUT"], dma=True)
            P.barrier()
            if stop == 2:
                break

            L1XS_O = 0
            SG1_O = 1024
            SIG_O = 2048
            VV_O = 2560
            VSQ_O = 4608
            LST_O = 5120
            Y1_O = 6144
            LT_O = 7168
            rms_to_ht(1, L1XS_O, X, XT)
            for ft in range(8):
                dve(lambda e, ft=ft: e.tensor_copy(out=U1[:, ft * 288 + 2:ft * 288 + 32],
                                                   in_=U1[:, ft * 288 + 258:ft * 288 + 288]), ["U1"], ["U1"])
            for half in range(2):
                wa, wat = load_w(w_in_c_d, 0, half * 512)
                wb2, wbt = load_w(w_in_c_d, 0, 1024 + half * 512)
                for f in range(4):
                    ft = half * 4 + f
                    ba, bat = proj_fm(wa, wat, f * 128, HT, "HT")
                    bb, bbt = proj_fm(wb2, wbt, f * 128, HT, "HT")
                    sg = AV(SIG_O, 0, 128, (ft % 2) * 256, [[1, TT]])
                    actf(sg, bb[:, 0:TT], AF.Sigmoid, [bbt], [f"SIG{ft % 2}"])
                    tt(U1[:, ft * 288 + 32:ft * 288 + 288], ba[:, 0:TT], sg, ALU.mult, [bat, f"SIG{ft % 2}"], ["U1"])
            for half in range(2):
                wb, wt = load_w(w_in_c_d, 0, 2048 + half * 512)
                for f in range(4):
                    ft = half * 4 + f
                    b, bt = proj_fm(wb, wt, f * 128, HT, "HT")
                    actf(AV(SG1_O, 0, 128, ft * TT, [[1, TT]], BF16), b[:, 0:TT], AF.Silu, [bt], ["SG1"])
            if t + 1 < NT:
                rms_to_ht(0, L1XS_O, XB[(t + 1) % 2], f"X{(t + 1) % 2}")
            dn = 0
            bsum, bsumt = PS[7], "ps7"
            for ft in range(8):
                b, bt = bank()
                di = wbn[0] % 3
                wbn[0] += 1
                P.op("sp", lambda e, di=di, ft=ft: e.dma_start(out=V(WB[di], 0, 128, 0, [[1, 31 * 128]]), in_=DG.ap()[ft]),
                     r=[f"DG{ft}"], w=[f"WB{di}"], dma=True, nobar=True)
                for k in range(31):
                    pe(lambda e, b=b, di=di, ft=ft, k=k: e.matmul(
                        b[:, 0:TT], lhsT=V(WB[di], 0, 128, k * 128, [[1, 128]]),
                        rhs=U1[:, ft * 288 + 2 + k:ft * 288 + 2 + k + TT],
                        start=(k == 0), stop=(k == 30)), [f"WB{di}", "U1"], [bt])
                vv = AV(VV_O, 0, 128, ft * TT, [[1, TT]])
                actf(vv, b[:, 0:TT], AF.Identity, [bt, "COLS"], ["VV"], bias=col(0, 32 + ft))
                vvb = AV(VSQ_O, 0, 128, 2 * ((ft % 2) * 256), [[1, TT]], BF16)
                vsq = AV(VSQ_O, 0, 128, 2 * ((ft % 2) * 256 + 128), [[1, TT]], BF16)
                dve(lambda e, vvb=vvb, vv=vv: e.tensor_copy(out=vvb, in_=vv), ["VV"], [f"VSQ{ft % 2}"])
                actf(vsq, vv, AF.Square, ["VV"], [f"VSQ{ft % 2}"])
                pe(lambda e, vvb=vvb, ft=ft: e.matmul(bsum[:, 0:TT], lhsT=ONESB[:], rhs=vvb,
                                                     start=(ft == 0), stop=False), [f"VSQ{ft % 2}", "ONESB"], [bsumt])
                pe(lambda e, vsq=vsq, ft=ft: e.matmul(bsum[:, 256:256 + TT], lhsT=ONESB[:], rhs=vsq,
                                                     start=False, stop=(ft == 7)), [f"VSQ{ft % 2}", "ONESB"], [bsumt])
            sm = AV(LST_O, 0, 128, 0, [[1, TT]])
            sv = AV(LST_O, 0, 128, 256, [[1, TT]])
            actf(sm, bsum[:, 0:TT], AF.Copy, [bsumt], ["LST0"], scale=1.0 / 1024)
            actf(sv, bsum[:, 0:TT], AF.Square, [bsumt], ["LST1"], scale=1.0 / 1024)
            dve(lambda e: e.scalar_tensor_tensor(out=sv, in0=bsum[:, 256:256 + TT], scalar=1.0 / 1024, in1=sv,
                                                 op0=ALU.mult, op1=ALU.subtract), [bsumt, "LST1"], ["LST1"])
            ts(sv, sv, EPS, None, ALU.add, None, ["LST1"], ["LST1"])
            actf(sv, sv, AF.Sqrt, ["LST1"], ["LST1"])
            dve(lambda e: e.reciprocal(out=sv, in_=sv), ["LST1"], ["LST1"])
            for ft in range(8):
                vv = AV(VV_O, 0, 128, ft * TT, [[1, TT]])
                lt = AV(LT_O, 0, 128, (ft % 2) * 256, [[1, TT]])
                ltk = f"LT{ft % 2}"
                tt(lt, vv, sm, ALU.subtract, ["VV", "LST0"], [ltk])
                tt(lt, lt, sv, ALU.mult, [ltk, "LST1"], [ltk])
                actf(lt, lt, AF.Silu, [ltk, "COLS"], [ltk], scale=col(0, 40 + ft), bias=col(0, 48 + ft))
                tt(AV(Y1_O, 0, 128, ft * TT, [[1, TT]], BF16), lt, AV(SG1_O, 0, 128, ft * TT, [[1, TT]], BF16), ALU.mult,
                   [ltk, "SG1"], [f"Y1_{ft}"])
            wcs = [load_w(w_out_c_d, 0, ns * 512) for ns in range(2)]
            obk = {(ns, s): bank() for ns in range(2) for s in range(2)}
            for kc in range(8):
                for ns in range(2):
                    wb, wt = wcs[ns]
                    for s in range(2):
                        b, bt = obk[(ns, s)]
                        pe(lambda e, b=b, kc=kc, s=s, wb=wb: e.matmul(
                            b[:, 0:512], lhsT=AV(Y1_O, 0, 128, kc * TT + s * 128, [[1, 128]], BF16),
                            rhs=V(wb, 0, 128, kc * 512, [[1, 512]]), start=(kc == 0), stop=(kc == 7)), [f"Y1_{kc}", wt], [bt])
            for ns in range(2):
                for s in range(2):
                    b, bt = obk[(ns, s)]
                    xs_ = X[:, s * 1024 + ns * 512:s * 1024 + ns * 512 + 512]
                    tt(xs_, xs_, b[:, 0:512], ALU.add, [XT, bt], [XT])
            if t + 1 < NT:
                wb, wt = load_w(w_in_ab_d, 0, 0)
                pre_u = [proj_fm(wb, wt, f * 128, HT, "HT") for f in range(4)]
            prev_final[0] = (t, X, XT)
            P.barrier()

        if prev_final[0] is not None:
            final_norm(*prev_final[0], GYB_O, "GYB")
    except StopBuild:
        pass
    P.op("sp", lambda e: e.nop(), r=["OUT"])
    P.barrier()
    P.emit()
    P.close()
    return nc


def prep_inputs(inp, b, NT):
    T = NT * TT
    f = lambda a: np.ascontiguousarray(np.asarray(a, dtype=np.float32))
    return {
        "x": f(inp["x"][b, :T]),
        "norm_g": f(inp["norm_g"]).reshape(16, 128),
        "final_g": f(inp["final_g"]),
        "w_in_ab": f(inp["w_in_ab"][0]),
        "a_re": f(inp["s5_a_re"][0]).reshape(32, 128),
        "a_im": f(inp["s5_a_im"][0]).reshape(32, 128),
        "log_dt": f(inp["s5_log_dt"][0]).reshape(32, 2),
        "b_re": f(inp["s5_b_re"][0]).reshape(-1),
        "b_im": f(inp["s5_b_im"][0]).reshape(-1),
        "c_re": f(inp["s5_c_re"][0]).reshape(-1),
        "c_im": f(inp["s5_c_im"][0]).reshape(-1),
        "s5_d": f(inp["s5_d"][0]).reshape(8, 128),
        "glu_w": f(inp["s5_glu_w"][0]),
        "glu_b": f(inp["s5_glu_b"][0]).reshape(8, 128),
        "w_out_ab": f(inp["w_out_ab"][0]),
        "w_in_c": f(inp["w_in_c"][0]),
        "conv_w": f(inp["conv_w"][0]).reshape(248, 128),
        "conv_b": f(inp["conv_b"][0]).reshape(8, 128),
        "ln_g": f(inp["conv_ln_g"][0]).reshape(8, 128),
        "ln_b": f(inp["conv_ln_b"][0]).reshape(8, 128),
        "w_out_c": f(inp["w_out_c"][0]),
        "ctab": make_ctab(NT * 2),
    }


def kernel(**inputs):
    NT = 16
    nc = build(NT)
    in_maps = [prep_inputs(inputs, b, NT) for b in range(8)]
    res = run_bass_kernel_spmd(nc, in_maps, core_ids=list(range(8)))
    return np.stack([np.asarray(r["out"]) for r in res.results], axis=0).astype(np.float32)
```
